# Optimizing a Trainium2 kernel written in Bass

```python
import jax, jax.numpy as jnp
from jax import lax
import numpy as np

D_MODEL = 1024
BATCH = 8
SEQ = 4096
DEPTH = 1
DEC_BATCH = 128
DEC_SEQ = 1
PAST_LEN = 16384
PAGE_SIZE = 128

N_META = 16
MIX_WIDTH = D_MODEL
ATTN_WIDTH = MIX_WIDTH // 2
POOL_WIDTH = MIX_WIDTH - ATTN_WIDTH
HEAD_DIM = 64
N_HEADS = ATTN_WIDTH // HEAD_DIM
N_KV_HEADS = max(1, N_HEADS // 4)
GQA_GROUP = N_HEADS // N_KV_HEADS
WINDOW = 128
BLOCK = 128
SM_SCALE = HEAD_DIM ** -0.5
POOL_WINDOWS = (2, 4, 8, 16)
N_POOL_GROUPS = len(POOL_WINDOWS)
POOL_GROUP_WIDTH = POOL_WIDTH // N_POOL_GROUPS
POOL_STATE = max(POOL_WINDOWS) - 1
D_FF = ((8 * D_MODEL // 3 + 127) // 128) * 128
CONV_WIDTH = 3
CONV_STATE = CONV_WIDTH - 1
QKV_WIDTH = (N_HEADS + 2 * N_KV_HEADS) * HEAD_DIM
IN_WIDTH = QKV_WIDTH + POOL_WIDTH
RMS_EPS = 1e-6

kernel_name = 'hymba_pool_swa_sink_convffn_step'

F32 = jnp.float32


def rms_norm(x, g):
    xf = x.astype(F32)
    y = xf * lax.rsqrt(jnp.mean(xf * xf, axis=-1, keepdims=True) + RMS_EPS)
    return (y * g.astype(F32)).astype(x.dtype)


def alibi_slopes():
    exps = jnp.arange(1, N_HEADS + 1, dtype=F32) * (8.0 / N_HEADS)
    return jnp.exp2(-exps).reshape(N_KV_HEADS, GQA_GROUP)


def mixer_projection(h, w_in, b_in):
    z = h @ w_in + b_in
    lead = z.shape[:-1]
    nq = N_HEADS * HEAD_DIM
    nk = N_KV_HEADS * HEAD_DIM
    q = z[..., :nq].reshape(*lead, N_KV_HEADS, GQA_GROUP, HEAD_DIM)
    k = z[..., nq:nq + nk].reshape(*lead, N_KV_HEADS, HEAD_DIM)
    v = z[..., nq + nk:nq + 2 * nk].reshape(*lead, N_KV_HEADS, HEAD_DIM)
    u = z[..., nq + 2 * nk:]
    return q, k, v, u


def sink_softmax_attend(q, k, v, dist, valid, sinks, slopes):
    s = jnp.einsum('...qkgd,...skd->...kgqs', q.astype(F32), k.astype(F32)) * SM_SCALE
    s = s - slopes[:, :, None, None] * dist.astype(F32)
    s = jnp.where(valid, s, -jnp.inf)
    sink = sinks.astype(F32).reshape(N_KV_HEADS, GQA_GROUP, 1, 1)
    m = jnp.maximum(jnp.max(s, axis=-1, keepdims=True), sink)
    p = jnp.exp(s - m)
    w = p / (jnp.sum(p, axis=-1, keepdims=True) + jnp.exp(sink - m))
    return jnp.einsum('...kgqs,...skd->...qkgd', w, v.astype(F32))


def window_attention_prompt(q, k, v, sinks, slopes):
    b, L = q.shape[:2]
    front = (-L) % BLOCK
    n_blk = (L + front) // BLOCK
    qb = jnp.pad(q, ((0, 0), (front, 0), (0, 0), (0, 0), (0, 0))).reshape(
        b, n_blk, BLOCK, N_KV_HEADS, GQA_GROUP, HEAD_DIM)

    def band(t):
        tp = jnp.pad(t, ((0, 0), (front + BLOCK, 0), (0, 0), (0, 0))).reshape(
            b, n_blk + 1, BLOCK, N_KV_HEADS, HEAD_DIM)
        return jnp.concatenate([tp[:, :-1], tp[:, 1:]], axis=2)

    qi = jnp.arange(BLOCK)[:, None]
    kj = jnp.arange(2 * BLOCK)[None, :]
    dist = BLOCK + qi - kj
    kpos = (jnp.arange(n_blk)[:, None, None] - 1) * BLOCK + kj[None] - front
    valid = (dist >= 0) & (dist <= WINDOW) & (kpos >= 0)
    out = sink_softmax_attend(qb, band(k), band(v), dist, valid[:, None, None], sinks, slopes)
    return out.reshape(b, n_blk * BLOCK, ATTN_WIDTH)[:, front:].astype(q.dtype)


def window_attention_step(q, k_ext, v_ext, sinks, slopes):
    b, S = q.shape[:2]
    qi = jnp.arange(S)[:, None]
    kj = jnp.arange(WINDOW + S)[None, :]
    dist = WINDOW + qi - kj
    valid = (dist >= 0) & (dist <= WINDOW)
    out = sink_softmax_attend(q, k_ext, v_ext, dist, valid, sinks, slopes)
    return out.reshape(b, S, ATTN_WIDTH).astype(q.dtype)


def multiscale_pool(u, first_pos, w_pool, pool_scale):
    b, R, _ = u.shape
    max_w = max(POOL_WINDOWS)
    uf = u.astype(F32)
    cs = jnp.pad(jnp.cumsum(uf, axis=1), ((0, 0), (max_w, 0), (0, 0)))
    pos = first_pos + jnp.arange(R)
    means = []
    for g, w in enumerate(POOL_WINDOWS):
        sl = slice(g * POOL_GROUP_WIDTH, (g + 1) * POOL_GROUP_WIDTH)
        win_sum = cs[:, max_w:, sl] - cs[:, max_w - w:max_w - w + R, sl]
        cnt = jnp.minimum(pos + 1, w).astype(F32)[None, :, None]
        means.append(win_sum / cnt)
    mean = jnp.stack(means, axis=2)
    ug = uf.reshape(b, R, N_POOL_GROUPS, POOL_GROUP_WIDTH)
    mixed = jnp.einsum('brgc,gcd->brgd', mean - ug, w_pool.astype(F32))
    return (mixed.reshape(b, R, POOL_WIDTH) * pool_scale.astype(F32)).astype(u.dtype)


def conv_ffn(up_ext, conv_w, conv_b, w_down):
    L = up_ext.shape[1] - CONV_STATE
    c = conv_b.astype(F32) + up_ext[:, 0:L].astype(F32) * conv_w[0].astype(F32)
    for i in range(1, CONV_WIDTH):
        c = c + up_ext[:, i:i + L].astype(F32) * conv_w[i].astype(F32)
    a = jax.nn.gelu(c[..., :D_FF], approximate=True) * c[..., D_FF:]
    return a.astype(up_ext.dtype) @ w_down


def decoder_layer(x, k_buf, v_buf, pool_buf, conv_buf, lw):
    (w_in, b_in, sinks, w_pool, pool_scale, g_attn_out, g_pool_out, w_o,
     g_pre_mix, g_post_mix, g_pre_ffn, g_post_ffn, w_up, conv_w, conv_b, w_down) = lw
    slopes = alibi_slopes()
    b = x.shape[0]
    h = rms_norm(x, g_pre_mix)
    q, k, v, u = mixer_projection(h, w_in, b_in)
    if k_buf is None:
        k_all, v_all, u_all = k, v, u
        attn = window_attention_prompt(q, k, v, sinks, slopes)
        pool = multiscale_pool(u, 0, w_pool, pool_scale)
    else:
        k_all = jnp.concatenate([k_buf.astype(k.dtype), k], axis=1)
        v_all = jnp.concatenate([v_buf.astype(v.dtype), v], axis=1)
        u_all = jnp.concatenate([pool_buf.astype(u.dtype), u], axis=1)
        attn = window_attention_step(q, k_all, v_all, sinks, slopes)
        pool = multiscale_pool(u_all, PAST_LEN - POOL_STATE, w_pool, pool_scale)[:, POOL_STATE:]
    mix = jnp.concatenate([rms_norm(attn, g_attn_out), rms_norm(pool, g_pool_out)], axis=-1) @ w_o
    x = x + rms_norm(mix, g_post_mix)
    up = rms_norm(x, g_pre_ffn) @ w_up
    if conv_buf is None:
        up_ext = jnp.concatenate([jnp.zeros((b, CONV_STATE, up.shape[-1]), up.dtype), up], axis=1)
    else:
        up_ext = jnp.concatenate([conv_buf.astype(up.dtype), up], axis=1)
    x = x + rms_norm(conv_ffn(up_ext, conv_w, conv_b, w_down), g_post_ffn)
    return (x, k_all[:, -WINDOW:], v_all[:, -WINDOW:], u_all[:, -POOL_STATE:], up_ext[:, -CONV_STATE:])


def setup_inputs(seed: int = 0) -> dict:
    key = jax.random.key(seed)
    ks = jax.random.split(key, 24)

    def nrm(k, shape, scale):
        return jax.random.normal(k, shape, F32) * scale

    def gain(k, shape):
        return 1.0 + nrm(k, shape, 0.05)

    return {
        'x_prompt': nrm(ks[0], (BATCH, SEQ, D_MODEL), 1.0),
        'x_sample': nrm(ks[1], (DEC_BATCH, DEC_SEQ, D_MODEL), 1.0),
        'cache_k': nrm(ks[2], (DEPTH, DEC_BATCH, WINDOW, N_KV_HEADS, HEAD_DIM), 1.0),
        'cache_v': nrm(ks[3], (DEPTH, DEC_BATCH, WINDOW, N_KV_HEADS, HEAD_DIM), 1.0),
        'state_pool': nrm(ks[4], (DEPTH, DEC_BATCH, POOL_STATE, POOL_WIDTH), 1.0),
        'state_conv': nrm(ks[5], (DEPTH, DEC_BATCH, CONV_STATE, 2 * D_FF), 1.0),
        'meta': nrm(ks[6], (N_META, D_MODEL), 1.0),
        'w_in': nrm(ks[7], (DEPTH, D_MODEL, IN_WIDTH), D_MODEL ** -0.5),
        'b_in': nrm(ks[8], (DEPTH, IN_WIDTH), 0.02),
        'sinks': nrm(ks[9], (DEPTH, N_HEADS), 1.0),
        'w_pool': nrm(ks[10], (DEPTH, N_POOL_GROUPS, POOL_GROUP_WIDTH, POOL_GROUP_WIDTH), POOL_GROUP_WIDTH ** -0.5),
        'pool_scale': 1.0 + nrm(ks[11], (DEPTH, POOL_WIDTH), 0.1),
        'g_attn_out': gain(ks[12], (DEPTH, ATTN_WIDTH)),
        'g_pool_out': gain(ks[13], (DEPTH, POOL_WIDTH)),
        'w_o': nrm(ks[14], (DEPTH, MIX_WIDTH, D_MODEL), MIX_WIDTH ** -0.5),
        'g_pre_mix': gain(ks[15], (DEPTH, D_MODEL)),
        'g_post_mix': gain(ks[16], (DEPTH, D_MODEL)),
        'g_pre_ffn': gain(ks[17], (DEPTH, D_MODEL)),
        'g_post_ffn': gain(ks[18], (DEPTH, D_MODEL)),
        'w_up': nrm(ks[19], (DEPTH, D_MODEL, 2 * D_FF), D_MODEL ** -0.5),
        'conv_w': nrm(ks[20], (DEPTH, CONV_WIDTH, 2 * D_FF), CONV_WIDTH ** -0.5),
        'conv_b': nrm(ks[21], (DEPTH, 2 * D_FF), 0.02),
        'w_down': nrm(ks[22], (DEPTH, D_FF, D_MODEL), D_FF ** -0.5),
    }


def reference(x_prompt, x_sample, cache_k, cache_v, state_pool, state_conv, meta,
              w_in, b_in, sinks, w_pool, pool_scale, g_attn_out, g_pool_out, w_o,
              g_pre_mix, g_post_mix, g_pre_ffn, g_post_ffn, w_up, conv_w, conv_b, w_down):
    b = x_prompt.shape[0]
    xp = jnp.concatenate(
        [jnp.broadcast_to(meta.astype(x_prompt.dtype)[None], (b, N_META, D_MODEL)), x_prompt], axis=1)
    xs = x_sample
    kp_l, vp_l, pp_l, cp_l = [], [], [], []
    ks_l, vs_l, ps_l, cs_l = [], [], [], []
    for l in range(DEPTH):
        lw = (w_in[l], b_in[l], sinks[l], w_pool[l], pool_scale[l], g_attn_out[l], g_pool_out[l], w_o[l],
              g_pre_mix[l], g_post_mix[l], g_pre_ffn[l], g_post_ffn[l], w_up[l], conv_w[l], conv_b[l], w_down[l])
        xp, kp, vp, pp, cp = decoder_layer(xp, None, None, None, None, lw)
        xs, k_s, v_s, p_s, c_s = decoder_layer(xs, cache_k[l], cache_v[l], state_pool[l], state_conv[l], lw)
        kp_l.append(kp); vp_l.append(vp); pp_l.append(pp); cp_l.append(cp)
        ks_l.append(k_s); vs_l.append(v_s); ps_l.append(p_s); cs_l.append(c_s)
    y_prompt = xp[:, N_META:]
    return (y_prompt, xs,
            jnp.stack(kp_l), jnp.stack(vp_l), jnp.stack(pp_l), jnp.stack(cp_l),
            jnp.stack(ks_l), jnp.stack(vs_l), jnp.stack(ps_l), jnp.stack(cs_l))
```

```python
import contextlib
import numpy as np
import concourse.bass as bass
import concourse.mybir as mybir
from concourse.bass_utils import run_bass_kernel_spmd

F32 = mybir.dt.float32
BF16 = mybir.dt.bfloat16
AF = mybir.ActivationFunctionType
ALU = mybir.AluOpType
AX = mybir.AxisListType

D = 1024
NIN = 1280
DFF = 2816
NUP = 5632
NJ = 22
SEQ = 4096
NBLK = 33
NS = 16
HAL = 16
EPS = 1e-6
NEG = -30000.0
POOL_W = (2, 4, 8, 16)

SAME_ENGINE_SYNC = True
NB1 = 4
NB2 = 4
DBG_NSUPER = None
DBG_STOP = None


class _Stop(Exception):
    pass


_STOPPED = [False]


def _stop(name):
    if DBG_STOP == name:
        _STOPPED[0] = True


class Buf:
    def __init__(self, name):
        self.name = name
        self.w = None
        self.r = {}
        self.dsem = None
        self.dcount = 0


class Tile:
    def __init__(self, t, b):
        self.t = t
        self.b = b

    def __getitem__(self, idx):
        return self.t[idx]


class Ring:
    def __init__(self, tiles):
        self.tiles = tiles
        self.i = -1

    def next(self):
        self.i = (self.i + 1) % len(self.tiles)
        return self.tiles[self.i]


class Eng:
    def __init__(self, name, obj, sem):
        self.name = name
        self.obj = obj
        self.sem = sem
        self.count = 0
        self.waited = {}


class Prog:
    def __init__(self, nc, stack):
        self.nc = nc
        self.stack = stack
        self.sems = {}
        self.engs = {}
        for n, o in [("pe", nc.tensor), ("act", nc.scalar), ("dve", nc.vector),
                     ("pool", nc.gpsimd), ("sp", nc.sync)]:
            s = stack.enter_context(nc.semaphore("sem_" + n))
            self.sems["e:" + n] = s
            self.engs[n] = Eng(n, o, s)
        self.nuid = 0
        self.dcounts = {}
        self.noinc = False

    def uid(self):
        self.nuid += 1
        return self.nuid

    def sb(self, name, shape, dtype, stack=None):
        st = stack if stack is not None else self.stack
        t = st.enter_context(self.nc.sbuf_tensor("%s_%d" % (name, self.uid()), list(shape), dtype))
        return Tile(t, Buf(name))

    def ps(self, name, shape, dtype, stack=None):
        st = stack if stack is not None else self.stack
        t = st.enter_context(self.nc.psum_tensor("%s_%d" % (name, self.uid()), list(shape), dtype))
        return Tile(t, Buf(name))

    def ring(self, name, shape, dtype, n, stack=None):
        return Ring([self.sb("%s%d" % (name, i), shape, dtype, stack) for i in range(n)])

    def _dsem(self, b, queue):
        if b.dsem is None:
            b.dsem = {}
        if queue not in b.dsem:
            key = "d:%s:%s:%d" % (b.name, queue, self.uid())
            s = self.stack.enter_context(self.nc.semaphore("ds_%d" % len(self.sems)))
            self.sems[key] = s
            b.dsem[queue] = key
            self.dcounts[key] = 0
        return b.dsem[queue]

    def _wait(self, e, deps):
        for key, val in deps.items():
            if key == "e:" + e.name and not (SAME_ENGINE_SYNC and e.name in ("act", "dve", "pool")):
                continue
            if key in self.dcounts:
                val = 16 * self.dcounts[key]
            if e.waited.get(key, 0) >= val:
                continue
            e.obj.wait_ge(self.sems[key], val)
            e.waited[key] = val

    @staticmethod
    def _collect(reads, writes):
        deps = {}

        def add(d):
            if d is None:
                return
            k, v = d
            if deps.get(k, 0) < v:
                deps[k] = v
        for b in reads:
            add(b.w)
        for b in writes:
            add(b.w)
            for k, v in b.r.items():
                add((k, v))
        return deps

    @staticmethod
    def _commit(dep, reads, writes):
        k, v = dep
        for b in reads:
            if b in writes:
                continue
            if b.r.get(k, 0) < v:
                b.r[k] = v
        for b in writes:
            b.w = dep
            b.r = {}

    @staticmethod
    def _bufs(xs):
        out = []
        for x in xs:
            if isinstance(x, (list, tuple)):
                out.extend(Prog._bufs(x))
            else:
                out.append(x.b if isinstance(x, Tile) else x)
        return out

    def op(self, eng, fn, reads=(), writes=()):
        noinc = self.noinc and eng == "pe"
        self.noinc = False
        if _STOPPED[0]:
            return None
        reads = self._bufs(reads)
        writes = self._bufs(writes)
        e = self.engs[eng]
        self._wait(e, self._collect(reads, writes))
        ins = fn(e.obj)
        if noinc:
            self._commit(("e:" + eng, e.count + 1), reads, writes)
            return ins
        e.count += 1
        ins.then_inc(e.sem, 1)
        self._commit(("e:" + eng, e.count), reads, writes)
        return ins

    def dma(self, queue, out, in_, reads=(), writes=(), sembuf=None, **kw):
        if _STOPPED[0]:
            return None
        reads = self._bufs(reads)
        writes = self._bufs(writes)
        e = self.engs[queue]
        self._wait(e, self._collect(reads, writes))
        if sembuf is None:
            sembuf = (list(writes) + list(reads))[0]
        elif isinstance(sembuf, Tile):
            sembuf = sembuf.b
        key = self._dsem(sembuf, queue)
        ins = e.obj.dma_start(out=out, in_=in_, **kw)
        ins.then_inc(self.sems[key], 16)
        self.dcounts[key] += 1
        self._commit((key, 16 * self.dcounts[key]), reads, writes)
        return ins

    def barrier(self):
        if _STOPPED[0]:
            return
        targets = {}
        for n, e in self.engs.items():
            if e.count > 0:
                targets["e:" + n] = e.count
        for k, c in self.dcounts.items():
            targets[k] = 16 * c
        for n, e in self.engs.items():
            deps = {k: v for k, v in targets.items() if k != "e:" + n}
            self._wait(e, deps)

    def finish(self, eng="sp"):
        e = self.engs[eng]
        targets = {}
        for n, o in self.engs.items():
            if o.count > 0 and n != eng:
                targets["e:" + n] = o.count
        for k, c in self.dcounts.items():
            targets[k] = 16 * c
        self._wait(e, targets)


def build_program():
    _STOPPED[0] = False
    nc = bass.Bass("TRN2", target_bir_lowering=False)

    def din(name, shape):
        return nc.dram_tensor(name, list(shape), F32, kind="ExternalInput").ap()

    def dout(name, shape):
        return nc.dram_tensor(name, list(shape), F32, kind="ExternalOutput").ap()

    xp = din("xp", [SEQ, D])
    meta = din("meta", [16, D])
    xs = din("xs", [NS, D])
    ck = din("ck", [NS, 128, 128])
    cv = din("cv", [NS, 128, 128])
    spool = din("spool", [NS, 15, 512])
    sconv = din("sconv", [NS * 2, NUP])
    w_in = din("w_in", [D, NIN])
    b_in = din("b_in", [1, NIN])
    sinks = din("sinks", [1, 8])
    w_pool = din("w_pool", [512, 128])
    pool_scale = din("pool_scale", [1, 512])
    g_attn = din("g_attn", [1, 512])
    g_pool = din("g_pool", [1, 512])
    w_o = din("w_o", [D, D])
    g_pre_mix = din("g_pre_mix", [1, D])
    g_post_mix = din("g_post_mix", [1, D])
    g_pre_ffn = din("g_pre_ffn", [1, D])
    g_post_ffn = din("g_post_ffn", [1, D])
    w_up = din("w_up", [D, NUP])
    conv_w = din("conv_w", [3, NUP])
    conv_b = din("conv_b", [1, NUP])
    w_down = din("w_down", [DFF, D])
    c_ident = din("c_ident", [128, 128])
    c_bbase = din("c_bbase", [128, 8 * 256])
    c_cinv = din("c_cinv", [1, 4 * 128])
    c_sel = din("c_sel", [120, 2 * 4 * 16])
    c_sbias = din("c_sbias", [128, 129])

    y_prompt = dout("y_prompt", [SEQ, D])
    y_sample = dout("y_sample", [NS, D])
    k_prompt = dout("k_prompt", [128, 128])
    v_prompt = dout("v_prompt", [128, 128])
    pool_prompt = dout("pool_prompt", [15, 512])
    conv_prompt = dout("conv_prompt", [2, NUP])
    k_sample = dout("k_sample", [NS, 128, 128])
    v_sample = dout("v_sample", [NS, 128, 128])
    pool_sample = dout("pool_sample", [NS, 15, 512])
    conv_sample = dout("conv_sample", [NS, 2, NUP])

    x1d = nc.dram_tensor("x1_scratch", [(NBLK + 1) * 128, D], F32, kind="Internal").ap()
    x1d_bufs = [Buf("x1d%d" % i) for i in range(NBLK + 1)]
    sc_d = nc.dram_tensor("sconvT_scratch", [128, 44 * 32], F32, kind="Internal").ap()
    cw_d = nc.dram_tensor("convwT_scratch", [128, 44 * 4], F32, kind="Internal").ap()
    csd_buf = Buf("csd")
    outb = Buf("outs")

    with contextlib.ExitStack() as top:
        P = Prog(nc, top)
        try:
            _body(P, nc, locals())
        except _Stop:
            pass
        P.finish("sp")
        P.finish("pool")
    return nc


def _body(P, nc, L):
    (xp, meta, xs, ck, cv, spool, sconv, w_in, b_in, sinks, w_pool, pool_scale, g_attn, g_pool, w_o, g_pre_mix,
     g_post_mix, g_pre_ffn, g_post_ffn, w_up, conv_w, conv_b, w_down, c_ident, c_bbase, c_cinv, c_sel, c_sbias,
     y_prompt, y_sample, k_prompt, v_prompt, pool_prompt, conv_prompt, k_sample, v_sample, pool_sample, conv_sample,
     x1d, x1d_bufs, outb, sc_d, cw_d, csd_buf) = [L[k] for k in (
        "xp meta xs ck cv spool sconv w_in b_in sinks w_pool pool_scale g_attn g_pool w_o g_pre_mix "
        "g_post_mix g_pre_ffn g_post_ffn w_up conv_w conv_b w_down c_ident c_bbase c_cinv c_sel c_sbias "
        "y_prompt y_sample k_prompt v_prompt pool_prompt conv_prompt k_sample v_sample pool_sample conv_sample "
        "x1d x1d_bufs outb sc_d cw_d csd_buf").split()]
    if True:

        ident_f = P.sb("ident_f", [128, 128], F32)
        ident_b = P.sb("ident_b", [128, 128], BF16)
        eps_t = P.sb("eps", [128, 1], F32)
        P.dma("sp", ident_f[:], c_ident, writes=[ident_f])
        P.op("dve", lambda e: e.tensor_copy(out=ident_b[:], in_=ident_f[:]), reads=[ident_f], writes=[ident_b])
        P.op("dve", lambda e: e.memset(eps_t[:], EPS), writes=[eps_t])
        mhalf_t = P.sb("mhalf", [128, 1], F32)
        P.op("dve", lambda e: e.memset(mhalf_t[:], -0.5), writes=[mhalf_t])

        stat_ring = P.ring("stat", [128, 8], F32, 16)
        _stop("setup0")

        def rstd_from_ssq(ssq_ap, ssq_tile, n, M=128):
            stt = stat_ring.next()
            P.op("dve", lambda e: e.tensor_scalar(out=stt[0:M, 0:1], in0=ssq_ap, scalar1=1.0 / n, scalar2=EPS,
                                                  op0=ALU.mult, op1=ALU.add), reads=[ssq_tile], writes=[stt])
            P.op("pool", lambda e: e.tensor_tensor(out=stt[0:M, 2:3], in0=stt[0:M, 0:1], in1=mhalf_t[0:M, 0:1], op=ALU.pow),
                 reads=[stt, mhalf_t], writes=[stt])
            return stt, stt[0:M, 2:3]

        with contextlib.ExitStack() as s01:
            W_in_sb = P.sb("W_in", [128, 8, NIN], BF16, s01)
            W_kd = P.sb("W_kd", [128, 8, 2, 2, 64], BF16, s01)
            W_o_sb = P.sb("W_o", [128, 8, D], BF16, s01)
            W_pool_sb = P.sb("W_pool", [128, 4, 128], BF16, s01)
            b_in_bc = P.sb("b_in_bc", [128, NIN], F32, s01)
            b_fm = P.sb("b_fm", [128, 10], F32, s01)
            b_kd = P.sb("b_kd", [128, 2], F32, s01)
            g_pre_mix_bc = P.sb("g_pre_mix_bc", [128, D], F32, s01)
            g_mix_bc = P.sb("g_mix_bc", [128, D], F32, s01)
            g_post_mix_bc = P.sb("g_post_mix_bc", [128, D], F32, s01)
            pool_scale_bc = P.sb("pool_scale_bc", [128, 512], F32, s01)
            sink_bc = P.sb("sink_bc", [128, 8], F32, s01)
            B_hi = P.sb("B_hi", [128, 8, 256], BF16, s01)
            B_lo = P.sb("B_lo", [128, 8, 256], BF16, s01)
            cinv_bc = P.sb("cinv_bc", [128, 4, 128], F32, s01)

            P.dma("pool", W_in_sb[:], w_in.rearrange("(c p) n -> p c n", p=128), writes=[W_in_sb])
            for dup in range(2):
                for g in range(2):
                    P.dma("pool", W_kd[:, :, g, dup, :],
                          w_in[:, 512 + g * 64:512 + (g + 1) * 64].rearrange("(c p) d -> p c d", p=128), writes=[W_kd])
            P.dma("pool", W_o_sb[:], w_o.rearrange("(c p) n -> p c n", p=128), writes=[W_o_sb])
            P.dma("pool", W_pool_sb[:], w_pool.rearrange("(g c) d -> c g d", c=128), writes=[W_pool_sb])
            P.dma("sp", b_in_bc[:], b_in.partition_broadcast(128), writes=[b_in_bc])
            P.dma("sp", g_pre_mix_bc[:], g_pre_mix.partition_broadcast(128), writes=[g_pre_mix_bc])
            P.dma("sp", g_mix_bc[:, 0:512], g_attn.partition_broadcast(128), writes=[g_mix_bc])
            P.dma("sp", g_mix_bc[:, 512:1024], g_pool.partition_broadcast(128), writes=[g_mix_bc])
            P.dma("sp", g_post_mix_bc[:], g_post_mix.partition_broadcast(128), writes=[g_post_mix_bc])
            P.dma("sp", pool_scale_bc[:], pool_scale.partition_broadcast(128), writes=[pool_scale_bc])
            P.dma("sp", sink_bc[:], sinks.partition_broadcast(128), writes=[sink_bc])
            P.dma("sp", cinv_bc[:], c_cinv.rearrange("o (g t) -> o g t", g=4).partition_broadcast(128),
                  writes=[cinv_bc])
            ssq_ring = P.ring("ssq", [128, 4], F32, 12, s01)
            RG = {}

            def make_rings(stack, deep):
                RG["xb"] = P.ring("xb", [128, D], BF16, 2 if deep else 1, stack)
                RG["attn"] = P.ring("attn_sb", [128, 512], F32, 4 if deep else 1, stack)
                RG["pool_sb"] = P.ring("pool_sb", [128, 512], F32, 4 if deep else 1, stack)
                RG["mix_in"] = P.ring("mix_in", [128, D], BF16, 4 if deep else 1, stack)
                RG["mixT"] = P.ring("mixT", [128, 8, 128], BF16, 2 if deep else 1, stack)
                RG["x1"] = P.ring("x1", [128, D], F32, 2 if deep else 1, stack)

            class View:
                def __init__(self, ap, bufs):
                    self.t = ap
                    self.bufs = bufs

                def __getitem__(self, idx):
                    return self.t[idx]

            Q = [P.ps("Q%d" % i, [128, 1024], F32, s01) for i in range(4)]
            Hb = [Buf("H%d" % i) for i in range(8)]

            def half_ap(i):
                return Q[i // 2].t[:, (i % 2) * 512:(i % 2 + 1) * 512]

            tr_ring = Ring([View(half_ap(i).bitcast(BF16).rearrange("p (c t) -> p c t", c=8), [Hb[i]]) for i in (0, 1)])
            trA_ring = Ring([View(half_ap(i).bitcast(BF16).rearrange("p (c t) -> p c t", c=8), [Hb[i]]) for i in (6, 7)])
            mm_slots = [(half_ap(i), Hb[i]) for i in (2, 3, 4, 5, 6, 7)]
            mm_i = [0]
            sc_ring = Ring([View(Q[k].t[:].rearrange("p (j k q) -> p j k q", j=4, k=2), [Hb[2 * k], Hb[2 * k + 1]]) for k in (1, 2)])
            o_ring = Ring([View(half_ap(i).rearrange("p (j d) -> p j d", j=4), [Hb[i]]) for i in (6, 7)])
            wo_ring = Ring([View(Q[k].t[:], [Hb[2 * k], Hb[2 * k + 1]]) for k in (1, 2)])

            def next_mm():
                mm_i[0] = (mm_i[0] + 1) % len(mm_slots)
                return mm_slots[mm_i[0]]

            with contextlib.ExitStack() as sb0:
                brow = P.sb("brow", [1, NIN + 256], F32, sb0)
                P.dma("sp", brow[0:1, 0:NIN], b_in, writes=[brow])
                for g in range(2):
                    for dup in range(2):
                        c0 = NIN + g * 128 + dup * 64
                        P.dma("sp", brow[0:1, c0:c0 + 64], b_in[:, 512 + g * 64:512 + (g + 1) * 64], writes=[brow])
                bap, bbuf = mm_slots[0]
                for c in range(12):
                    P.op("pe", lambda e, c=c: e.transpose(out=bap[:, c:c + 1], in_=brow[0:1, c * 128:(c + 1) * 128],
                                                          identity=ident_f[0:1, 0:1]),
                         reads=[brow, ident_f], writes=[bbuf])
                P.op("dve", lambda e: e.tensor_copy(out=b_fm[:], in_=bap[:, 0:10]), reads=[bbuf], writes=[b_fm])
                P.op("dve", lambda e: e.tensor_scalar(out=b_fm[:, 0:4], in0=b_fm[:, 0:4], scalar1=0.125, scalar2=None, op0=ALU.mult),
                     writes=[b_fm])
                Bfull = P.sb("Bfull", [128, 8, 256], F32, sb0)
                P.dma("sp", Bfull[:], c_bbase.rearrange("p (h k) -> p h k", h=8), writes=[Bfull])
                for h in range(8):
                    P.op("dve", lambda e, h=h: e.tensor_scalar(out=Bfull[:, h, :], in0=Bfull[:, h, :],
                                                               scalar1=sink_bc[:, h:h + 1], scalar2=None, op0=ALU.subtract),
                         reads=[sink_bc], writes=[Bfull])
                P.op("dve", lambda e: e.tensor_copy(out=B_hi[:], in_=Bfull[:]), reads=[Bfull], writes=[B_hi])
                P.op("dve", lambda e: e.tensor_tensor(out=B_lo[:], in0=Bfull[:], in1=B_hi[:], op=ALU.subtract),
                     reads=[Bfull, B_hi], writes=[B_lo])
                P.op("dve", lambda e: e.tensor_copy(out=b_kd[:], in_=bap[:, 10:12]), reads=[bbuf], writes=[b_kd])
                P.barrier()

            def run(gen):
                for _ in gen:
                    pass

            def lockstep(gens):
                gens = list(gens)
                while gens:
                    for g_ in list(gens):
                        try:
                            next(g_)
                        except StopIteration:
                            gens.remove(g_)

            def norm_T(x_ap, x_tile, g_bc, dstT, col0):
                run(norm_T_gen(x_ap, x_tile, g_bc, dstT, col0))

            def norm_T_gen(x_ap, x_tile, g_bc, dstT, col0, trr=None):
                ssq = ssq_ring.next()
                xb = RG["xb"].next()
                tr = (trr if trr is not None else tr_ring).next()
                P.op("act", lambda e: e.activation(out=xb[:], in_=x_ap, func=AF.Square, accum_out=ssq[:, 0:1]),
                     reads=[x_tile], writes=[xb, ssq])
                yield
                stt, rstd = rstd_from_ssq(ssq[:, 0:1], ssq, D)
                yield
                P.op("dve", lambda e: e.scalar_tensor_tensor(out=xb[:], in0=x_ap, scalar=rstd, in1=g_bc[:],
                                                             op0=ALU.mult, op1=ALU.mult),
                     reads=[x_tile, stt, g_bc], writes=[xb])
                yield
                for c in range(8):
                    P.noinc = (c != 7)
                    P.op("pe", lambda e, c=c: e.transpose(out=tr[:, c, :], in_=xb[:, c * 128:(c + 1) * 128],
                                                          identity=ident_b[:]),
                         reads=[xb, ident_b], writes=tr.bufs)
                yield
                P.op("act", lambda e: e.copy(out=dstT[:, :, col0:col0 + 128], in_=tr[:]),
                     reads=tr.bufs, writes=[dstT])

            def tail_F_gen(attn, pool_mm_fn, out):
                pool_sb = RG["pool_sb"].next()
                mix_in = RG["mix_in"].next()
                ssq = ssq_ring.next()
                out["mix"] = mix_in
                pool_ps_ap, pool_ps_buf = pool_mm_fn()
                P.op("dve", lambda e: e.tensor_tensor(out=pool_sb[:], in0=pool_ps_ap, in1=pool_scale_bc[:], op=ALU.mult),
                     reads=[pool_ps_buf, pool_scale_bc], writes=[pool_sb])
                yield
                P.op("act", lambda e: e.activation(out=mix_in[:, 0:512], in_=attn[:], func=AF.Square, accum_out=ssq[:, 0:1]),
                     reads=[attn], writes=[mix_in, ssq])
                yield
                P.op("act", lambda e: e.activation(out=mix_in[:, 512:1024], in_=pool_sb[:], func=AF.Square, accum_out=ssq[:, 1:2]),
                     reads=[pool_sb], writes=[mix_in, ssq])
                st_a, r_a = rstd_from_ssq(ssq[:, 0:1], ssq, 512)
                yield
                P.op("dve", lambda e: e.scalar_tensor_tensor(out=mix_in[:, 0:512], in0=attn[:], scalar=r_a,
                                                             in1=g_mix_bc[:, 0:512], op0=ALU.mult, op1=ALU.mult),
                     reads=[attn, st_a, g_mix_bc], writes=[mix_in])
                st_p, r_p = rstd_from_ssq(ssq[:, 1:2], ssq, 512)
                yield
                P.op("dve", lambda e: e.scalar_tensor_tensor(out=mix_in[:, 512:1024], in0=pool_sb[:], scalar=r_p,
                                                             in1=g_mix_bc[:, 512:1024], op0=ALU.mult, op1=ALU.mult),
                     reads=[pool_sb, st_p, g_mix_bc], writes=[mix_in])

            def tail_G_gen(mix_in, x_ap, x_tile, x1row, x1buf):
                tr = tr_ring.next()
                mixT = RG["mixT"].next()
                wo = wo_ring.next()
                ssq2 = ssq_ring.next()
                x1 = RG["x1"].next()
                for c in range(8):
                    P.noinc = (c != 7)
                    P.op("pe", lambda e, c=c: e.transpose(out=tr[:, c, :], in_=mix_in[:, c * 128:(c + 1) * 128],
                                                          identity=ident_b[:]),
                         reads=[mix_in, ident_b], writes=tr.bufs)
                yield
                P.op("act", lambda e: e.copy(out=mixT[:], in_=tr[:]), reads=tr.bufs, writes=[mixT])
                yield
                for half in range(2):
                    for c in range(8):
                        P.noinc = (c != 7)
                        P.op("pe", lambda e, c=c, half=half: e.matmul(
                            out=wo[:, half * 512:(half + 1) * 512], lhsT=mixT[:, c, :],
                            rhs=W_o_sb[:, c, half * 512:(half + 1) * 512], start=(c == 0), stop=(c == 7)),
                            reads=[mixT, W_o_sb], writes=[wo.bufs[half]])
                    yield
                for half in range(2):
                    P.op("act", lambda e, half=half: e.activation(
                        out=x1[:, half * 512:(half + 1) * 512], in_=wo[:, half * 512:(half + 1) * 512],
                        func=AF.Square, accum_out=ssq2[:, half:half + 1]),
                        reads=[wo.bufs[half]], writes=[x1, ssq2])
                    yield
                P.op("dve", lambda e: e.tensor_tensor(out=ssq2[:, 2:3], in0=ssq2[:, 0:1], in1=ssq2[:, 1:2], op=ALU.add),
                     reads=[ssq2], writes=[ssq2])
                yield
                st_m, r_m = rstd_from_ssq(ssq2[:, 2:3], ssq2, D)
                yield
                for half in range(2):
                    P.op("dve", lambda e, half=half: e.scalar_tensor_tensor(
                        out=x1[:, half * 512:(half + 1) * 512], in0=wo[:, half * 512:(half + 1) * 512], scalar=r_m,
                        in1=g_post_mix_bc[:, half * 512:(half + 1) * 512], op0=ALU.mult, op1=ALU.mult),
                        reads=[wo.bufs[half], st_m, g_post_mix_bc], writes=[x1])
                    yield
                P.op("dve", lambda e: e.tensor_tensor(out=x1[:], in0=x1[:], in1=x_ap, op=ALU.add),
                     reads=[x_tile], writes=[x1])
                yield
                P.dma("sp", x1d[x1row:x1row + 128, :], x1[:], reads=[x1], writes=[x1buf], sembuf=x1)

            def mix_tail(x_ap, x_tile, attn, pool_ps_ap, pool_ps_buf, x1row, x1buf):
                out = {}
                run(tail_F_gen(attn, lambda: (pool_ps_ap, pool_ps_buf), out))
                run(tail_G_gen(out["mix"], x_ap, x_tile, x1row, x1buf))

            _stop("setup1")
            with contextlib.ExitStack() as s0:
                x_s = P.sb("x_s", [128, D], F32, s0)
                xT_s = P.sb("xT_s", [128, 8, 128], BF16, s0)
                z_s = P.sb("z_s", [128, NIN], F32, s0)
                q_hb = P.sb("q_hb", [128, 64], F32, s0)
                kn_hb = P.sb("kn_hb", [128, 64], F32, s0)
                vn_hb = P.sb("vn_hb", [128, 64], F32, s0)
                sink_hb = P.sb("sink_hb", [128, 1], F32, s0)
                sbias = P.sb("sbias", [128, 129], F32, s0)
                make_rings(s0, False)
                Kc = P.sb("Kc", [128, 128, 64], F32, s0)
                Vc = P.sb("Vc", [128, 128, 64], F32, s0)
                Kb = [Buf("Kc%d" % h) for h in range(8)]
                Vb = [Buf("Vc%d" % h) for h in range(8)]
                prod = P.sb("prod", [128, 128, 64], F32, s0)
                Sall = P.sb("Sall", [128, 129], F32, s0)
                Pm = P.sb("Pm", [128, 129], F32, s0)
                sm = P.sb("sm", [128, 8], F32, s0)
                o_hb = P.sb("o_hb", [128, 64], F32, s0)
                attn_s = P.sb("attn_s", [128, 512], F32, s0)
                spl = P.sb("spl", [128, 2, 512], F32, s0)
                sel = P.sb("sel", [128, 2, 4, 16], F32, s0)
                wsum = P.sb("wsum", [128, 512], F32, s0)
                d_s = P.sb("d_s", [128, 512], BF16, s0)
                dT_s = P.sb("dT_s", [128, 4, 128], BF16, s0)

                P.op("pool", lambda e: e.memset(x_s[:], 0.0), writes=[x_s])
                P.dma("sp", x_s[0:NS, :], xs, writes=[x_s])
                cstage = prod.t[0:36, 0:88, :].rearrange("p a b -> p (a b)")
                rs_sc = prod.t[:, 96:118, :].rearrange("p a b -> p (a b)").rearrange("p (c r) -> p c r", r=32)
                rs_cw = prod.t[:, 118:121, :].rearrange("p a b -> p (a b)")[:, 0:176].rearrange("p (c r) -> p c r", r=4)
                P.dma("sp", cstage[0:32, :], sconv, writes=[prod])
                P.dma("sp", cstage[32:35, :], conv_w, writes=[prod])
                P.dma("sp", cstage[35:36, :], conv_b, writes=[prod])
                for g0 in range(0, 44, 14):
                    n = min(14, 44 - g0)
                    bap_, bbuf_ = next_mm()
                    for k in range(n):
                        P.op("pe", lambda e, k=k: e.transpose(out=bap_[:, k * 36:(k + 1) * 36], in_=cstage[0:36, (g0 + k) * 128:(g0 + k + 1) * 128],
                                                              identity=ident_f[0:36, 0:36]),
                             reads=[prod, ident_f], writes=[bbuf_])
                    pv = bap_[:, 0:n * 36].rearrange("p (c r) -> p c r", r=36)
                    P.op("dve", lambda e: e.tensor_copy(out=rs_sc[:, g0:g0 + n, :], in_=pv[:, :, 0:32]), reads=[bbuf_], writes=[prod])
                    P.op("dve", lambda e: e.tensor_copy(out=rs_cw[:, g0:g0 + n, :], in_=pv[:, :, 32:36]), reads=[bbuf_], writes=[prod])
                P.dma("sp", sc_d, prod.t[:, 96:118, :].rearrange("p a b -> p (a b)"), reads=[prod], writes=[csd_buf], sembuf=prod)
                P.dma("sp", cw_d, prod.t[:, 118:121, :].rearrange("p a b -> p (a b)")[:, 0:176], reads=[prod], writes=[csd_buf], sembuf=prod)
                P.dma("sp", sbias[:], c_sbias, writes=[sbias])
                P.dma("sp", sel[0:120].rearrange("p t g b -> p (t g b)"), c_sel, writes=[sel])
                for t in range(2):
                    P.dma("sp", spl[0:120, t, :], spool[t * 8:(t + 1) * 8].rearrange("b r c -> (b r) c"), writes=[spl])
                def load_cache(dst, bufs, src):
                    for g in range(2):
                        h0 = 4 * g
                        P.dma("act", dst[h0 * 16:(h0 + 1) * 16], src[:, :, g * 64:(g + 1) * 64], writes=[bufs[h0]])
                    for g in range(2):
                        h0 = 4 * g
                        for j in range(1, 4):
                            P.dma("sp", dst[(h0 + j) * 16:(h0 + j + 1) * 16], dst[h0 * 16:(h0 + 1) * 16],
                                  reads=[bufs[h0]], writes=[bufs[h0 + j]])
                load_cache(Kc, Kb, ck)
                load_cache(Vc, Vb, cv)
                for h in range(8):
                    P.dma("sp", sink_hb[h * 16:(h + 1) * 16, :], sinks[:, h:h + 1].partition_broadcast(16), writes=[sink_hb])
                P.dma("sp", k_sample[:, 0:127, :], ck[:, 1:128, :], writes=[outb])
                P.dma("sp", v_sample[:, 0:127, :], cv[:, 1:128, :], writes=[outb])
                P.dma("sp", pool_sample[:, 0:14, :], spool[:, 1:15, :], writes=[outb])

                norm_T(x_s[:], x_s, g_pre_mix_bc, xT_s, 0)
                for (n0, n1) in ((0, 512), (512, 1024), (1024, NIN)):
                    ap, b = next_mm()
                    for c in range(8):
                        P.op("pe", lambda e, c=c, ap=ap, n0=n0, n1=n1: e.matmul(
                            out=ap[:, 0:n1 - n0], lhsT=xT_s[:, c, :], rhs=W_in_sb[:, c, n0:n1],
                            start=(c == 0), stop=(c == 7)), reads=[xT_s, W_in_sb], writes=[b])
                    P.op("dve", lambda e, ap=ap, n0=n0, n1=n1: e.tensor_tensor(
                        out=z_s[:, n0:n1], in0=ap[:, 0:n1 - n0], in1=b_in_bc[:, n0:n1], op=ALU.add),
                        reads=[b, b_in_bc], writes=[z_s])
                P.dma("pool", k_sample[:, 127, :], z_s[0:NS, 512:640], reads=[z_s], writes=[outb], sembuf=z_s)
                P.dma("pool", v_sample[:, 127, :], z_s[0:NS, 640:768], reads=[z_s], writes=[outb], sembuf=z_s)
                P.dma("pool", pool_sample[:, 14, :], z_s[0:NS, 768:1280], reads=[z_s], writes=[outb], sembuf=z_s)
                _stop("p0a")
                for h in range(8):
                    g = h // 4
                    P.dma("sp", q_hb[h * 16:(h + 1) * 16, :], z_s[0:NS, h * 64:(h + 1) * 64], reads=[z_s], writes=[q_hb])
                    P.dma("sp", kn_hb[h * 16:(h + 1) * 16, :], z_s[0:NS, 512 + g * 64:512 + (g + 1) * 64], reads=[z_s], writes=[kn_hb])
                    P.dma("sp", vn_hb[h * 16:(h + 1) * 16, :], z_s[0:NS, 640 + g * 64:640 + (g + 1) * 64], reads=[z_s], writes=[vn_hb])
                P.op("dve", lambda e: e.tensor_tensor(out=prod[:], in0=Kc[:], in1=q_hb[:].unsqueeze(1).to_broadcast([128, 128, 64]),
                                                      op=ALU.mult), reads=Kb + [q_hb], writes=[prod])
                P.op("dve", lambda e: e.tensor_reduce(out=Sall[:, 0:128], in_=prod[:], axis=AX.X, op=ALU.add),
                     reads=[prod], writes=[Sall])
                P.op("dve", lambda e: e.tensor_tensor(out=o_hb[:], in0=kn_hb[:], in1=q_hb[:], op=ALU.mult),
                     reads=[kn_hb, q_hb], writes=[o_hb])
                P.op("dve", lambda e: e.tensor_reduce(out=Sall[:, 128:129], in_=o_hb[:], axis=AX.X, op=ALU.add),
                     reads=[o_hb], writes=[Sall])
                P.op("dve", lambda e: e.scalar_tensor_tensor(out=Sall[:], in0=Sall[:], scalar=0.125, in1=sbias[:],
                                                             op0=ALU.mult, op1=ALU.add), reads=[sbias], writes=[Sall])
                P.op("dve", lambda e: e.tensor_reduce(out=sm[:, 0:1], in_=Sall[:], axis=AX.X, op=ALU.max),
                     reads=[Sall], writes=[sm])
                P.op("dve", lambda e: e.tensor_tensor(out=sm[:, 1:2], in0=sm[:, 0:1], in1=sink_hb[:], op=ALU.max),
                     reads=[sink_hb], writes=[sm])
                P.op("dve", lambda e: e.tensor_scalar(out=sm[:, 2:3], in0=sm[:, 1:2], scalar1=-1.0, scalar2=None, op0=ALU.mult),
                     writes=[sm])
                P.op("act", lambda e: e.activation(out=Pm[:], in_=Sall[:], func=AF.Exp, bias=sm[:, 2:3], scale=1.0,
                                                   accum_out=sm[:, 3:4]), reads=[Sall, sm], writes=[Pm, sm])
                P.op("act", lambda e: e.activation(out=sm[:, 4:5], in_=sink_hb[:], func=AF.Exp, bias=sm[:, 2:3], scale=1.0),
                     reads=[sink_hb], writes=[sm])
                P.op("dve", lambda e: e.tensor_tensor(out=sm[:, 5:6], in0=sm[:, 3:4], in1=sm[:, 4:5], op=ALU.add), writes=[sm])
                P.op("dve", lambda e: e.reciprocal(out=sm[:, 6:7], in_=sm[:, 5:6]), writes=[sm])
                P.op("dve", lambda e: e.tensor_tensor(out=prod[:], in0=Vc[:],
                                                      in1=Pm[:, 0:128].unsqueeze(2).to_broadcast([128, 128, 64]),
                                                      op=ALU.mult), reads=Vb + [Pm], writes=[prod])
                P.op("dve", lambda e: e.tensor_reduce(out=o_hb[:], in_=prod[:].rearrange("p k d -> p d k"), axis=AX.X,
                                                      op=ALU.add), reads=[prod], writes=[o_hb])
                P.op("dve", lambda e: e.scalar_tensor_tensor(out=o_hb[:], in0=vn_hb[:], scalar=Pm[:, 128:129], in1=o_hb[:],
                                                             op0=ALU.mult, op1=ALU.add), reads=[vn_hb, Pm], writes=[o_hb])
                P.op("dve", lambda e: e.tensor_scalar(out=o_hb[:], in0=o_hb[:], scalar1=sm[:, 6:7], scalar2=None, op0=ALU.mult),
                     reads=[sm], writes=[o_hb])
                P.op("pool", lambda e: e.memset(attn_s[:], 0.0), writes=[attn_s])
                for h in range(8):
                    P.dma("sp", attn_s[0:NS, h * 64:(h + 1) * 64], o_hb[h * 16:(h + 1) * 16, :], reads=[o_hb], writes=[attn_s])
                _stop("p0b")
                ap, b = next_mm()
                for g in range(4):
                    for t in range(2):
                        P.op("pe", lambda e, g=g, t=t, ap=ap: e.matmul(
                            out=ap[0:NS, g * 128:(g + 1) * 128], lhsT=sel[0:120, t, g, :],
                            rhs=spl[0:120, t, g * 128:(g + 1) * 128], start=(t == 0), stop=(t == 1)),
                            reads=[sel, spl], writes=[b])
                P.op("pool", lambda e: e.memset(d_s[:], 0.0), writes=[d_s])
                P.op("dve", lambda e, ap=ap: e.tensor_tensor(out=wsum[0:NS, :], in0=ap[0:NS, :], in1=z_s[0:NS, 768:1280], op=ALU.add),
                     reads=[b, z_s], writes=[wsum])
                for g in range(4):
                    P.op("dve", lambda e, g=g: e.scalar_tensor_tensor(
                        out=d_s[0:NS, g * 128:(g + 1) * 128], in0=wsum[0:NS, g * 128:(g + 1) * 128],
                        scalar=1.0 / POOL_W[g], in1=z_s[0:NS, 768 + g * 128:768 + (g + 1) * 128],
                        op0=ALU.mult, op1=ALU.subtract), reads=[wsum, z_s], writes=[d_s])
                trs = tr_ring.next()
                for g in range(4):
                    P.op("pe", lambda e, g=g: e.transpose(out=trs[:, g, :], in_=d_s[:, g * 128:(g + 1) * 128], identity=ident_b[:]),
                         reads=[d_s, ident_b], writes=trs.bufs)
                P.op("dve", lambda e: e.tensor_copy(out=dT_s[:], in_=trs[:, 0:4, :]), reads=trs.bufs, writes=[dT_s])
                pap, pb = next_mm()
                for g in range(4):
                    P.op("pe", lambda e, g=g, pap=pap: e.matmul(out=pap[:, g * 128:(g + 1) * 128], lhsT=dT_s[:, g, :],
                                                                  rhs=W_pool_sb[:, g, :], start=True, stop=True),
                         reads=[dT_s, W_pool_sb], writes=[pb])
                mix_tail(x_s[:], x_s, attn_s, pap, pb, NBLK * 128, x1d_bufs[NBLK])
                P.barrier()
                _stop("p0")

            with contextlib.ExitStack() as s1:
                T = NB1 * 128
                make_rings(s1, True)
                x_ring = P.ring("x_tm", [128, NB1, D], F32, 2, s1)
                xT = P.sb("xT", [128, 8, T], BF16, s1)
                qT = P.sb("qT", [128, 4, T], BF16, s1)
                kT2 = P.sb("kT2", [128, 2, 2, 128 + T], BF16, s1)
                uT = P.sb("uT", [128, 4, HAL + T], F32, s1)
                pA = P.sb("pA", [128, 4, HAL + T], F32, s1)
                pB = P.sb("pB", [128, 3, HAL + T], F32, s1)
                dd = P.sb("dd", [128, 4, T], BF16, s1)
                kv_ring = P.ring("kv_sb", [128, 256], F32, 2, s1)
                v_aug = P.sb("v_aug", [128, NB1 + 1, 2, 66], BF16, s1)
                PT_ring = P.ring("PT", [128, 4, 2, 128], BF16, 2, s1)
                PT_half = {id(t): [Buf("PTa"), Buf("PTb")] for t in PT_ring.tiles}
                den_ring = P.ring("den", [128, 8], F32, 4, s1)

                P.op("pool", lambda e: e.memset(kT2[:], 0.0), writes=[kT2])
                P.op("pool", lambda e: e.memset(uT[:], 0.0), writes=[uT])
                P.op("pool", lambda e: e.memset(v_aug[:], 0.0), writes=[v_aug])

                supers = [[0]] + [list(range(1 + i * NB1, 1 + (i + 1) * NB1)) for i in range((NBLK - 1) // NB1)]
                if DBG_NSUPER is not None:
                    supers = supers[:DBG_NSUPER]
                xts = {}

                def load_x(si):
                    xt = x_ring.next()
                    for bi, B in enumerate(supers[si]):
                        if B == 0:
                            P.op("pool", lambda e: e.memset(xt[:, 0, :], 0.0), writes=[xt])
                            P.dma("sp", xt[112:128, 0, :], meta, writes=[xt])
                        else:
                            P.dma("sp", xt[:, bi, :], xp[(B - 1) * 128:B * 128, :], writes=[xt])
                    xts[si] = xt

                def stage_B(si, prev_Tn):
                    blocks = supers[si]
                    Tn = len(blocks) * 128
                    if prev_Tn is not None:
                        P.op("pool", lambda e: e.tensor_copy(out=kT2[:, :, :, 0:128], in_=kT2[:, :, :, prev_Tn:prev_Tn + 128]), writes=[kT2])
                        P.op("pool", lambda e: e.tensor_copy(out=uT[:, :, 0:HAL], in_=uT[:, :, prev_Tn:prev_Tn + HAL]), writes=[uT])
                    for c_out in range(4):
                        ap, hb = next_mm()
                        for c in range(8):
                            P.noinc = (c != 7)
                            P.op("pe", lambda e, c=c: e.matmul(
                                out=ap[:, 0:Tn], lhsT=W_in_sb[:, c, c_out * 128:(c_out + 1) * 128], rhs=xT[:, c, 0:Tn],
                                start=(c == 0), stop=(c == 7)), reads=[W_in_sb, xT], writes=[hb])
                        P.op("act", lambda e: e.activation(
                            out=qT[:, c_out, 0:Tn], in_=ap[:, 0:Tn], func=AF.Identity, bias=b_fm[:, c_out:c_out + 1], scale=0.125),
                            reads=[hb, b_fm], writes=[qT])
                    for g in range(2):
                        ap, hb = next_mm()
                        for c in range(8):
                            P.noinc = (c != 7)
                            P.op("pe", lambda e, c=c: e.matmul(
                                out=ap[:, 0:Tn], lhsT=W_kd[:, c, g, :, :].rearrange("p a d -> p (a d)"), rhs=xT[:, c, 0:Tn],
                                start=(c == 0), stop=(c == 7)), reads=[W_kd, xT], writes=[hb])
                        for half in range(2):
                            hs = slice(half * 64, (half + 1) * 64)
                            P.op("act", lambda e, half=half, hs=hs: e.activation(
                                out=kT2[hs, g, half, 128:128 + Tn], in_=ap[hs, 0:Tn], func=AF.Identity, bias=b_kd[hs, g:g + 1], scale=1.0),
                                reads=[hb, b_kd], writes=[kT2])
                    for g in range(4):
                        ap, hb = next_mm()
                        for c in range(8):
                            P.noinc = (c != 7)
                            P.op("pe", lambda e, c=c: e.matmul(
                                out=ap[:, 0:Tn], lhsT=W_in_sb[:, c, 768 + g * 128:768 + (g + 1) * 128], rhs=xT[:, c, 0:Tn],
                                start=(c == 0), stop=(c == 7)), reads=[W_in_sb, xT], writes=[hb])
                        P.op("act", lambda e: e.activation(
                            out=uT[:, g, HAL:HAL + Tn], in_=ap[:, 0:Tn], func=AF.Identity, bias=b_fm[:, 6 + g:7 + g], scale=1.0),
                            reads=[hb, b_fm], writes=[uT])
                    if blocks[0] == 0:
                        P.op("pool", lambda e: e.memset(uT[:, :, HAL:HAL + 112], 0.0), writes=[uT])

                def stage_C_gen(si):
                    blocks = supers[si]
                    Tn = len(blocks) * 128
                    Wd = HAL + Tn
                    P.op("pool", lambda e: e.tensor_tensor(out=pA[:, :, 1:Wd], in0=uT[:, :, 1:Wd], in1=uT[:, :, 0:Wd - 1], op=ALU.add),
                         reads=[uT], writes=[pA])
                    yield
                    P.op("pool", lambda e: e.tensor_tensor(out=pB[:, :, 3:Wd], in0=pA[:, 1:4, 3:Wd], in1=pA[:, 1:4, 1:Wd - 2], op=ALU.add),
                         reads=[pA], writes=[pB])
                    yield

                    def emit_d(g, src, idx):
                        w = POOL_W[g]
                        if blocks[0] == 0:
                            tmpu = RG["pool_sb"].next()
                            P.op("dve", lambda e: e.tensor_tensor(out=tmpu[:, 0:128], in0=src[:, idx, HAL:HAL + 128],
                                                                  in1=cinv_bc[:, g, :], op=ALU.mult),
                                 reads=[src, cinv_bc], writes=[tmpu])
                            P.op("dve", lambda e: e.tensor_tensor(out=dd[:, g, 0:128], in0=tmpu[:, 0:128],
                                                                  in1=uT[:, g, HAL:HAL + 128], op=ALU.subtract),
                                 reads=[tmpu, uT], writes=[dd])
                        else:
                            P.op("dve", lambda e: e.scalar_tensor_tensor(
                                out=dd[:, g, 0:Tn], in0=src[:, idx, HAL:HAL + Tn], scalar=1.0 / w, in1=uT[:, g, HAL:HAL + Tn],
                                op0=ALU.mult, op1=ALU.subtract), reads=[src, uT], writes=[dd])
                    emit_d(0, pA, 0)
                    yield
                    emit_d(1, pB, 0)
                    yield
                    P.op("pool", lambda e: e.tensor_tensor(out=pA[:, 0:2, 7:Wd], in0=pB[:, 1:3, 7:Wd], in1=pB[:, 1:3, 3:Wd - 4], op=ALU.add),
                         reads=[pB], writes=[pA])
                    yield
                    emit_d(2, pA, 0)
                    yield
                    P.op("pool", lambda e: e.tensor_tensor(out=pB[:, 0:1, 15:Wd], in0=pA[:, 1:2, 15:Wd], in1=pA[:, 1:2, 7:Wd - 8], op=ALU.add),
                         reads=[pA], writes=[pB])
                    yield
                    emit_d(3, pB, 0)
                    yield
                    if blocks[-1] == NBLK - 1:
                        ap, hb = half_ap(0), Hb[0]
                        for g in range(4):
                            P.op("pe", lambda e, g=g: e.transpose(
                                out=ap[0:15, g * 128:(g + 1) * 128], in_=uT[:, g, HAL + Tn - 15:HAL + Tn], identity=ident_f[:]),
                                reads=[uT, ident_f], writes=[hb])
                        tmpu = RG["pool_sb"].next()
                        P.op("dve", lambda e: e.tensor_copy(out=tmpu[0:15, :], in_=ap[0:15, :]), reads=[hb], writes=[tmpu])
                        P.dma("pool", pool_prompt, tmpu[0:15, :], reads=[tmpu], writes=[outb], sembuf=tmpu)

                def stage_D(si, prev_nb):
                    blocks = supers[si]
                    if prev_nb is not None:
                        P.op("pool", lambda e: e.tensor_copy(out=v_aug[:, 0], in_=v_aug[:, prev_nb]), writes=[v_aug])
                    for bi, B in enumerate(blocks):
                        ap, hb = next_mm()
                        for c in range(8):
                            P.noinc = (c != 7)
                            P.op("pe", lambda e, c=c: e.matmul(
                                out=ap[:, 0:256], lhsT=xT[:, c, bi * 128:(bi + 1) * 128], rhs=W_in_sb[:, c, 512:768],
                                start=(c == 0), stop=(c == 7)), reads=[xT, W_in_sb], writes=[hb])
                        kv_sb = kv_ring.next()
                        P.op("dve", lambda e: e.tensor_tensor(out=kv_sb[:], in0=ap[:, 0:256], in1=b_in_bc[:, 512:768], op=ALU.add),
                             reads=[hb, b_in_bc], writes=[kv_sb])
                        P.op("pool", lambda e: e.tensor_copy(
                            out=v_aug[:, bi + 1, :, 0:64], in_=kv_sb[:, 128:256].rearrange("p (g d) -> p g d", g=2)),
                            reads=[kv_sb], writes=[v_aug])
                        P.op("pool", lambda e: e.memset(v_aug[:, bi + 1, :, 64:65], 1.0), writes=[v_aug])
                        if B == 0:
                            P.op("pool", lambda e: e.memset(v_aug[0:112, bi + 1, :, :], 0.0), writes=[v_aug])
                        if B == NBLK - 1:
                            P.dma("pool", k_prompt, kv_sb[:, 0:128], reads=[kv_sb], writes=[outb], sembuf=kv_sb)
                            P.dma("pool", v_prompt, kv_sb[:, 128:256], reads=[kv_sb], writes=[outb], sembuf=kv_sb)

                def step(gens):
                    for g_ in list(gens):
                        try:
                            next(g_)
                        except StopIteration:
                            gens.remove(g_)

                def stage_E(si, bg):
                    blocks = supers[si]
                    nb = len(blocks)
                    units = [(bi, g) for bi in range(nb) for g in range(2)]
                    attn_tiles = [RG["attn"].next() for _ in blocks]
                    outs = [dict() for _ in blocks]
                    scv = {}
                    fgens = []
                    pending_F = []

                    def QK(u):
                        bi, g = units[u]
                        sc = sc_ring.next()
                        scv[u] = sc
                        for hf in range(2):
                            h0 = 4 * g + 2 * hf
                            for Bt, first in ((B_hi, True), (B_lo, False)):
                                P.noinc = True
                                P.op("pe", lambda e, hf=hf, h0=h0, Bt=Bt, first=first: e.matmul(
                                    out=sc[:, 2 * hf:2 * hf + 2, :, :].rearrange("p j k q -> p (j k q)"), lhsT=ident_b[:],
                                    rhs=Bt[:, h0:h0 + 2, :].rearrange("p h k -> p (h k)"),
                                    start=first, stop=False), reads=[Bt, ident_b], writes=[sc.bufs[hf]])
                            for j in (2 * hf, 2 * hf + 1):
                                h = 4 * g + j
                                cq, half = h // 2, h % 2
                                for kb in range(2):
                                    k0 = (bi + kb) * 128
                                    last = (j == 2 * hf + 1 and kb == 1)
                                    P.noinc = (not last)
                                    P.op("pe", lambda e, j=j, kb=kb, k0=k0, cq=cq, half=half, last=last: e.matmul(
                                        out=sc[:, j, kb, :], lhsT=kT2[:, g, half, k0:k0 + 128],
                                        rhs=qT[:, cq, bi * 128:(bi + 1) * 128], start=False, stop=last),
                                        reads=[kT2, qT], writes=[sc.bufs[hf]])

                    def SM_PV_gen(u):
                        bi, g = units[u]
                        sc = scv.pop(u)
                        PT = PT_ring.next()
                        o = o_ring.next()
                        den = den_ring.next()
                        attn = attn_tiles[bi]
                        for hf in range(2):
                            P.op("act", lambda e, hf=hf: e.activation(
                                out=PT[:, 2 * hf:2 * hf + 2, :, :].rearrange("p j k q -> p (j k q)"),
                                in_=sc[:, 2 * hf:2 * hf + 2, :, :].rearrange("p j k q -> p (j k q)"), func=AF.Exp),
                                reads=[sc.bufs[hf]], writes=[PT_half[id(PT)][hf]])
                            yield
                        for j in range(4):
                            for kb in range(2):
                                P.noinc = (not (j == 3 and kb == 1))
                                P.op("pe", lambda e, j=j, kb=kb: e.matmul(
                                    out=o[:, j, 0:65], lhsT=PT[:, j, kb, :], rhs=v_aug[:, bi + kb, g, 0:65],
                                    start=(kb == 0), stop=(kb == 1)), reads=[PT_half[id(PT)][j // 2], v_aug], writes=o.bufs)
                        yield
                        P.op("dve", lambda e: e.tensor_scalar(out=den[:, 0:4], in0=o[:, :, 64], scalar1=1.0, scalar2=None, op0=ALU.add),
                             reads=o.bufs, writes=[den])
                        yield
                        P.op("dve", lambda e: e.reciprocal(out=den[:, 4:8], in_=den[:, 0:4]), writes=[den])
                        yield
                        P.op("dve", lambda e: e.tensor_tensor(
                            out=attn[:, g * 256:(g + 1) * 256].rearrange("p (j d) -> p j d", j=4), in0=o[:, :, 0:64],
                            in1=den[:, 4:8].unsqueeze(2).to_broadcast([128, 4, 64]), op=ALU.mult),
                            reads=o.bufs + [den], writes=[attn])

                    QK(0)
                    if len(units) > 1:
                        QK(1)
                    for p in range(0, len(units), 2):
                        pair = [SM_PV_gen(u) for u in (p, p + 1) if u < len(units)]
                        while pair:
                            step(pair)
                            step(fgens)
                        for u in (p + 2, p + 3):
                            if u < len(units):
                                QK(u)
                        for _ in range(4):
                            step(bg)
                        pending_F.append(p // 2)
                        if not bg:
                            for bi in pending_F:
                                fgens.append(stage_F_gen(si, bi, attn_tiles[bi], outs[bi]))
                            pending_F = []
                    while bg:
                        step(bg)
                    for bi in pending_F:
                        fgens.append(stage_F_gen(si, bi, attn_tiles[bi], outs[bi]))
                    lockstep(fgens)
                    return outs

                fslot = [0]

                def stage_F_gen(si, bi, attn, out):
                    def pool_mm():
                        fslot[0] = (fslot[0] + 1) % 2
                        pap, pb = half_ap(fslot[0]), Hb[fslot[0]]
                        for g in range(4):
                            P.noinc = (g != 3)
                            P.op("pe", lambda e, g=g: e.matmul(
                                out=pap[:, g * 128:(g + 1) * 128], lhsT=dd[:, g, bi * 128:(bi + 1) * 128],
                                rhs=W_pool_sb[:, g, :], start=True, stop=True), reads=[dd, W_pool_sb], writes=[pb])
                        return pap, pb
                    return tail_F_gen(attn, pool_mm, out)

                def stage_G_gen(si, bi, mix_in):
                    B = supers[si][bi]
                    return tail_G_gen(mix_in, xts[si][:, bi, :], xts[si], B * 128, x1d_bufs[B])

                def stage_A_gen(si, bi):
                    return norm_T_gen(xts[si][:, bi, :], xts[si], g_pre_mix_bc, xT, bi * 128, trA_ring)

                load_x(0)
                if len(supers) > 1:
                    load_x(1)
                for bi in range(len(supers[0])):
                    run(stage_A_gen(0, bi))
                stage_B(0, None)
                stage_D(0, None)
                for si, blocks in enumerate(supers):
                    nb = len(blocks)
                    has_next = si + 1 < len(supers)
                    nnb = len(supers[si + 1]) if has_next else 0
                    outs = stage_E(si, [stage_C_gen(si)])
                    _stop("p1e")
                    for p0 in range(0, max(nb, nnb), 2):
                        gens = []
                        for bi in range(p0, min(p0 + 2, max(nb, nnb))):
                            if bi < nnb:
                                gens.append(stage_A_gen(si + 1, bi))
                            if bi < nb:
                                gens.append(stage_G_gen(si, bi, outs[bi]["mix"]))
                        lockstep(gens)
                    if has_next:
                        stage_B(si + 1, nb * 128)
                        stage_D(si + 1, nb)
                        if si + 2 < len(supers):
                            load_x(si + 2)
                P.barrier()
                _stop("p1")
        with contextlib.ExitStack() as s2:
            T2 = NB2 * 128
            W_up_sb = P.sb("W_up", [128, 8, NUP], BF16, s2)
            W_dn_sb = P.sb("W_dn", [128, NJ, D], BF16, s2)
            g_pre_ffn_bc = P.sb("g_pre_ffn_bc", [128, D], F32, s2)
            g_post_ffn_bc = P.sb("g_post_ffn_bc", [128, D], F32, s2)
            cwb = P.sb("cwb", [128, 44, 4], F32, s2)
            halo = P.sb("halo", [128, 44, 2], F32, s2)
            WUP_GROUPS = [(0, 6), (6, 12), (12, 17), (17, 22)]
            wup_bufs = [Buf("wup%d" % q) for q in range(len(WUP_GROUPS))]
            wup_of_j = {}
            for q, (j0, j1) in enumerate(WUP_GROUPS):
                for j in range(j0, j1):
                    wup_of_j[j] = wup_bufs[q]
                for hv in range(2):
                    c0, c1 = hv * DFF + j0 * 128, hv * DFF + j1 * 128
                    P.dma("pool", W_up_sb[:, :, c0:c1], w_up[:, c0:c1].rearrange("(c p) n -> p c n", p=128),
                          writes=[wup_bufs[q]])
            for j0 in range(0, NJ, 11):
                P.dma("pool", W_dn_sb[:, j0:j0 + 11, :], w_down[j0 * 128:(j0 + 11) * 128, :].rearrange("(j p) n -> p j n", p=128),
                      writes=[W_dn_sb])
            P.dma("sp", g_pre_ffn_bc[:], g_pre_ffn.partition_broadcast(128), writes=[g_pre_ffn_bc])
            P.dma("sp", g_post_ffn_bc[:], g_post_ffn.partition_broadcast(128), writes=[g_post_ffn_bc])
            P.op("pool", lambda e: e.memset(halo[:], 0.0), writes=[halo])

            xb2_ring = P.ring("xb2", [128, D], BF16, 1, s2)
            ssq2_ring = P.ring("ssq_f", [128, 4], F32, 6, s2)
            x1_in = P.ring("x1_in", [128, D], F32, 2, s2)
            x1nT = P.sb("x1nT", [128, 8, T2], BF16, s2)
            a_sb = P.sb("a_sb", [128, NJ, T2], BF16, s2)
            a_main = a_sb
            a_bufs = [Buf("a%d" % j) for j in range(NJ)]
            a_alias = a_bufs[0:10]
            x1nT_s = P.sb("x1nT_s", [128, 8, NS], BF16, s2)
            a_s = Tile(a_sb.t[:, 9, 0:NJ * NS].rearrange("p (j n) -> p j n", n=NS), Buf("a_s"))
            ytmp_ring = P.ring("ytmp", [128, D], F32, 2, s2)
            ytmp = ytmp_ring.tiles[0]

            tr2 = P.ps("tr2", [128, 8, 128], BF16, s2)
            gv_ps = [P.ps("gv%d" % i, [128, 2, 512], F32, s2) for i in range(2)]
            ffn_ps = P.ps("ffn", [128, 1024], F32, s2)
            ffA = Buf("ffA")
            ffB = Buf("ffB")
            trf = P.ps("trf", [128, 512], F32, s2)

            def norm_pre2(x_ap, x_tile):
                ssq = ssq2_ring.next()
                xb = xb2_ring.next()
                P.op("act", lambda e: e.activation(out=xb[:], in_=x_ap, func=AF.Square, accum_out=ssq[:, 0:1]),
                     reads=[x_tile], writes=[xb, ssq])
                stt, rstd = rstd_from_ssq(ssq[:, 0:1], ssq, D)
                P.op("dve", lambda e: e.scalar_tensor_tensor(out=xb[:], in0=x_ap, scalar=rstd, in1=g_pre_ffn_bc[:],
                                                             op0=ALU.mult, op1=ALU.mult),
                     reads=[x_tile, stt, g_pre_ffn_bc], writes=[xb])
                return xb

            def norm_tr2(xb, dstT, col0):
                for c in range(8):
                    P.noinc = (c != 7)
                    P.op("pe", lambda e, c=c: e.transpose(out=tr2[:, c, :], in_=xb[:, c * 128:(c + 1) * 128], identity=ident_b[:]),
                         reads=[xb, ident_b], writes=[tr2])
                P.op("act", lambda e: e.copy(out=dstT[:, :, col0:col0 + 128], in_=tr2[:]), reads=[tr2], writes=[dstT])

            class ChAcc:
                def __init__(self, at, b):
                    self.at = at
                    self.b = b

            def to_token_major(src, n, dst_dram_fn):
                for q in range(11):
                    for i in range(4):
                        jj = q * 4 + i
                        P.op("pe", lambda e, jj=jj, i=i: e.transpose(out=trf[0:n, i * 128:(i + 1) * 128], in_=src.at(jj),
                                                                     identity=ident_f[:]),
                             reads=[src.b, ident_f, a_alias], writes=[trf])
                    P.op("dve", lambda e: e.tensor_copy(out=ytmp[0:n, 0:512], in_=trf[0:n, :]), reads=[trf], writes=[ytmp])
                    P.dma("pool", dst_dram_fn(q), ytmp[0:n, 0:512], reads=[ytmp], writes=[outb], sembuf=ytmp)

            class UpBuf:
                def __init__(self, name, T, stack):
                    self.up = [P.sb("%s_up%d" % (name, hv), [128, 2 + T], F32, stack) for hv in range(2)]
                    self.hb = [Buf("%s_halo%d" % (name, hv)) for hv in range(2)]
                    self.c = [P.sb("%s_c%d" % (name, hv), [128, T], F32, stack) for hv in range(2)]

            def ffn_norm_pre(B, bi):
                x1 = x1_in.next()
                P.dma("sp", x1[:], x1d[B * 128:(B + 1) * 128, :], reads=[x1d_bufs[B]], writes=[x1])
                return norm_pre2(x1[:], x1)

            def ffn_norm_tr(B, bi, xb, dst=None, ncols=128):
                if dst is not None:
                    for c in range(8):
                        P.noinc = (c != 7)
                        P.op("pe", lambda e, c=c: e.transpose(out=tr2[:, c, :], in_=xb[:, c * 128:(c + 1) * 128], identity=ident_b[:]),
                             reads=[xb, ident_b], writes=[tr2])
                    P.op("dve", lambda e: e.tensor_copy(out=dst[:, :, 0:ncols], in_=tr2[:, :, 0:ncols]), reads=[tr2], writes=[dst])
                    return
                norm_tr2(xb, x1nT, bi * 128)
                if B == 0:
                    P.op("pool", lambda e: e.memset(x1nT[:, :, 0:112], 0.0), writes=[x1nT])

            def ffn_norm(B, bi):
                ffn_norm_tr(B, bi, ffn_norm_pre(B, bi))

            def run2(gen):
                for _ in gen:
                    pass

            def lockstep2(gens):
                gens = list(gens)
                while gens:
                    for g_ in list(gens):
                        try:
                            next(g_)
                        except StopIteration:
                            gens.remove(g_)

            def ffn_up(blocks, sample, ub_ring, scT=None, upS=None):
                run2(ffn_up_gen(blocks, sample, ub_ring, x1nT, a_sb, scT, upS))

            def ffn_halo_gen():
                for j in range(NJ):
                    gv = gv_ps[j % 2]
                    for hv in range(2):
                        col = hv * DFF + j * 128
                        for c in range(8):
                            P.noinc = (c != 7)
                            P.op("pe", lambda e, c=c, hv=hv, col=col: e.matmul(
                                out=gv[:, hv, 0:16], lhsT=W_up_sb[:, c, col:col + 128], rhs=x1nT[:, c, 112:128],
                                start=(c == 0), stop=(c == 7)), reads=[wup_of_j[j], x1nT], writes=[gv])
                    for hv in range(2):
                        ch = hv * NJ + j
                        P.op("act", lambda e, hv=hv, ch=ch: e.copy(out=halo[:, ch, :], in_=gv[:, hv, 14:16]),
                             reads=[gv], writes=[halo])
                    yield

            def ffn_up_gen(blocks, sample, ub_ring, x1nT, a_sb, scT=None, upS=None):
                N = NS if sample else len(blocks) * 128
                pend = None
                for j in range(NJ + 1):
                    if j < NJ:
                        gv = gv_ps[j % 2]
                        ub = ub_ring[j % len(ub_ring)]
                        for hv in range(2):
                            col = hv * DFF + j * 128
                            for c in range(8):
                                P.noinc = (c != 7)
                                P.op("pe", lambda e, c=c, hv=hv, col=col, gv=gv: e.matmul(
                                    out=gv[:, hv, 0:N], lhsT=W_up_sb[:, c, col:col + 128], rhs=x1nT[:, c, 0:N],
                                    start=(c == 0), stop=(c == 7)), reads=[wup_of_j[j], x1nT], writes=[gv])
                        for hv in range(2):
                            ch = hv * NJ + j
                            if not sample:
                                P.op("pool", lambda e, hv=hv, ch=ch: e.tensor_copy(out=ub.up[hv][:, 0:2], in_=halo[:, ch, :]),
                                     reads=[halo], writes=[ub.hb[hv]])
                            P.op("act", lambda e, hv=hv: e.copy(out=ub.up[hv][:, 2:2 + N], in_=gv[:, hv, 0:N]),
                                 reads=[gv], writes=[ub.up[hv]])
                        for hv in range(2):
                            ch = hv * NJ + j
                            P.op("act", lambda e, hv=hv, ch=ch: e.activation(
                                out=ub.c[hv][:, 0:N], in_=gv[:, hv, 0:N], func=AF.Identity, bias=cwb[:, ch, 3:4], scale=cwb[:, ch, 2:3]),
                                reads=[gv, cwb], writes=[ub.c[hv]])
                        for hv in range(2):
                            ch = hv * NJ + j
                            if not sample:
                                P.op("pool", lambda e, hv=hv, ch=ch: e.tensor_copy(out=halo[:, ch, :], in_=ub.up[hv][:, N:N + 2]),
                                     reads=[ub.up[hv]], writes=[halo])
                            else:
                                P.op("pool", lambda e, hv=hv, ch=ch: e.tensor_copy(out=upS.at(ch), in_=ub.up[hv][:, 2:2 + N]),
                                     reads=[ub.up[hv], a_alias], writes=[upS.b])
                        for i in (1, 0):
                            for hv in range(2):
                                ch = hv * NJ + j
                                if sample:
                                    in0 = scT.at(ch).rearrange("p (b i) -> p i b", i=2)[:, i, :]
                                    rd = [scT.b, cwb, a_alias]
                                else:
                                    in0 = ub.up[hv][:, i:i + N]
                                    rd = [ub.up[hv], ub.hb[hv], cwb]
                                P.op("dve", lambda e, hv=hv, ch=ch, i=i, in0=in0: e.scalar_tensor_tensor(
                                    out=ub.c[hv][:, 0:N], in0=in0, scalar=cwb[:, ch, i:i + 1], in1=ub.c[hv][:, 0:N],
                                    op0=ALU.mult, op1=ALU.add), reads=rd, writes=[ub.c[hv]])
                    if pend is not None:
                        pj, pub = pend
                        P.op("act", lambda e: e.activation(out=pub.c[0][:, 0:N], in_=pub.c[0][:, 0:N], func=AF.Gelu_apprx_tanh),
                             writes=[pub.c[0]])
                        P.op("pool", lambda e: e.tensor_tensor(out=a_sb[:, pj, 0:N], in0=pub.c[0][:, 0:N], in1=pub.c[1][:, 0:N], op=ALU.mult),
                             reads=[pub.c[0], pub.c[1]], writes=[a_sb if sample else a_bufs[pj]])
                    pend = (j, ub) if j < NJ else None
                    yield

            ffn_bufs = [(ffn_ps.t, [ffA, ffB]), (gv_ps[1].t[:].rearrange("p a n -> p (a n)"), [gv_ps[1].b, gv_ps[1].b])]

            def ffn_down_mm(B, bi, sample, fb, a_sb=None):
                a_sb = a_sb if a_sb is not None else a_main
                M = NS if sample else 128
                fap, fbufs = fb
                for half in range(2):
                    for j in range(NJ):
                        P.noinc = (j != NJ - 1)
                        P.op("pe", lambda e, j=j, half=half: e.matmul(
                            out=fap[0:M, half * 512:(half + 1) * 512], lhsT=a_sb[:, j, bi * 128:bi * 128 + M],
                            rhs=W_dn_sb[:, j, half * 512:(half + 1) * 512], start=(j == 0), stop=(j == NJ - 1)),
                            reads=([a_sb, a_alias] if sample else [a_bufs[j]]) + [W_dn_sb], writes=[fbufs[half]])

            def ffn_down_tail(B, bi, sample, fb):
                M = NS if sample else 128
                fap, fbufs = fb
                x1 = x1_in.next()
                P.dma("sp", x1[:], x1d[B * 128:(B + 1) * 128, :], reads=[x1d_bufs[B]], writes=[x1])
                ssq = ssq2_ring.next()
                yt = ytmp_ring.next()
                for half in range(2):
                    P.op("act", lambda e, half=half: e.activation(
                        out=yt[0:M, half * 512:(half + 1) * 512], in_=fap[0:M, half * 512:(half + 1) * 512],
                        func=AF.Square, accum_out=ssq[0:M, half:half + 1]),
                        reads=[fbufs[half]], writes=[yt, ssq])
                P.op("dve", lambda e: e.tensor_tensor(out=ssq[0:M, 2:3], in0=ssq[0:M, 0:1], in1=ssq[0:M, 1:2], op=ALU.add), writes=[ssq])
                stt, rstd = rstd_from_ssq(ssq[0:M, 2:3], ssq, D, M)
                for half in range(2):
                    P.op("dve", lambda e, half=half: e.scalar_tensor_tensor(
                        out=yt[0:M, half * 512:(half + 1) * 512], in0=fap[0:M, half * 512:(half + 1) * 512], scalar=stt[0:M, 2:3],
                        in1=g_post_ffn_bc[0:M, half * 512:(half + 1) * 512], op0=ALU.mult, op1=ALU.mult),
                        reads=[fbufs[half], stt, g_post_ffn_bc], writes=[yt])
                P.op("dve", lambda e: e.tensor_tensor(out=yt[0:M, :], in0=yt[0:M, :], in1=x1[0:M, :], op=ALU.add),
                     reads=[x1], writes=[yt])
                if sample:
                    P.dma("sp", y_sample, yt[0:NS, :], reads=[yt], writes=[outb], sembuf=yt)
                else:
                    P.dma("sp", y_prompt[(B - 1) * 128:B * 128, :], yt[:], reads=[yt], writes=[outb], sembuf=yt)

            def ffn_down(B, bi, sample):
                ffn_down_mm(B, bi, sample, ffn_bufs[0])
                ffn_down_tail(B, bi, sample, ffn_bufs[0])

            P.dma("sp", cwb[:].rearrange("p c r -> p (c r)"), cw_d, reads=[csd_buf], writes=[cwb])
            ub_ring = [UpBuf("ub%d" % i, T2, s2) for i in range(2)]
            supers2 = [[0]] + [list(range(1 + i * NB2, 1 + (i + 1) * NB2)) for i in range((NBLK - 1) // NB2)]
            if DBG_NSUPER is not None:
                supers2 = supers2[:DBG_NSUPER]
            for bi, B in enumerate(supers2[0]):
                ffn_norm(B, bi)
            scT_v = a_sb.t[:, 0:6, :].bitcast(F32).rearrange("p a b -> p (a b)")[:, 0:1408].rearrange("p (c r) -> p c r", r=32)
            upS_v = a_sb.t[:, 6:9, :].bitcast(F32).rearrange("p a b -> p (a b)")[:, 0:704].rearrange("p (c r) -> p c r", r=NS)
            scT = ChAcc(lambda ch: scT_v[:, ch, :], Buf("scT"))
            upS = ChAcc(lambda ch: upS_v[:, ch, :], Buf("upS"))
            P.dma("sp", a_sb.t[:, 0:6, :].bitcast(F32).rearrange("p a b -> p (a b)")[:, 0:1408], sc_d,
                  reads=[csd_buf, a_alias], writes=[scT.b])
            P.dma("sp", conv_sample[:, 0, :], sconv.rearrange("(b i) c -> b i c", i=2)[:, 1, :], writes=[outb])
            ub_s = [UpBuf("ubs%d" % i, NS, s2) for i in range(2)]
            ffn_norm_tr(NBLK, 0, ffn_norm_pre(NBLK, 0), dst=x1nT_s, ncols=NS)
            for si, blocks in enumerate(supers2):
                if si == 0:
                    lockstep2([ffn_halo_gen(),
                               ffn_up_gen([NBLK], True, ub_s, x1nT_s, a_s, scT, upS)])
                    ffn_down_mm(NBLK, 0, True, ffn_bufs[0], a_s)
                    ffn_down_tail(NBLK, 0, True, ffn_bufs[0])
                    to_token_major(upS, NS, lambda q: conv_sample[:, 1, q * 512:(q + 1) * 512])
                else:
                    ffn_up(blocks, False, ub_ring)
                nxt = supers2[si + 1] if si + 1 < len(supers2) else []
                if blocks[0] == 0:
                    for bi, B in enumerate(nxt):
                        ffn_norm(B, bi)
                    continue
                nbk = len(blocks)
                fbs = [ffn_bufs[(nbk - 1 - bi) % 2] for bi in range(nbk)]
                ffn_down_mm(blocks[0], 0, False, fbs[0])
                for bi, B in enumerate(blocks):
                    if bi + 1 < nbk:
                        ffn_down_mm(blocks[bi + 1], bi + 1, False, fbs[bi + 1])
                    xb_n = ffn_norm_pre(nxt[bi], bi) if bi < len(nxt) else None
                    ffn_down_tail(B, bi, False, fbs[bi])
                    if xb_n is not None:
                        ffn_norm_tr(nxt[bi], bi, xb_n)
                for bi in range(nbk, len(nxt)):
                    ffn_norm(nxt[bi], bi)
            to_token_major(ChAcc(lambda jj: halo[:, jj, :], halo.b), 2, lambda q: conv_prompt[:, q * 512:(q + 1) * 512])


def _consts():
    ident = np.eye(128, dtype=np.float32)
    slopes = np.exp2(-(np.arange(1, 9, dtype=np.float32) * 1.0)).astype(np.float32)
    key = np.arange(128)[:, None, None]
    kb = np.arange(2)[None, :, None]
    q = np.arange(128)[None, None, :]
    dist = 128 + q - (kb * 128 + key)
    valid = (dist >= 0) & (dist <= 128)
    bb = np.empty((128, 8, 2, 128), np.float32)
    for h in range(8):
        bb[:, h] = np.where(valid, -slopes[h] * dist.astype(np.float32), NEG)
    bbase = bb.reshape(128, 8 * 256)
    cinv = np.ones((4, 128), np.float32)
    for g, w in enumerate(POOL_W):
        pos = np.arange(128) - 112
        cnt = np.where(pos >= 0, np.minimum(pos + 1, w), w).astype(np.float32)
        cinv[g] = 1.0 / cnt
    cinv = cinv.reshape(1, 512)
    sel = np.zeros((120, 2, 4, 16), np.float32)
    for t in range(2):
        for bl in range(8):
            for r in range(15):
                for g, w in enumerate(POOL_W):
                    if r >= 16 - w:
                        sel[bl * 15 + r, t, g, t * 8 + bl] = 1.0
    sel = sel.reshape(120, 128)
    sbias = np.zeros((128, 129), np.float32)
    for h in range(8):
        sbias[h * 16:(h + 1) * 16, 0:128] = -slopes[h] * (128 - np.arange(128, dtype=np.float32))[None, :]
    return ident, bbase, cinv, sel, sbias


_NC_CACHE = {}


def kernel(x_prompt, x_sample, cache_k, cache_v, state_pool, state_conv, meta,
           w_in, b_in, sinks, w_pool, pool_scale, g_attn_out, g_pool_out, w_o,
           g_pre_mix, g_post_mix, g_pre_ffn, g_post_ffn, w_up, conv_w, conv_b, w_down):
    f = lambda a: np.ascontiguousarray(np.asarray(a, dtype=np.float32))
    ident, bbase, cinv, sel, sbias = _consts()
    if "nc" not in _NC_CACHE:
        _NC_CACHE["nc"] = build_program()
    nc = _NC_CACHE["nc"]
    shared = {
        "meta": f(meta), "w_in": f(w_in[0]), "b_in": f(b_in[0]).reshape(1, NIN), "sinks": f(sinks[0]).reshape(1, 8),
        "w_pool": f(w_pool[0]).reshape(512, 128), "pool_scale": f(pool_scale[0]).reshape(1, 512),
        "g_attn": f(g_attn_out[0]).reshape(1, 512), "g_pool": f(g_pool_out[0]).reshape(1, 512), "w_o": f(w_o[0]),
        "g_pre_mix": f(g_pre_mix[0]).reshape(1, D), "g_post_mix": f(g_post_mix[0]).reshape(1, D),
        "g_pre_ffn": f(g_pre_ffn[0]).reshape(1, D), "g_post_ffn": f(g_post_ffn[0]).reshape(1, D),
        "w_up": f(w_up[0]), "conv_w": f(conv_w[0]), "conv_b": f(conv_b[0]).reshape(1, NUP), "w_down": f(w_down[0]),
        "c_ident": ident, "c_bbase": bbase, "c_cinv": cinv, "c_sel": sel, "c_sbias": sbias,
    }
    xpn = np.asarray(x_prompt, dtype=np.float32)
    xsn = np.asarray(x_sample, dtype=np.float32)
    ckn = np.asarray(cache_k, dtype=np.float32)
    cvn = np.asarray(cache_v, dtype=np.float32)
    spn = np.asarray(state_pool, dtype=np.float32)
    scn = np.asarray(state_conv, dtype=np.float32)
    in_maps = []
    for i in range(8):
        sl = slice(i * NS, (i + 1) * NS)
        m = dict(shared)
        m["xp"] = f(xpn[i])
        m["xs"] = f(xsn[sl, 0, :])
        m["ck"] = f(ckn[0, sl].reshape(NS, 128, 128))
        m["cv"] = f(cvn[0, sl].reshape(NS, 128, 128))
        m["spool"] = f(spn[0, sl])
        m["sconv"] = f(scn[0, sl].reshape(NS * 2, NUP))
        in_maps.append(m)
    res = run_bass_kernel_spmd(nc, in_maps, core_ids=list(range(8)))
    R = res.results
    y_prompt = np.stack([R[i]["y_prompt"] for i in range(8)], 0)
    y_sample = np.concatenate([R[i]["y_sample"] for i in range(8)], 0).reshape(128, 1, D)
    k_prompt = np.stack([R[i]["k_prompt"].reshape(128, 2, 64) for i in range(8)], 0)[None]
    v_prompt = np.stack([R[i]["v_prompt"].reshape(128, 2, 64) for i in range(8)], 0)[None]
    pool_prompt = np.stack([R[i]["pool_prompt"] for i in range(8)], 0)[None]
    conv_prompt = np.stack([R[i]["conv_prompt"] for i in range(8)], 0)[None]
    k_sample = np.concatenate([R[i]["k_sample"].reshape(NS, 128, 2, 64) for i in range(8)], 0)[None]
    v_sample = np.concatenate([R[i]["v_sample"].reshape(NS, 128, 2, 64) for i in range(8)], 0)[None]
    pool_sample = np.concatenate([R[i]["pool_sample"] for i in range(8)], 0)[None]
    conv_sample = np.concatenate([R[i]["conv_sample"] for i in range(8)], 0)[None]
    outs = (y_prompt, y_sample, k_prompt, v_prompt, pool_prompt, conv_prompt, k_sample, v_sample, pool_sample, conv_sample)
    return tuple(np.ascontiguousarray(o, dtype=np.float32) for o in outs)
```

```python
import contextlib
import numpy as np
import concourse.bass as bass
import concourse.mybir as mybir
from concourse.bass_utils import run_bass_kernel_spmd

F32 = mybir.dt.float32
BF16 = mybir.dt.bfloat16
AF = mybir.ActivationFunctionType
ALU = mybir.AluOpType
AX = mybir.AxisListType

D = 1024
NIN = 1280
DFF = 2816
NUP = 5632
NJ = 22
SEQ = 4096
NBLK = 33
NS = 16
HAL = 16
EPS = 1e-6
NEG = -30000.0
POOL_W = (2, 4, 8, 16)

SAME_ENGINE_SYNC = True
NB1 = 4
NB2 = 4
DBG_NSUPER = None
DBG_STOP = None


class _Stop(Exception):
    pass


_STOPPED = [False]


def _stop(name):
    if DBG_STOP == name:
        _STOPPED[0] = True


class Buf:
    def __init__(self, name):
        self.name = name
        self.w = None
        self.r = {}
        self.dsem = None
        self.dcount = 0


class Tile:
    def __init__(self, t, b):
        self.t = t
        self.b = b

    def __getitem__(self, idx):
        return self.t[idx]


class Ring:
    def __init__(self, tiles):
        self.tiles = tiles
        self.i = -1

    def next(self):
        self.i = (self.i + 1) % len(self.tiles)
        return self.tiles[self.i]


class Eng:
    def __init__(self, name, obj, sem):
        self.name = name
        self.obj = obj
        self.sem = sem
        self.count = 0
        self.waited = {}


class Prog:
    def __init__(self, nc, stack):
        self.nc = nc
        self.stack = stack
        self.sems = {}
        self.engs = {}
        for n, o in [("pe", nc.tensor), ("act", nc.scalar), ("dve", nc.vector),
                     ("pool", nc.gpsimd), ("sp", nc.sync)]:
            s = stack.enter_context(nc.semaphore("sem_" + n))
            self.sems["e:" + n] = s
            self.engs[n] = Eng(n, o, s)
        self.nuid = 0
        self.dcounts = {}
        self.noinc = False

    def uid(self):
        self.nuid += 1
        return self.nuid

    def sb(self, name, shape, dtype, stack=None):
        st = stack if stack is not None else self.stack
        t = st.enter_context(self.nc.sbuf_tensor("%s_%d" % (name, self.uid()), list(shape), dtype))
        return Tile(t, Buf(name))

    def ps(self, name, shape, dtype, stack=None):
        st = stack if stack is not None else self.stack
        t = st.enter_context(self.nc.psum_tensor("%s_%d" % (name, self.uid()), list(shape), dtype))
        return Tile(t, Buf(name))

    def ring(self, name, shape, dtype, n, stack=None):
        return Ring([self.sb("%s%d" % (name, i), shape, dtype, stack) for i in range(n)])

    def _dsem(self, b, queue):
        if b.dsem is None:
            b.dsem = {}
        if queue not in b.dsem:
            key = "d:%s:%s:%d" % (b.name, queue, self.uid())
            s = self.stack.enter_context(self.nc.semaphore("ds_%d" % len(self.sems)))
            self.sems[key] = s
            b.dsem[queue] = key
            self.dcounts[key] = 0
        return b.dsem[queue]

    def _wait(self, e, deps):
        for key, val in deps.items():
            if key == "e:" + e.name and not (SAME_ENGINE_SYNC and e.name in ("act", "dve", "pool")):
                continue
            if key in self.dcounts:
                val = 16 * self.dcounts[key]
            if e.waited.get(key, 0) >= val:
                continue
            e.obj.wait_ge(self.sems[key], val)
            e.waited[key] = val

    @staticmethod
    def _collect(reads, writes):
        deps = {}

        def add(d):
            if d is None:
                return
            k, v = d
            if deps.get(k, 0) < v:
                deps[k] = v
        for b in reads:
            add(b.w)
        for b in writes:
            add(b.w)
            for k, v in b.r.items():
                add((k, v))
        return deps

    @staticmethod
    def _commit(dep, reads, writes):
        k, v = dep
        for b in reads:
            if b in writes:
                continue
            if b.r.get(k, 0) < v:
                b.r[k] = v
        for b in writes:
            b.w = dep
            b.r = {}

    @staticmethod
    def _bufs(xs):
        out = []
        for x in xs:
            if isinstance(x, (list, tuple)):
                out.extend(Prog._bufs(x))
            else:
                out.append(x.b if isinstance(x, Tile) else x)
        return out

    def op(self, eng, fn, reads=(), writes=()):
        noinc = self.noinc and eng == "pe"
        self.noinc = False
        if _STOPPED[0]:
            return None
        reads = self._bufs(reads)
        writes = self._bufs(writes)
        e = self.engs[eng]
        self._wait(e, self._collect(reads, writes))
        ins = fn(e.obj)
        if noinc:
            self._commit(("e:" + eng, e.count + 1), reads, writes)
            return ins
        e.count += 1
        ins.then_inc(e.sem, 1)
        self._commit(("e:" + eng, e.count), reads, writes)
        return ins

    def dma(self, queue, out, in_, reads=(), writes=(), sembuf=None, **kw):
        if _STOPPED[0]:
            return None
        reads = self._bufs(reads)
        writes = self._bufs(writes)
        e = self.engs[queue]
        self._wait(e, self._collect(reads, writes))
        if sembuf is None:
            sembuf = (list(writes) + list(reads))[0]
        elif isinstance(sembuf, Tile):
            sembuf = sembuf.b
        key = self._dsem(sembuf, queue)
        ins = e.obj.dma_start(out=out, in_=in_, **kw)
        ins.then_inc(self.sems[key], 16)
        self.dcounts[key] += 1
        self._commit((key, 16 * self.dcounts[key]), reads, writes)
        return ins

    def barrier(self):
        if _STOPPED[0]:
            return
        targets = {}
        for n, e in self.engs.items():
            if e.count > 0:
                targets["e:" + n] = e.count
        for k, c in self.dcounts.items():
            targets[k] = 16 * c
        for n, e in self.engs.items():
            deps = {k: v for k, v in targets.items() if k != "e:" + n}
            self._wait(e, deps)

    def finish(self, eng="sp"):
        e = self.engs[eng]
        targets = {}
        for n, o in self.engs.items():
            if o.count > 0 and n != eng:
                targets["e:" + n] = o.count
        for k, c in self.dcounts.items():
            targets[k] = 16 * c
        self._wait(e, targets)


def build_program():
    _STOPPED[0] = False
    nc = bass.Bass("TRN2", target_bir_lowering=False)

    def din(name, shape):
        return nc.dram_tensor(name, list(shape), F32, kind="ExternalInput").ap()

    def dout(name, shape):
        return nc.dram_tensor(name, list(shape), F32, kind="ExternalOutput").ap()

    xp = din("xp", [SEQ, D])
    meta = din("meta", [16, D])
    xs = din("xs", [NS, D])
    ck = din("ck", [NS, 128, 128])
    cv = din("cv", [NS, 128, 128])
    spool = din("spool", [NS, 15, 512])
    sconv = din("sconv", [NS * 2, NUP])
    w_in = din("w_in", [D, NIN])
    b_in = din("b_in", [1, NIN])
    sinks = din("sinks", [1, 8])
    w_pool = din("w_pool", [512, 128])
    pool_scale = din("pool_scale", [1, 512])
    g_attn = din("g_attn", [1, 512])
    g_pool = din("g_pool", [1, 512])
    w_o = din("w_o", [D, D])
    g_pre_mix = din("g_pre_mix", [1, D])
    g_post_mix = din("g_post_mix", [1, D])
    g_pre_ffn = din("g_pre_ffn", [1, D])
    g_post_ffn = din("g_post_ffn", [1, D])
    w_up = din("w_up", [D, NUP])
    conv_w = din("conv_w", [3, NUP])
    conv_b = din("conv_b", [1, NUP])
    w_down = din("w_down", [DFF, D])
    c_ident = din("c_ident", [128, 128])
    c_bbase = din("c_bbase", [128, 8 * 256])
    c_cinv = din("c_cinv", [1, 4 * 128])
    c_sel = din("c_sel", [120, 2 * 4 * 16])
    c_sbias = din("c_sbias", [128, 129])

    y_prompt = dout("y_prompt", [SEQ, D])
    y_sample = dout("y_sample", [NS, D])
    k_prompt = dout("k_prompt", [128, 128])
    v_prompt = dout("v_prompt", [128, 128])
    pool_prompt = dout("pool_prompt", [15, 512])
    conv_prompt = dout("conv_prompt", [2, NUP])
    k_sample = dout("k_sample", [NS, 128, 128])
    v_sample = dout("v_sample", [NS, 128, 128])
    pool_sample = dout("pool_sample", [NS, 15, 512])
    conv_sample = dout("conv_sample", [NS, 2, NUP])

    x1d = nc.dram_tensor("x1_scratch", [(NBLK + 1) * 128, D], F32, kind="Internal").ap()
    x1d_bufs = [Buf("x1d%d" % i) for i in range(NBLK + 1)]
    sc_d = nc.dram_tensor("sconvT_scratch", [128, 44 * 32], F32, kind="Internal").ap()
    cw_d = nc.dram_tensor("convwT_scratch", [128, 44 * 4], F32, kind="Internal").ap()
    csd_buf = Buf("csd")
    outb = Buf("outs")

    with contextlib.ExitStack() as top:
        P = Prog(nc, top)
        try:
            _body(P, nc, locals())
        except _Stop:
            pass
        P.finish("sp")
        P.finish("pool")
    return nc


def _body(P, nc, L):
    (xp, meta, xs, ck, cv, spool, sconv, w_in, b_in, sinks, w_pool, pool_scale, g_attn, g_pool, w_o, g_pre_mix,
     g_post_mix, g_pre_ffn, g_post_ffn, w_up, conv_w, conv_b, w_down, c_ident, c_bbase, c_cinv, c_sel, c_sbias,
     y_prompt, y_sample, k_prompt, v_prompt, pool_prompt, conv_prompt, k_sample, v_sample, pool_sample, conv_sample,
     x1d, x1d_bufs, outb, sc_d, cw_d, csd_buf) = [L[k] for k in (
        "xp meta xs ck cv spool sconv w_in b_in sinks w_pool pool_scale g_attn g_pool w_o g_pre_mix "
        "g_post_mix g_pre_ffn g_post_ffn w_up conv_w conv_b w_down c_ident c_bbase c_cinv c_sel c_sbias "
        "y_prompt y_sample k_prompt v_prompt pool_prompt conv_prompt k_sample v_sample pool_sample conv_sample "
        "x1d x1d_bufs outb sc_d cw_d csd_buf").split()]
    if True:

        ident_f = P.sb("ident_f", [128, 128], F32)
        ident_b = P.sb("ident_b", [128, 128], BF16)
        eps_t = P.sb("eps", [128, 1], F32)
        P.dma("sp", ident_f[:], c_ident, writes=[ident_f])
        P.op("dve", lambda e: e.tensor_copy(out=ident_b[:], in_=ident_f[:]), reads=[ident_f], writes=[ident_b])
        P.op("dve", lambda e: e.memset(eps_t[:], EPS), writes=[eps_t])
        mhalf_t = P.sb("mhalf", [128, 1], F32)
        P.op("dve", lambda e: e.memset(mhalf_t[:], -0.5), writes=[mhalf_t])

        stat_ring = P.ring("stat", [128, 8], F32, 16)
        _stop("setup0")

        def rstd_from_ssq(ssq_ap, ssq_tile, n, M=128):
            stt = stat_ring.next()
            P.op("dve", lambda e: e.tensor_scalar(out=stt[0:M, 0:1], in0=ssq_ap, scalar1=1.0 / n, scalar2=EPS,
                                                  op0=ALU.mult, op1=ALU.add), reads=[ssq_tile], writes=[stt])
            P.op("pool", lambda e: e.tensor_tensor(out=stt[0:M, 2:3], in0=stt[0:M, 0:1], in1=mhalf_t[0:M, 0:1], op=ALU.pow),
                 reads=[stt, mhalf_t], writes=[stt])
            return stt, stt[0:M, 2:3]

        with contextlib.ExitStack() as s01:
            W_in_sb = P.sb("W_in", [128, 8, NIN], BF16, s01)
            W_kd = P.sb("W_kd", [128, 8, 2, 2, 64], BF16, s01)
            W_o_sb = P.sb("W_o", [128, 8, D], BF16, s01)
            W_pool_sb = P.sb("W_pool", [128, 4, 128], BF16, s01)
            b_in_bc = P.sb("b_in_bc", [128, NIN], F32, s01)
            b_fm = P.sb("b_fm", [128, 10], F32, s01)
            b_kd = P.sb("b_kd", [128, 2], F32, s01)
            g_pre_mix_bc = P.sb("g_pre_mix_bc", [128, D], F32, s01)
            g_mix_bc = P.sb("g_mix_bc", [128, D], F32, s01)
            g_post_mix_bc = P.sb("g_post_mix_bc", [128, D], F32, s01)
            pool_scale_bc = P.sb("pool_scale_bc", [128, 512], F32, s01)
            sink_bc = P.sb("sink_bc", [128, 8], F32, s01)
            B_hi = P.sb("B_hi", [128, 8, 256], BF16, s01)
            B_lo = P.sb("B_lo", [128, 8, 256], BF16, s01)
            cinv_bc = P.sb("cinv_bc", [128, 4, 128], F32, s01)

            P.dma("pool", W_in_sb[:], w_in.rearrange("(c p) n -> p c n", p=128), writes=[W_in_sb])
            for dup in range(2):
                for g in range(2):
                    P.dma("pool", W_kd[:, :, g, dup, :],
                          w_in[:, 512 + g * 64:512 + (g + 1) * 64].rearrange("(c p) d -> p c d", p=128), writes=[W_kd])
            P.dma("pool", W_o_sb[:], w_o.rearrange("(c p) n -> p c n", p=128), writes=[W_o_sb])
            P.dma("pool", W_pool_sb[:], w_pool.rearrange("(g c) d -> c g d", c=128), writes=[W_pool_sb])
            P.dma("sp", b_in_bc[:], b_in.partition_broadcast(128), writes=[b_in_bc])
            P.dma("sp", g_pre_mix_bc[:], g_pre_mix.partition_broadcast(128), writes=[g_pre_mix_bc])
            P.dma("sp", g_mix_bc[:, 0:512], g_attn.partition_broadcast(128), writes=[g_mix_bc])
            P.dma("sp", g_mix_bc[:, 512:1024], g_pool.partition_broadcast(128), writes=[g_mix_bc])
            P.dma("sp", g_post_mix_bc[:], g_post_mix.partition_broadcast(128), writes=[g_post_mix_bc])
            P.dma("sp", pool_scale_bc[:], pool_scale.partition_broadcast(128), writes=[pool_scale_bc])
            P.dma("sp", sink_bc[:], sinks.partition_broadcast(128), writes=[sink_bc])
            P.dma("sp", cinv_bc[:], c_cinv.rearrange("o (g t) -> o g t", g=4).partition_broadcast(128),
                  writes=[cinv_bc])
            ssq_ring = P.ring("ssq", [128, 4], F32, 12, s01)
            RG = {}

            def make_rings(stack, deep):
                RG["xb"] = P.ring("xb", [128, D], BF16, 2 if deep else 1, stack)
                RG["attn"] = P.ring("attn_sb", [128, 512], F32, 4 if deep else 1, stack)
                RG["pool_sb"] = P.ring("pool_sb", [128, 512], F32, 4 if deep else 1, stack)
                RG["mix_in"] = P.ring("mix_in", [128, D], BF16, 4 if deep else 1, stack)
                RG["mixT"] = P.ring("mixT", [128, 8, 128], BF16, 2 if deep else 1, stack)
                RG["x1"] = P.ring("x1", [128, D], F32, 2 if deep else 1, stack)

            class View:
                def __init__(self, ap, bufs):
                    self.t = ap
                    self.bufs = bufs

                def __getitem__(self, idx):
                    return self.t[idx]

            Q = [P.ps("Q%d" % i, [128, 1024], F32, s01) for i in range(4)]
            Hb = [Buf("H%d" % i) for i in range(8)]

            def half_ap(i):
                return Q[i // 2].t[:, (i % 2) * 512:(i % 2 + 1) * 512]

            tr_ring = Ring([View(half_ap(i).bitcast(BF16).rearrange("p (c t) -> p c t", c=8), [Hb[i]]) for i in (0, 1)])
            trA_ring = Ring([View(half_ap(i).bitcast(BF16).rearrange("p (c t) -> p c t", c=8), [Hb[i]]) for i in (6, 7)])
            mm_slots = [(half_ap(i), Hb[i]) for i in (2, 3, 4, 5, 6, 7)]
            mm_i = [0]
            sc_ring = Ring([View(Q[k].t[:].rearrange("p (j k q) -> p j k q", j=4, k=2), [Hb[2 * k], Hb[2 * k + 1]]) for k in (1, 2)])
            o_ring = Ring([View(half_ap(i).rearrange("p (j d) -> p j d", j=4), [Hb[i]]) for i in (6, 7)])
            wo_ring = Ring([View(Q[k].t[:], [Hb[2 * k], Hb[2 * k + 1]]) for k in (1, 2)])

            def next_mm():
                mm_i[0] = (mm_i[0] + 1) % len(mm_slots)
                return mm_slots[mm_i[0]]

            with contextlib.ExitStack() as sb0:
                brow = P.sb("brow", [1, NIN + 256], F32, sb0)
                P.dma("sp", brow[0:1, 0:NIN], b_in, writes=[brow])
                for g in range(2):
                    for dup in range(2):
                        c0 = NIN + g * 128 + dup * 64
                        P.dma("sp", brow[0:1, c0:c0 + 64], b_in[:, 512 + g * 64:512 + (g + 1) * 64], writes=[brow])
                bap, bbuf = mm_slots[0]
                for c in range(12):
                    P.op("pe", lambda e, c=c: e.transpose(out=bap[:, c:c + 1], in_=brow[0:1, c * 128:(c + 1) * 128],
                                                          identity=ident_f[0:1, 0:1]),
                         reads=[brow, ident_f], writes=[bbuf])
                P.op("dve", lambda e: e.tensor_copy(out=b_fm[:], in_=bap[:, 0:10]), reads=[bbuf], writes=[b_fm])
                P.op("dve", lambda e: e.tensor_scalar(out=b_fm[:, 0:4], in0=b_fm[:, 0:4], scalar1=0.125, scalar2=None, op0=ALU.mult),
                     writes=[b_fm])
                Bfull = P.sb("Bfull", [128, 8, 256], F32, sb0)
                P.dma("sp", Bfull[:], c_bbase.rearrange("p (h k) -> p h k", h=8), writes=[Bfull])
                for h in range(8):
                    P.op("dve", lambda e, h=h: e.tensor_scalar(out=Bfull[:, h, :], in0=Bfull[:, h, :],
                                                               scalar1=sink_bc[:, h:h + 1], scalar2=None, op0=ALU.subtract),
                         reads=[sink_bc], writes=[Bfull])
                P.op("dve", lambda e: e.tensor_copy(out=B_hi[:], in_=Bfull[:]), reads=[Bfull], writes=[B_hi])
                P.op("dve", lambda e: e.tensor_tensor(out=B_lo[:], in0=Bfull[:], in1=B_hi[:], op=ALU.subtract),
                     reads=[Bfull, B_hi], writes=[B_lo])
                P.op("dve", lambda e: e.tensor_copy(out=b_kd[:], in_=bap[:, 10:12]), reads=[bbuf], writes=[b_kd])
                P.barrier()

            def run(gen):
                for _ in gen:
                    pass

            def lockstep(gens):
                gens = list(gens)
                while gens:
                    for g_ in list(gens):
                        try:
                            next(g_)
                        except StopIteration:
                            gens.remove(g_)

            def norm_T(x_ap, x_tile, g_bc, dstT, col0):
                run(norm_T_gen(x_ap, x_tile, g_bc, dstT, col0))

            def norm_T_gen(x_ap, x_tile, g_bc, dstT, col0, trr=None):
                ssq = ssq_ring.next()
                xb = RG["xb"].next()
                tr = (trr if trr is not None else tr_ring).next()
                P.op("act", lambda e: e.activation(out=xb[:], in_=x_ap, func=AF.Square, accum_out=ssq[:, 0:1]),
                     reads=[x_tile], writes=[xb, ssq])
                yield
                stt, rstd = rstd_from_ssq(ssq[:, 0:1], ssq, D)
                yield
                P.op("dve", lambda e: e.scalar_tensor_tensor(out=xb[:], in0=x_ap, scalar=rstd, in1=g_bc[:],
                                                             op0=ALU.mult, op1=ALU.mult),
                     reads=[x_tile, stt, g_bc], writes=[xb])
                yield
                for c in range(8):
                    P.noinc = (c != 7)
                    P.op("pe", lambda e, c=c: e.transpose(out=tr[:, c, :], in_=xb[:, c * 128:(c + 1) * 128],
                                                          identity=ident_b[:]),
                         reads=[xb, ident_b], writes=tr.bufs)
                yield
                P.op("act", lambda e: e.copy(out=dstT[:, :, col0:col0 + 128], in_=tr[:]),
                     reads=tr.bufs, writes=[dstT])

            def tail_F_gen(attn, pool_mm_fn, out):
                pool_sb = RG["pool_sb"].next()
                mix_in = RG["mix_in"].next()
                ssq = ssq_ring.next()
                out["mix"] = mix_in
                pool_ps_ap, pool_ps_buf = pool_mm_fn()
                P.op("dve", lambda e: e.tensor_tensor(out=pool_sb[:], in0=pool_ps_ap, in1=pool_scale_bc[:], op=ALU.mult),
                     reads=[pool_ps_buf, pool_scale_bc], writes=[pool_sb])
                yield
                P.op("act", lambda e: e.activation(out=mix_in[:, 0:512], in_=attn[:], func=AF.Square, accum_out=ssq[:, 0:1]),
                     reads=[attn], writes=[mix_in, ssq])
                yield
                P.op("act", lambda e: e.activation(out=mix_in[:, 512:1024], in_=pool_sb[:], func=AF.Square, accum_out=ssq[:, 1:2]),
                     reads=[pool_sb], writes=[mix_in, ssq])
                st_a, r_a = rstd_from_ssq(ssq[:, 0:1], ssq, 512)
                yield
                P.op("dve", lambda e: e.scalar_tensor_tensor(out=mix_in[:, 0:512], in0=attn[:], scalar=r_a,
                                                             in1=g_mix_bc[:, 0:512], op0=ALU.mult, op1=ALU.mult),
                     reads=[attn, st_a, g_mix_bc], writes=[mix_in])
                st_p, r_p = rstd_from_ssq(ssq[:, 1:2], ssq, 512)
                yield
                P.op("dve", lambda e: e.scalar_tensor_tensor(out=mix_in[:, 512:1024], in0=pool_sb[:], scalar=r_p,
                                                             in1=g_mix_bc[:, 512:1024], op0=ALU.mult, op1=ALU.mult),
                     reads=[pool_sb, st_p, g_mix_bc], writes=[mix_in])

            def tail_G_gen(mix_in, x_ap, x_tile, x1row, x1buf):
                tr = tr_ring.next()
                mixT = RG["mixT"].next()
                wo = wo_ring.next()
                ssq2 = ssq_ring.next()
                x1 = RG["x1"].next()
                for c in range(8):
                    P.noinc = (c != 7)
                    P.op("pe", lambda e, c=c: e.transpose(out=tr[:, c, :], in_=mix_in[:, c * 128:(c + 1) * 128],
                                                          identity=ident_b[:]),
                         reads=[mix_in, ident_b], writes=tr.bufs)
                yield
                P.op("act", lambda e: e.copy(out=mixT[:], in_=tr[:]), reads=tr.bufs, writes=[mixT])
                yield
                for half in range(2):
                    for c in range(8):
                        P.noinc = (c != 7)
                        P.op("pe", lambda e, c=c, half=half: e.matmul(
                            out=wo[:, half * 512:(half + 1) * 512], lhsT=mixT[:, c, :],
                            rhs=W_o_sb[:, c, half * 512:(half + 1) * 512], start=(c == 0), stop=(c == 7)),
                            reads=[mixT, W_o_sb], writes=[wo.bufs[half]])
                    yield
                P.op("act", lambda e: e.activation(out=x1[:], in_=wo[:, 0:1024], func=AF.Square, accum_out=ssq2[:, 2:3]),
                     reads=wo.bufs, writes=[x1, ssq2])
                yield
                st_m, r_m = rstd_from_ssq(ssq2[:, 2:3], ssq2, D)
                yield
                for half in range(2):
                    P.op("dve", lambda e, half=half: e.scalar_tensor_tensor(
                        out=x1[:, half * 512:(half + 1) * 512], in0=wo[:, half * 512:(half + 1) * 512], scalar=r_m,
                        in1=g_post_mix_bc[:, half * 512:(half + 1) * 512], op0=ALU.mult, op1=ALU.mult),
                        reads=[wo.bufs[half], st_m, g_post_mix_bc], writes=[x1])
                    yield
                P.op("dve", lambda e: e.tensor_tensor(out=x1[:], in0=x1[:], in1=x_ap, op=ALU.add),
                     reads=[x_tile], writes=[x1])
                yield
                P.dma("sp", x1d[x1row:x1row + 128, :], x1[:], reads=[x1], writes=[x1buf], sembuf=x1)

            def mix_tail(x_ap, x_tile, attn, pool_ps_ap, pool_ps_buf, x1row, x1buf):
                out = {}
                run(tail_F_gen(attn, lambda: (pool_ps_ap, pool_ps_buf), out))
                run(tail_G_gen(out["mix"], x_ap, x_tile, x1row, x1buf))

            _stop("setup1")
            with contextlib.ExitStack() as s0:
                x_s = P.sb("x_s", [128, D], F32, s0)
                xT_s = P.sb("xT_s", [128, 8, 128], BF16, s0)
                z_s = P.sb("z_s", [128, NIN], F32, s0)
                q_hb = P.sb("q_hb", [128, 64], F32, s0)
                kn_hb = P.sb("kn_hb", [128, 64], F32, s0)
                vn_hb = P.sb("vn_hb", [128, 64], F32, s0)
                sink_hb = P.sb("sink_hb", [128, 1], F32, s0)
                sbias = P.sb("sbias", [128, 129], F32, s0)
                make_rings(s0, False)
                Kc = P.sb("Kc", [128, 128, 64], F32, s0)
                Vc = P.sb("Vc", [128, 128, 64], F32, s0)
                Kb = [Buf("Kc%d" % h) for h in range(8)]
                Vb = [Buf("Vc%d" % h) for h in range(8)]
                prod = P.sb("prod", [128, 128, 64], F32, s0)
                Sall = P.sb("Sall", [128, 129], F32, s0)
                Pm = P.sb("Pm", [128, 129], F32, s0)
                sm = P.sb("sm", [128, 8], F32, s0)
                o_hb = P.sb("o_hb", [128, 64], F32, s0)
                attn_s = P.sb("attn_s", [128, 512], F32, s0)
                spl = P.sb("spl", [128, 2, 512], F32, s0)
                sel = P.sb("sel", [128, 2, 4, 16], F32, s0)
                wsum = P.sb("wsum", [128, 512], F32, s0)
                d_s = P.sb("d_s", [128, 512], BF16, s0)
                dT_s = P.sb("dT_s", [128, 4, 128], BF16, s0)

                P.op("pool", lambda e: e.memset(x_s[:], 0.0), writes=[x_s])
                P.dma("sp", x_s[0:NS, :], xs, writes=[x_s])
                cstage = prod.t[0:36, 0:88, :].rearrange("p a b -> p (a b)")
                rs_sc = prod.t[:, 96:118, :].rearrange("p a b -> p (a b)").rearrange("p (c r) -> p c r", r=32)
                rs_cw = prod.t[:, 118:121, :].rearrange("p a b -> p (a b)")[:, 0:176].rearrange("p (c r) -> p c r", r=4)
                P.dma("sp", cstage[0:32, :], sconv, writes=[prod])
                P.dma("sp", cstage[32:35, :], conv_w, writes=[prod])
                P.dma("sp", cstage[35:36, :], conv_b, writes=[prod])
                for g0 in range(0, 44, 14):
                    n = min(14, 44 - g0)
                    bap_, bbuf_ = next_mm()
                    for k in range(n):
                        P.op("pe", lambda e, k=k: e.transpose(out=bap_[:, k * 36:(k + 1) * 36], in_=cstage[0:36, (g0 + k) * 128:(g0 + k + 1) * 128],
                                                              identity=ident_f[0:36, 0:36]),
                             reads=[prod, ident_f], writes=[bbuf_])
                    pv = bap_[:, 0:n * 36].rearrange("p (c r) -> p c r", r=36)
                    P.op("dve", lambda e: e.tensor_copy(out=rs_sc[:, g0:g0 + n, :], in_=pv[:, :, 0:32]), reads=[bbuf_], writes=[prod])
                    P.op("dve", lambda e: e.tensor_copy(out=rs_cw[:, g0:g0 + n, :], in_=pv[:, :, 32:36]), reads=[bbuf_], writes=[prod])
                P.dma("sp", sc_d, prod.t[:, 96:118, :].rearrange("p a b -> p (a b)"), reads=[prod], writes=[csd_buf], sembuf=prod)
                P.dma("sp", cw_d, prod.t[:, 118:121, :].rearrange("p a b -> p (a b)")[:, 0:176], reads=[prod], writes=[csd_buf], sembuf=prod)
                P.dma("sp", sbias[:], c_sbias, writes=[sbias])
                P.dma("sp", sel[0:120].rearrange("p t g b -> p (t g b)"), c_sel, writes=[sel])
                for t in range(2):
                    P.dma("sp", spl[0:120, t, :], spool[t * 8:(t + 1) * 8].rearrange("b r c -> (b r) c"), writes=[spl])
                def load_cache(dst, bufs, src):
                    for g in range(2):
                        h0 = 4 * g
                        P.dma("act", dst[h0 * 16:(h0 + 1) * 16], src[:, :, g * 64:(g + 1) * 64], writes=[bufs[h0]])
                    for g in range(2):
                        h0 = 4 * g
                        for j in range(1, 4):
                            P.dma("sp", dst[(h0 + j) * 16:(h0 + j + 1) * 16], dst[h0 * 16:(h0 + 1) * 16],
                                  reads=[bufs[h0]], writes=[bufs[h0 + j]])
                load_cache(Kc, Kb, ck)
                load_cache(Vc, Vb, cv)
                for h in range(8):
                    P.dma("sp", sink_hb[h * 16:(h + 1) * 16, :], sinks[:, h:h + 1].partition_broadcast(16), writes=[sink_hb])
                P.dma("sp", k_sample[:, 0:127, :], ck[:, 1:128, :], writes=[outb])
                P.dma("sp", v_sample[:, 0:127, :], cv[:, 1:128, :], writes=[outb])
                P.dma("sp", pool_sample[:, 0:14, :], spool[:, 1:15, :], writes=[outb])

                norm_T(x_s[:], x_s, g_pre_mix_bc, xT_s, 0)
                for (n0, n1) in ((0, 512), (512, 1024), (1024, NIN)):
                    ap, b = next_mm()
                    for c in range(8):
                        P.op("pe", lambda e, c=c, ap=ap, n0=n0, n1=n1: e.matmul(
                            out=ap[:, 0:n1 - n0], lhsT=xT_s[:, c, :], rhs=W_in_sb[:, c, n0:n1],
                            start=(c == 0), stop=(c == 7)), reads=[xT_s, W_in_sb], writes=[b])
                    P.op("dve", lambda e, ap=ap, n0=n0, n1=n1: e.tensor_tensor(
                        out=z_s[:, n0:n1], in0=ap[:, 0:n1 - n0], in1=b_in_bc[:, n0:n1], op=ALU.add),
                        reads=[b, b_in_bc], writes=[z_s])
                P.dma("pool", k_sample[:, 127, :], z_s[0:NS, 512:640], reads=[z_s], writes=[outb], sembuf=z_s)
                P.dma("pool", v_sample[:, 127, :], z_s[0:NS, 640:768], reads=[z_s], writes=[outb], sembuf=z_s)
                P.dma("pool", pool_sample[:, 14, :], z_s[0:NS, 768:1280], reads=[z_s], writes=[outb], sembuf=z_s)
                _stop("p0a")
                for h in range(8):
                    g = h // 4
                    P.dma("sp", q_hb[h * 16:(h + 1) * 16, :], z_s[0:NS, h * 64:(h + 1) * 64], reads=[z_s], writes=[q_hb])
                    P.dma("sp", kn_hb[h * 16:(h + 1) * 16, :], z_s[0:NS, 512 + g * 64:512 + (g + 1) * 64], reads=[z_s], writes=[kn_hb])
                    P.dma("sp", vn_hb[h * 16:(h + 1) * 16, :], z_s[0:NS, 640 + g * 64:640 + (g + 1) * 64], reads=[z_s], writes=[vn_hb])
                P.op("dve", lambda e: e.tensor_tensor(out=prod[:], in0=Kc[:], in1=q_hb[:].unsqueeze(1).to_broadcast([128, 128, 64]),
                                                      op=ALU.mult), reads=Kb + [q_hb], writes=[prod])
                P.op("dve", lambda e: e.tensor_reduce(out=Sall[:, 0:128], in_=prod[:], axis=AX.X, op=ALU.add),
                     reads=[prod], writes=[Sall])
                P.op("dve", lambda e: e.tensor_tensor(out=o_hb[:], in0=kn_hb[:], in1=q_hb[:], op=ALU.mult),
                     reads=[kn_hb, q_hb], writes=[o_hb])
                P.op("dve", lambda e: e.tensor_reduce(out=Sall[:, 128:129], in_=o_hb[:], axis=AX.X, op=ALU.add),
                     reads=[o_hb], writes=[Sall])
                P.op("dve", lambda e: e.scalar_tensor_tensor(out=Sall[:], in0=Sall[:], scalar=0.125, in1=sbias[:],
                                                             op0=ALU.mult, op1=ALU.add), reads=[sbias], writes=[Sall])
                P.op("dve", lambda e: e.tensor_reduce(out=sm[:, 0:1], in_=Sall[:], axis=AX.X, op=ALU.max),
                     reads=[Sall], writes=[sm])
                P.op("dve", lambda e: e.tensor_tensor(out=sm[:, 1:2], in0=sm[:, 0:1], in1=sink_hb[:], op=ALU.max),
                     reads=[sink_hb], writes=[sm])
                P.op("dve", lambda e: e.tensor_scalar(out=sm[:, 2:3], in0=sm[:, 1:2], scalar1=-1.0, scalar2=None, op0=ALU.mult),
                     writes=[sm])
                P.op("act", lambda e: e.activation(out=Pm[:], in_=Sall[:], func=AF.Exp, bias=sm[:, 2:3], scale=1.0,
                                                   accum_out=sm[:, 3:4]), reads=[Sall, sm], writes=[Pm, sm])
                P.op("act", lambda e: e.activation(out=sm[:, 4:5], in_=sink_hb[:], func=AF.Exp, bias=sm[:, 2:3], scale=1.0),
                     reads=[sink_hb], writes=[sm])
                P.op("dve", lambda e: e.tensor_tensor(out=sm[:, 5:6], in0=sm[:, 3:4], in1=sm[:, 4:5], op=ALU.add), writes=[sm])
                P.op("dve", lambda e: e.reciprocal(out=sm[:, 6:7], in_=sm[:, 5:6]), writes=[sm])
                P.op("dve", lambda e: e.tensor_tensor(out=prod[:], in0=Vc[:],
                                                      in1=Pm[:, 0:128].unsqueeze(2).to_broadcast([128, 128, 64]),
                                                      op=ALU.mult), reads=Vb + [Pm], writes=[prod])
                P.op("dve", lambda e: e.tensor_reduce(out=o_hb[:], in_=prod[:].rearrange("p k d -> p d k"), axis=AX.X,
                                                      op=ALU.add), reads=[prod], writes=[o_hb])
                P.op("dve", lambda e: e.scalar_tensor_tensor(out=o_hb[:], in0=vn_hb[:], scalar=Pm[:, 128:129], in1=o_hb[:],
                                                             op0=ALU.mult, op1=ALU.add), reads=[vn_hb, Pm], writes=[o_hb])
                P.op("dve", lambda e: e.tensor_scalar(out=o_hb[:], in0=o_hb[:], scalar1=sm[:, 6:7], scalar2=None, op0=ALU.mult),
                     reads=[sm], writes=[o_hb])
                P.op("pool", lambda e: e.memset(attn_s[:], 0.0), writes=[attn_s])
                for h in range(8):
                    P.dma("sp", attn_s[0:NS, h * 64:(h + 1) * 64], o_hb[h * 16:(h + 1) * 16, :], reads=[o_hb], writes=[attn_s])
                _stop("p0b")
                ap, b = next_mm()
                for g in range(4):
                    for t in range(2):
                        P.op("pe", lambda e, g=g, t=t, ap=ap: e.matmul(
                            out=ap[0:NS, g * 128:(g + 1) * 128], lhsT=sel[0:120, t, g, :],
                            rhs=spl[0:120, t, g * 128:(g + 1) * 128], start=(t == 0), stop=(t == 1)),
                            reads=[sel, spl], writes=[b])
                P.op("pool", lambda e: e.memset(d_s[:], 0.0), writes=[d_s])
                P.op("dve", lambda e, ap=ap: e.tensor_tensor(out=wsum[0:NS, :], in0=ap[0:NS, :], in1=z_s[0:NS, 768:1280], op=ALU.add),
                     reads=[b, z_s], writes=[wsum])
                for g in range(4):
                    P.op("dve", lambda e, g=g: e.scalar_tensor_tensor(
                        out=d_s[0:NS, g * 128:(g + 1) * 128], in0=wsum[0:NS, g * 128:(g + 1) * 128],
                        scalar=1.0 / POOL_W[g], in1=z_s[0:NS, 768 + g * 128:768 + (g + 1) * 128],
                        op0=ALU.mult, op1=ALU.subtract), reads=[wsum, z_s], writes=[d_s])
                trs = tr_ring.next()
                for g in range(4):
                    P.op("pe", lambda e, g=g: e.transpose(out=trs[:, g, :], in_=d_s[:, g * 128:(g + 1) * 128], identity=ident_b[:]),
                         reads=[d_s, ident_b], writes=trs.bufs)
                P.op("dve", lambda e: e.tensor_copy(out=dT_s[:], in_=trs[:, 0:4, :]), reads=trs.bufs, writes=[dT_s])
                pap, pb = next_mm()
                for g in range(4):
                    P.op("pe", lambda e, g=g, pap=pap: e.matmul(out=pap[:, g * 128:(g + 1) * 128], lhsT=dT_s[:, g, :],
                                                                  rhs=W_pool_sb[:, g, :], start=True, stop=True),
                         reads=[dT_s, W_pool_sb], writes=[pb])
                mix_tail(x_s[:], x_s, attn_s, pap, pb, NBLK * 128, x1d_bufs[NBLK])
                P.barrier()
                _stop("p0")

            with contextlib.ExitStack() as s1:
                T = NB1 * 128
                make_rings(s1, True)
                x_ring = P.ring("x_tm", [128, NB1, D], F32, 2, s1)
                xT = P.sb("xT", [128, 8, T], BF16, s1)
                qT = P.sb("qT", [128, 4, T], BF16, s1)
                kT2 = P.sb("kT2", [128, 2, 2, 128 + T], BF16, s1)
                uT = P.sb("uT", [128, 4, HAL + T], F32, s1)
                pA = P.sb("pA", [128, 4, HAL + T], F32, s1)
                pB = P.sb("pB", [128, 3, HAL + T], F32, s1)
                dd = P.sb("dd", [128, 4, T], BF16, s1)
                kv_ring = P.ring("kv_sb", [128, 256], F32, 2, s1)
                v_aug = P.sb("v_aug", [128, NB1 + 1, 2, 66], BF16, s1)
                PT_ring = P.ring("PT", [128, 4, 2, 128], BF16, 2, s1)
                PT_half = {id(t): [Buf("PTa"), Buf("PTb")] for t in PT_ring.tiles}
                den_ring = P.ring("den", [128, 8], F32, 4, s1)

                P.op("pool", lambda e: e.memset(kT2[:], 0.0), writes=[kT2])
                P.op("pool", lambda e: e.memset(uT[:], 0.0), writes=[uT])
                P.op("pool", lambda e: e.memset(v_aug[:], 0.0), writes=[v_aug])

                supers = [[0]] + [list(range(1 + i * NB1, 1 + (i + 1) * NB1)) for i in range((NBLK - 1) // NB1)]
                if DBG_NSUPER is not None:
                    supers = supers[:DBG_NSUPER]
                xts = {}

                def load_x(si):
                    xt = x_ring.next()
                    for bi, B in enumerate(supers[si]):
                        if B == 0:
                            P.op("pool", lambda e: e.memset(xt[:, 0, :], 0.0), writes=[xt])
                            P.dma("sp", xt[112:128, 0, :], meta, writes=[xt])
                        else:
                            P.dma("sp", xt[:, bi, :], xp[(B - 1) * 128:B * 128, :], writes=[xt])
                    xts[si] = xt

                def stage_B(si, prev_Tn):
                    blocks = supers[si]
                    Tn = len(blocks) * 128
                    if prev_Tn is not None:
                        P.op("pool", lambda e: e.tensor_copy(out=kT2[:, :, :, 0:128], in_=kT2[:, :, :, prev_Tn:prev_Tn + 128]), writes=[kT2])
                        P.op("pool", lambda e: e.tensor_copy(out=uT[:, :, 0:HAL], in_=uT[:, :, prev_Tn:prev_Tn + HAL]), writes=[uT])
                    for c_out in range(4):
                        ap, hb = next_mm()
                        for c in range(8):
                            P.noinc = (c != 7)
                            P.op("pe", lambda e, c=c: e.matmul(
                                out=ap[:, 0:Tn], lhsT=W_in_sb[:, c, c_out * 128:(c_out + 1) * 128], rhs=xT[:, c, 0:Tn],
                                start=(c == 0), stop=(c == 7)), reads=[W_in_sb, xT], writes=[hb])
                        P.op("act", lambda e: e.activation(
                            out=qT[:, c_out, 0:Tn], in_=ap[:, 0:Tn], func=AF.Identity, bias=b_fm[:, c_out:c_out + 1], scale=0.125),
                            reads=[hb, b_fm], writes=[qT])
                    for g in range(2):
                        ap, hb = next_mm()
                        for c in range(8):
                            P.noinc = (c != 7)
                            P.op("pe", lambda e, c=c: e.matmul(
                                out=ap[:, 0:Tn], lhsT=W_kd[:, c, g, :, :].rearrange("p a d -> p (a d)"), rhs=xT[:, c, 0:Tn],
                                start=(c == 0), stop=(c == 7)), reads=[W_kd, xT], writes=[hb])
                        for half in range(2):
                            hs = slice(half * 64, (half + 1) * 64)
                            P.op("act", lambda e, half=half, hs=hs: e.activation(
                                out=kT2[hs, g, half, 128:128 + Tn], in_=ap[hs, 0:Tn], func=AF.Identity, bias=b_kd[hs, g:g + 1], scale=1.0),
                                reads=[hb, b_kd], writes=[kT2])
                    for g in range(4):
                        ap, hb = next_mm()
                        for c in range(8):
                            P.noinc = (c != 7)
                            P.op("pe", lambda e, c=c: e.matmul(
                                out=ap[:, 0:Tn], lhsT=W_in_sb[:, c, 768 + g * 128:768 + (g + 1) * 128], rhs=xT[:, c, 0:Tn],
                                start=(c == 0), stop=(c == 7)), reads=[W_in_sb, xT], writes=[hb])
                        P.op("act", lambda e: e.activation(
                            out=uT[:, g, HAL:HAL + Tn], in_=ap[:, 0:Tn], func=AF.Identity, bias=b_fm[:, 6 + g:7 + g], scale=1.0),
                            reads=[hb, b_fm], writes=[uT])
                    if blocks[0] == 0:
                        P.op("pool", lambda e: e.memset(uT[:, :, HAL:HAL + 112], 0.0), writes=[uT])

                def stage_C_gen(si):
                    blocks = supers[si]
                    Tn = len(blocks) * 128
                    Wd = HAL + Tn
                    P.op("pool", lambda e: e.tensor_tensor(out=pA[:, :, 1:Wd], in0=uT[:, :, 1:Wd], in1=uT[:, :, 0:Wd - 1], op=ALU.add),
                         reads=[uT], writes=[pA])
                    yield
                    P.op("pool", lambda e: e.tensor_tensor(out=pB[:, :, 3:Wd], in0=pA[:, 1:4, 3:Wd], in1=pA[:, 1:4, 1:Wd - 2], op=ALU.add),
                         reads=[pA], writes=[pB])
                    yield

                    def emit_d(g, src, idx):
                        w = POOL_W[g]
                        if blocks[0] == 0:
                            tmpu = RG["pool_sb"].next()
                            P.op("dve", lambda e: e.tensor_tensor(out=tmpu[:, 0:128], in0=src[:, idx, HAL:HAL + 128],
                                                                  in1=cinv_bc[:, g, :], op=ALU.mult),
                                 reads=[src, cinv_bc], writes=[tmpu])
                            P.op("dve", lambda e: e.tensor_tensor(out=dd[:, g, 0:128], in0=tmpu[:, 0:128],
                                                                  in1=uT[:, g, HAL:HAL + 128], op=ALU.subtract),
                                 reads=[tmpu, uT], writes=[dd])
                        else:
                            P.op("dve", lambda e: e.scalar_tensor_tensor(
                                out=dd[:, g, 0:Tn], in0=src[:, idx, HAL:HAL + Tn], scalar=1.0 / w, in1=uT[:, g, HAL:HAL + Tn],
                                op0=ALU.mult, op1=ALU.subtract), reads=[src, uT], writes=[dd])
                    emit_d(0, pA, 0)
                    yield
                    emit_d(1, pB, 0)
                    yield
                    P.op("pool", lambda e: e.tensor_tensor(out=pA[:, 0:2, 7:Wd], in0=pB[:, 1:3, 7:Wd], in1=pB[:, 1:3, 3:Wd - 4], op=ALU.add),
                         reads=[pB], writes=[pA])
                    yield
                    emit_d(2, pA, 0)
                    yield
                    P.op("pool", lambda e: e.tensor_tensor(out=pB[:, 0:1, 15:Wd], in0=pA[:, 1:2, 15:Wd], in1=pA[:, 1:2, 7:Wd - 8], op=ALU.add),
                         reads=[pA], writes=[pB])
                    yield
                    emit_d(3, pB, 0)
                    yield
                    if blocks[-1] == NBLK - 1:
                        ap, hb = half_ap(0), Hb[0]
                        for g in range(4):
                            P.op("pe", lambda e, g=g: e.transpose(
                                out=ap[0:15, g * 128:(g + 1) * 128], in_=uT[:, g, HAL + Tn - 15:HAL + Tn], identity=ident_f[:]),
                                reads=[uT, ident_f], writes=[hb])
                        tmpu = RG["pool_sb"].next()
                        P.op("dve", lambda e: e.tensor_copy(out=tmpu[0:15, :], in_=ap[0:15, :]), reads=[hb], writes=[tmpu])
                        P.dma("pool", pool_prompt, tmpu[0:15, :], reads=[tmpu], writes=[outb], sembuf=tmpu)

                def stage_D(si, prev_nb):
                    blocks = supers[si]
                    if prev_nb is not None:
                        P.op("pool", lambda e: e.tensor_copy(out=v_aug[:, 0], in_=v_aug[:, prev_nb]), writes=[v_aug])
                    for bi, B in enumerate(blocks):
                        ap, hb = next_mm()
                        for c in range(8):
                            P.noinc = (c != 7)
                            P.op("pe", lambda e, c=c: e.matmul(
                                out=ap[:, 0:256], lhsT=xT[:, c, bi * 128:(bi + 1) * 128], rhs=W_in_sb[:, c, 512:768],
                                start=(c == 0), stop=(c == 7)), reads=[xT, W_in_sb], writes=[hb])
                        kv_sb = kv_ring.next()
                        P.op("dve", lambda e: e.tensor_tensor(out=kv_sb[:], in0=ap[:, 0:256], in1=b_in_bc[:, 512:768], op=ALU.add),
                             reads=[hb, b_in_bc], writes=[kv_sb])
                        P.op("pool", lambda e: e.tensor_copy(
                            out=v_aug[:, bi + 1, :, 0:64], in_=kv_sb[:, 128:256].rearrange("p (g d) -> p g d", g=2)),
                            reads=[kv_sb], writes=[v_aug])
                        P.op("pool", lambda e: e.memset(v_aug[:, bi + 1, :, 64:65], 1.0), writes=[v_aug])
                        if B == 0:
                            P.op("pool", lambda e: e.memset(v_aug[0:112, bi + 1, :, :], 0.0), writes=[v_aug])
                        if B == NBLK - 1:
                            P.dma("pool", k_prompt, kv_sb[:, 0:128], reads=[kv_sb], writes=[outb], sembuf=kv_sb)
                            P.dma("pool", v_prompt, kv_sb[:, 128:256], reads=[kv_sb], writes=[outb], sembuf=kv_sb)

                def step(gens):
                    for g_ in list(gens):
                        try:
                            next(g_)
                        except StopIteration:
                            gens.remove(g_)

                def stage_E(si, bg):
                    blocks = supers[si]
                    nb = len(blocks)
                    units = [(bi, g) for bi in range(nb) for g in range(2)]
                    attn_tiles = [RG["attn"].next() for _ in blocks]
                    outs = [dict() for _ in blocks]
                    scv = {}
                    fgens = []
                    pending_F = []

                    def QK(u):
                        bi, g = units[u]
                        sc = sc_ring.next()
                        scv[u] = sc
                        for hf in range(2):
                            h0 = 4 * g + 2 * hf
                            for Bt, first in ((B_hi, True), (B_lo, False)):
                                P.noinc = True
                                P.op("pe", lambda e, hf=hf, h0=h0, Bt=Bt, first=first: e.matmul(
                                    out=sc[:, 2 * hf:2 * hf + 2, :, :].rearrange("p j k q -> p (j k q)"), lhsT=ident_b[:],
                                    rhs=Bt[:, h0:h0 + 2, :].rearrange("p h k -> p (h k)"),
                                    start=first, stop=False), reads=[Bt, ident_b], writes=[sc.bufs[hf]])
                            for j in (2 * hf, 2 * hf + 1):
                                h = 4 * g + j
                                cq, half = h // 2, h % 2
                                for kb in range(2):
                                    k0 = (bi + kb) * 128
                                    last = (j == 2 * hf + 1 and kb == 1)
                                    P.noinc = (not last)
                                    P.op("pe", lambda e, j=j, kb=kb, k0=k0, cq=cq, half=half, last=last: e.matmul(
                                        out=sc[:, j, kb, :], lhsT=kT2[:, g, half, k0:k0 + 128],
                                        rhs=qT[:, cq, bi * 128:(bi + 1) * 128], start=False, stop=last),
                                        reads=[kT2, qT], writes=[sc.bufs[hf]])

                    def SM_PV_gen(u):
                        bi, g = units[u]
                        sc = scv.pop(u)
                        PT = PT_ring.next()
                        o = o_ring.next()
                        den = den_ring.next()
                        attn = attn_tiles[bi]
                        for hf in range(2):
                            P.op("act", lambda e, hf=hf: e.activation(
                                out=PT[:, 2 * hf:2 * hf + 2, :, :].rearrange("p j k q -> p (j k q)"),
                                in_=sc[:, 2 * hf:2 * hf + 2, :, :].rearrange("p j k q -> p (j k q)"), func=AF.Exp),
                                reads=[sc.bufs[hf]], writes=[PT_half[id(PT)][hf]])
                            yield
                        for j in range(4):
                            for kb in range(2):
                                P.noinc = (not (j == 3 and kb == 1))
                                P.op("pe", lambda e, j=j, kb=kb: e.matmul(
                                    out=o[:, j, 0:65], lhsT=PT[:, j, kb, :], rhs=v_aug[:, bi + kb, g, 0:65],
                                    start=(kb == 0), stop=(kb == 1)), reads=[PT_half[id(PT)][j // 2], v_aug], writes=o.bufs)
                        yield
                        P.op("dve", lambda e: e.tensor_scalar(out=den[:, 0:4], in0=o[:, :, 64], scalar1=1.0, scalar2=None, op0=ALU.add),
                             reads=o.bufs, writes=[den])
                        yield
                        P.op("dve", lambda e: e.reciprocal(out=den[:, 4:8], in_=den[:, 0:4]), writes=[den])
                        yield
                        P.op("dve", lambda e: e.tensor_tensor(
                            out=attn[:, g * 256:(g + 1) * 256].rearrange("p (j d) -> p j d", j=4), in0=o[:, :, 0:64],
                            in1=den[:, 4:8].unsqueeze(2).to_broadcast([128, 4, 64]), op=ALU.mult),
                            reads=o.bufs + [den], writes=[attn])

                    QK(0)
                    if len(units) > 1:
                        QK(1)
                    for p in range(0, len(units), 2):
                        pair = [SM_PV_gen(u) for u in (p, p + 1) if u < len(units)]
                        while pair:
                            step(pair)
                            step(fgens)
                        for u in (p + 2, p + 3):
                            if u < len(units):
                                QK(u)
                        for _ in range(4):
                            step(bg)
                        pending_F.append(p // 2)
                        if not bg:
                            for bi in pending_F:
                                fgens.append(stage_F_gen(si, bi, attn_tiles[bi], outs[bi]))
                            pending_F = []
                    while bg:
                        step(bg)
                    for bi in pending_F:
                        fgens.append(stage_F_gen(si, bi, attn_tiles[bi], outs[bi]))
                    lockstep(fgens)
                    return outs

                fslot = [0]

                def stage_F_gen(si, bi, attn, out):
                    def pool_mm():
                        fslot[0] = (fslot[0] + 1) % 2
                        pap, pb = half_ap(fslot[0]), Hb[fslot[0]]
                        for g in range(4):
                            P.noinc = (g != 3)
                            P.op("pe", lambda e, g=g: e.matmul(
                                out=pap[:, g * 128:(g + 1) * 128], lhsT=dd[:, g, bi * 128:(bi + 1) * 128],
                                rhs=W_pool_sb[:, g, :], start=True, stop=True), reads=[dd, W_pool_sb], writes=[pb])
                        return pap, pb
                    return tail_F_gen(attn, pool_mm, out)

                def stage_G_gen(si, bi, mix_in):
                    B = supers[si][bi]
                    return tail_G_gen(mix_in, xts[si][:, bi, :], xts[si], B * 128, x1d_bufs[B])

                def stage_A_gen(si, bi):
                    return norm_T_gen(xts[si][:, bi, :], xts[si], g_pre_mix_bc, xT, bi * 128, trA_ring)

                load_x(0)
                if len(supers) > 1:
                    load_x(1)
                for bi in range(len(supers[0])):
                    run(stage_A_gen(0, bi))
                stage_B(0, None)
                stage_D(0, None)
                for si, blocks in enumerate(supers):
                    nb = len(blocks)
                    has_next = si + 1 < len(supers)
                    nnb = len(supers[si + 1]) if has_next else 0
                    outs = stage_E(si, [stage_C_gen(si)])
                    _stop("p1e")
                    for p0 in range(0, max(nb, nnb), 2):
                        gens = []
                        for bi in range(p0, min(p0 + 2, max(nb, nnb))):
                            if bi < nnb:
                                gens.append(stage_A_gen(si + 1, bi))
                            if bi < nb:
                                gens.append(stage_G_gen(si, bi, outs[bi]["mix"]))
                        lockstep(gens)
                    if has_next:
                        stage_B(si + 1, nb * 128)
                        stage_D(si + 1, nb)
                        if si + 2 < len(supers):
                            load_x(si + 2)
                P.barrier()
                _stop("p1")
        with contextlib.ExitStack() as s2:
            T2 = NB2 * 128
            W_up_sb = P.sb("W_up", [128, 8, NUP], BF16, s2)
            W_dn_sb = P.sb("W_dn", [128, NJ, D], BF16, s2)
            g_pre_ffn_bc = P.sb("g_pre_ffn_bc", [128, D], F32, s2)
            g_post_ffn_bc = P.sb("g_post_ffn_bc", [128, D], F32, s2)
            cwb = P.sb("cwb", [128, 44, 4], F32, s2)
            halo = P.sb("halo", [128, 44, 2], F32, s2)
            WUP_GROUPS = [(0, 6), (6, 12), (12, 17), (17, 22)]
            wup_bufs = [Buf("wup%d" % q) for q in range(len(WUP_GROUPS))]
            wup_of_j = {}
            for q, (j0, j1) in enumerate(WUP_GROUPS):
                for j in range(j0, j1):
                    wup_of_j[j] = wup_bufs[q]
                for hv in range(2):
                    c0, c1 = hv * DFF + j0 * 128, hv * DFF + j1 * 128
                    P.dma("pool", W_up_sb[:, :, c0:c1], w_up[:, c0:c1].rearrange("(c p) n -> p c n", p=128),
                          writes=[wup_bufs[q]])
            for j0 in range(0, NJ, 11):
                P.dma("pool", W_dn_sb[:, j0:j0 + 11, :], w_down[j0 * 128:(j0 + 11) * 128, :].rearrange("(j p) n -> p j n", p=128),
                      writes=[W_dn_sb])
            P.dma("sp", g_pre_ffn_bc[:], g_pre_ffn.partition_broadcast(128), writes=[g_pre_ffn_bc])
            P.dma("sp", g_post_ffn_bc[:], g_post_ffn.partition_broadcast(128), writes=[g_post_ffn_bc])
            P.op("pool", lambda e: e.memset(halo[:], 0.0), writes=[halo])

            xb2_ring = P.ring("xb2", [128, D], BF16, 1, s2)
            ssq2_ring = P.ring("ssq_f", [128, 4], F32, 6, s2)
            x1_in = P.ring("x1_in", [128, D], F32, 2, s2)
            x1nT = P.sb("x1nT", [128, 8, T2], BF16, s2)
            a_sb = P.sb("a_sb", [128, NJ, T2], BF16, s2)
            a_main = a_sb
            a_bufs = [Buf("a%d" % j) for j in range(NJ)]
            a_alias = a_bufs[0:10]
            x1nT_s = P.sb("x1nT_s", [128, 8, NS], BF16, s2)
            a_s = Tile(a_sb.t[:, 9, 0:NJ * NS].rearrange("p (j n) -> p j n", n=NS), Buf("a_s"))
            ytmp_ring = P.ring("ytmp", [128, D], F32, 2, s2)
            ytmp = ytmp_ring.tiles[0]

            tr2 = P.ps("tr2", [128, 8, 128], BF16, s2)
            gv_ps = [P.ps("gv%d" % i, [128, 2, 512], F32, s2) for i in range(2)]
            ffn_ps = P.ps("ffn", [128, 1024], F32, s2)
            ffA = Buf("ffA")
            ffB = Buf("ffB")
            trf = P.ps("trf", [128, 512], F32, s2)

            def norm_pre2(x_ap, x_tile):
                ssq = ssq2_ring.next()
                xb = xb2_ring.next()
                P.op("act", lambda e: e.activation(out=xb[:], in_=x_ap, func=AF.Square, accum_out=ssq[:, 0:1]),
                     reads=[x_tile], writes=[xb, ssq])
                stt, rstd = rstd_from_ssq(ssq[:, 0:1], ssq, D)
                P.op("dve", lambda e: e.scalar_tensor_tensor(out=xb[:], in0=x_ap, scalar=rstd, in1=g_pre_ffn_bc[:],
                                                             op0=ALU.mult, op1=ALU.mult),
                     reads=[x_tile, stt, g_pre_ffn_bc], writes=[xb])
                return xb

            def norm_tr2(xb, dstT, col0):
                for c in range(8):
                    P.noinc = (c != 7)
                    P.op("pe", lambda e, c=c: e.transpose(out=tr2[:, c, :], in_=xb[:, c * 128:(c + 1) * 128], identity=ident_b[:]),
                         reads=[xb, ident_b], writes=[tr2])
                P.op("act", lambda e: e.copy(out=dstT[:, :, col0:col0 + 128], in_=tr2[:]), reads=[tr2], writes=[dstT])

            class ChAcc:
                def __init__(self, at, b):
                    self.at = at
                    self.b = b

            def to_token_major(src, n, dst_dram_fn):
                for q in range(11):
                    for i in range(4):
                        jj = q * 4 + i
                        P.op("pe", lambda e, jj=jj, i=i: e.transpose(out=trf[0:n, i * 128:(i + 1) * 128], in_=src.at(jj),
                                                                     identity=ident_f[:]),
                             reads=[src.b, ident_f, a_alias], writes=[trf])
                    P.op("dve", lambda e: e.tensor_copy(out=ytmp[0:n, 0:512], in_=trf[0:n, :]), reads=[trf], writes=[ytmp])
                    P.dma("pool", dst_dram_fn(q), ytmp[0:n, 0:512], reads=[ytmp], writes=[outb], sembuf=ytmp)

            class UpBuf:
                def __init__(self, name, T, stack):
                    self.up = [P.sb("%s_up%d" % (name, hv), [128, 2 + T], F32, stack) for hv in range(2)]
                    self.hb = [Buf("%s_halo%d" % (name, hv)) for hv in range(2)]
                    self.c = [P.sb("%s_c%d" % (name, hv), [128, T], F32, stack) for hv in range(2)]

            def ffn_norm_pre(B, bi):
                x1 = x1_in.next()
                P.dma("sp", x1[:], x1d[B * 128:(B + 1) * 128, :], reads=[x1d_bufs[B]], writes=[x1])
                return norm_pre2(x1[:], x1)

            def ffn_norm_tr(B, bi, xb, dst=None, ncols=128):
                if dst is not None:
                    for c in range(8):
                        P.noinc = (c != 7)
                        P.op("pe", lambda e, c=c: e.transpose(out=tr2[:, c, :], in_=xb[:, c * 128:(c + 1) * 128], identity=ident_b[:]),
                             reads=[xb, ident_b], writes=[tr2])
                    P.op("dve", lambda e: e.tensor_copy(out=dst[:, :, 0:ncols], in_=tr2[:, :, 0:ncols]), reads=[tr2], writes=[dst])
                    return
                norm_tr2(xb, x1nT, bi * 128)
                if B == 0:
                    P.op("pool", lambda e: e.memset(x1nT[:, :, 0:112], 0.0), writes=[x1nT])

            def ffn_norm(B, bi):
                ffn_norm_tr(B, bi, ffn_norm_pre(B, bi))

            def run2(gen):
                for _ in gen:
                    pass

            def lockstep2(gens):
                gens = list(gens)
                while gens:
                    for g_ in list(gens):
                        try:
                            next(g_)
                        except StopIteration:
                            gens.remove(g_)

            def ffn_up(blocks, sample, ub_ring, scT=None, upS=None):
                run2(ffn_up_gen(blocks, sample, ub_ring, x1nT, a_sb, scT, upS))

            def ffn_halo_gen():
                for j in range(NJ):
                    gv = gv_ps[j % 2]
                    for hv in range(2):
                        col = hv * DFF + j * 128
                        for c in range(8):
                            P.noinc = (c != 7)
                            P.op("pe", lambda e, c=c, hv=hv, col=col: e.matmul(
                                out=gv[:, hv, 0:16], lhsT=W_up_sb[:, c, col:col + 128], rhs=x1nT[:, c, 112:128],
                                start=(c == 0), stop=(c == 7)), reads=[wup_of_j[j], x1nT], writes=[gv])
                    for hv in range(2):
                        ch = hv * NJ + j
                        P.op("act", lambda e, hv=hv, ch=ch: e.copy(out=halo[:, ch, :], in_=gv[:, hv, 14:16]),
                             reads=[gv], writes=[halo])
                    yield

            def ffn_up_gen(blocks, sample, ub_ring, x1nT, a_sb, scT=None, upS=None):
                N = NS if sample else len(blocks) * 128
                pend = None
                for j in range(NJ + 1):
                    if j < NJ:
                        gv = gv_ps[j % 2]
                        ub = ub_ring[j % len(ub_ring)]
                        for hv in range(2):
                            col = hv * DFF + j * 128
                            for c in range(8):
                                P.noinc = (c != 7)
                                P.op("pe", lambda e, c=c, hv=hv, col=col, gv=gv: e.matmul(
                                    out=gv[:, hv, 0:N], lhsT=W_up_sb[:, c, col:col + 128], rhs=x1nT[:, c, 0:N],
                                    start=(c == 0), stop=(c == 7)), reads=[wup_of_j[j], x1nT], writes=[gv])
                        for hv in range(2):
                            ch = hv * NJ + j
                            if not sample:
                                P.op("pool", lambda e, hv=hv, ch=ch: e.tensor_copy(out=ub.up[hv][:, 0:2], in_=halo[:, ch, :]),
                                     reads=[halo], writes=[ub.hb[hv]])
                            P.op("act", lambda e, hv=hv: e.copy(out=ub.up[hv][:, 2:2 + N], in_=gv[:, hv, 0:N]),
                                 reads=[gv], writes=[ub.up[hv]])
                        for hv in range(2):
                            ch = hv * NJ + j
                            P.op("act", lambda e, hv=hv, ch=ch: e.activation(
                                out=ub.c[hv][:, 0:N], in_=gv[:, hv, 0:N], func=AF.Identity, bias=cwb[:, ch, 3:4], scale=cwb[:, ch, 2:3]),
                                reads=[gv, cwb], writes=[ub.c[hv]])
                        for hv in range(2):
                            ch = hv * NJ + j
                            if not sample:
                                P.op("pool", lambda e, hv=hv, ch=ch: e.tensor_copy(out=halo[:, ch, :], in_=ub.up[hv][:, N:N + 2]),
                                     reads=[ub.up[hv]], writes=[halo])
                            else:
                                P.op("pool", lambda e, hv=hv, ch=ch: e.tensor_copy(out=upS.at(ch), in_=ub.up[hv][:, 2:2 + N]),
                                     reads=[ub.up[hv], a_alias], writes=[upS.b])
                        for i in (1, 0):
                            for hv in range(2):
                                ch = hv * NJ + j
                                if sample:
                                    in0 = scT.at(ch).rearrange("p (b i) -> p i b", i=2)[:, i, :]
                                    rd = [scT.b, cwb, a_alias]
                                else:
                                    in0 = ub.up[hv][:, i:i + N]
                                    rd = [ub.up[hv], ub.hb[hv], cwb]
                                P.op("dve", lambda e, hv=hv, ch=ch, i=i, in0=in0: e.scalar_tensor_tensor(
                                    out=ub.c[hv][:, 0:N], in0=in0, scalar=cwb[:, ch, i:i + 1], in1=ub.c[hv][:, 0:N],
                                    op0=ALU.mult, op1=ALU.add), reads=rd, writes=[ub.c[hv]])
                    if pend is not None:
                        pj, pub = pend
                        P.op("act", lambda e: e.activation(out=pub.c[0][:, 0:N], in_=pub.c[0][:, 0:N], func=AF.Gelu_apprx_tanh),
                             writes=[pub.c[0]])
                        P.op("pool", lambda e: e.tensor_tensor(out=a_sb[:, pj, 0:N], in0=pub.c[0][:, 0:N], in1=pub.c[1][:, 0:N], op=ALU.mult),
                             reads=[pub.c[0], pub.c[1]], writes=[a_sb if sample else a_bufs[pj]])
                    pend = (j, ub) if j < NJ else None
                    yield

            ffn_bufs = [(ffn_ps.t, [ffA, ffB]), (gv_ps[1].t[:].rearrange("p a n -> p (a n)"), [gv_ps[1].b, gv_ps[1].b])]

            def ffn_down_mm(B, bi, sample, fb, a_sb=None):
                a_sb = a_sb if a_sb is not None else a_main
                M = NS if sample else 128
                fap, fbufs = fb
                for half in range(2):
                    for j in range(NJ):
                        P.noinc = (j != NJ - 1)
                        P.op("pe", lambda e, j=j, half=half: e.matmul(
                            out=fap[0:M, half * 512:(half + 1) * 512], lhsT=a_sb[:, j, bi * 128:bi * 128 + M],
                            rhs=W_dn_sb[:, j, half * 512:(half + 1) * 512], start=(j == 0), stop=(j == NJ - 1)),
                            reads=([a_sb, a_alias] if sample else [a_bufs[j]]) + [W_dn_sb], writes=[fbufs[half]])

            def ffn_down_tail(B, bi, sample, fb):
                M = NS if sample else 128
                fap, fbufs = fb
                x1 = x1_in.next()
                P.dma("sp", x1[:], x1d[B * 128:(B + 1) * 128, :], reads=[x1d_bufs[B]], writes=[x1])
                ssq = ssq2_ring.next()
                yt = ytmp_ring.next()
                for half in range(2):
                    P.op("act", lambda e, half=half: e.activation(
                        out=yt[0:M, half * 512:(half + 1) * 512], in_=fap[0:M, half * 512:(half + 1) * 512],
                        func=AF.Square, accum_out=ssq[0:M, half:half + 1]),
                        reads=[fbufs[half]], writes=[yt, ssq])
                P.op("dve", lambda e: e.tensor_tensor(out=ssq[0:M, 2:3], in0=ssq[0:M, 0:1], in1=ssq[0:M, 1:2], op=ALU.add), writes=[ssq])
                stt, rstd = rstd_from_ssq(ssq[0:M, 2:3], ssq, D, M)
                for half in range(2):
                    P.op("dve", lambda e, half=half: e.scalar_tensor_tensor(
                        out=yt[0:M, half * 512:(half + 1) * 512], in0=fap[0:M, half * 512:(half + 1) * 512], scalar=stt[0:M, 2:3],
                        in1=g_post_ffn_bc[0:M, half * 512:(half + 1) * 512], op0=ALU.mult, op1=ALU.mult),
                        reads=[fbufs[half], stt, g_post_ffn_bc], writes=[yt])
                P.op("dve", lambda e: e.tensor_tensor(out=yt[0:M, :], in0=yt[0:M, :], in1=x1[0:M, :], op=ALU.add),
                     reads=[x1], writes=[yt])
                if sample:
                    P.dma("pool", y_sample, yt[0:NS, :], reads=[yt], writes=[outb], sembuf=yt)
                else:
                    P.dma("pool", y_prompt[(B - 1) * 128:B * 128, :], yt[:], reads=[yt], writes=[outb], sembuf=yt)

            def ffn_down(B, bi, sample):
                ffn_down_mm(B, bi, sample, ffn_bufs[0])
                ffn_down_tail(B, bi, sample, ffn_bufs[0])

            P.dma("sp", cwb[:].rearrange("p c r -> p (c r)"), cw_d, reads=[csd_buf], writes=[cwb])
            ub_ring = [UpBuf("ub%d" % i, T2, s2) for i in range(2)]
            supers2 = [[0]] + [list(range(1 + i * NB2, 1 + (i + 1) * NB2)) for i in range((NBLK - 1) // NB2)]
            if DBG_NSUPER is not None:
                supers2 = supers2[:DBG_NSUPER]
            for bi, B in enumerate(supers2[0]):
                ffn_norm(B, bi)
            scT_v = a_sb.t[:, 0:6, :].bitcast(F32).rearrange("p a b -> p (a b)")[:, 0:1408].rearrange("p (c r) -> p c r", r=32)
            upS_v = a_sb.t[:, 6:9, :].bitcast(F32).rearrange("p a b -> p (a b)")[:, 0:704].rearrange("p (c r) -> p c r", r=NS)
            scT = ChAcc(lambda ch: scT_v[:, ch, :], Buf("scT"))
            upS = ChAcc(lambda ch: upS_v[:, ch, :], Buf("upS"))
            P.dma("sp", a_sb.t[:, 0:6, :].bitcast(F32).rearrange("p a b -> p (a b)")[:, 0:1408], sc_d,
                  reads=[csd_buf, a_alias], writes=[scT.b])
            P.dma("sp", conv_sample[:, 0, :], sconv.rearrange("(b i) c -> b i c", i=2)[:, 1, :], writes=[outb])
            ub_s = [UpBuf("ubs%d" % i, NS, s2) for i in range(2)]
            ffn_norm_tr(NBLK, 0, ffn_norm_pre(NBLK, 0), dst=x1nT_s, ncols=NS)
            for si, blocks in enumerate(supers2):
                if si == 0:
                    lockstep2([ffn_halo_gen(),
                               ffn_up_gen([NBLK], True, ub_s, x1nT_s, a_s, scT, upS)])
                    ffn_down_mm(NBLK, 0, True, ffn_bufs[0], a_s)
                    ffn_down_tail(NBLK, 0, True, ffn_bufs[0])
                    to_token_major(upS, NS, lambda q: conv_sample[:, 1, q * 512:(q + 1) * 512])
                else:
                    ffn_up(blocks, False, ub_ring)
                nxt = supers2[si + 1] if si + 1 < len(supers2) else []
                if blocks[0] == 0:
                    for bi, B in enumerate(nxt):
                        ffn_norm(B, bi)
                    continue
                nbk = len(blocks)
                fbs = [ffn_bufs[(nbk - 1 - bi) % 2] for bi in range(nbk)]
                ffn_down_mm(blocks[0], 0, False, fbs[0])
                for bi, B in enumerate(blocks):
                    if bi + 1 < nbk:
                        ffn_down_mm(blocks[bi + 1], bi + 1, False, fbs[bi + 1])
                    xb_n = ffn_norm_pre(nxt[bi], bi) if bi < len(nxt) else None
                    ffn_down_tail(B, bi, False, fbs[bi])
                    if xb_n is not None:
                        ffn_norm_tr(nxt[bi], bi, xb_n)
                for bi in range(nbk, len(nxt)):
                    ffn_norm(nxt[bi], bi)
            to_token_major(ChAcc(lambda jj: halo[:, jj, :], halo.b), 2, lambda q: conv_prompt[:, q * 512:(q + 1) * 512])


def _consts():
    ident = np.eye(128, dtype=np.float32)
    slopes = np.exp2(-(np.arange(1, 9, dtype=np.float32) * 1.0)).astype(np.float32)
    key = np.arange(128)[:, None, None]
    kb = np.arange(2)[None, :, None]
    q = np.arange(128)[None, None, :]
    dist = 128 + q - (kb * 128 + key)
    valid = (dist >= 0) & (dist <= 128)
    bb = np.empty((128, 8, 2, 128), np.float32)
    for h in range(8):
        bb[:, h] = np.where(valid, -slopes[h] * dist.astype(np.float32), NEG)
    bbase = bb.reshape(128, 8 * 256)
    cinv = np.ones((4, 128), np.float32)
    for g, w in enumerate(POOL_W):
        pos = np.arange(128) - 112
        cnt = np.where(pos >= 0, np.minimum(pos + 1, w), w).astype(np.float32)
        cinv[g] = 1.0 / cnt
    cinv = cinv.reshape(1, 512)
    sel = np.zeros((120, 2, 4, 16), np.float32)
    for t in range(2):
        for bl in range(8):
            for r in range(15):
                for g, w in enumerate(POOL_W):
                    if r >= 16 - w:
                        sel[bl * 15 + r, t, g, t * 8 + bl] = 1.0
    sel = sel.reshape(120, 128)
    sbias = np.zeros((128, 129), np.float32)
    for h in range(8):
        sbias[h * 16:(h + 1) * 16, 0:128] = -slopes[h] * (128 - np.arange(128, dtype=np.float32))[None, :]
    return ident, bbase, cinv, sel, sbias


_NC_CACHE = {}


def kernel(x_prompt, x_sample, cache_k, cache_v, state_pool, state_conv, meta,
           w_in, b_in, sinks, w_pool, pool_scale, g_attn_out, g_pool_out, w_o,
           g_pre_mix, g_post_mix, g_pre_ffn, g_post_ffn, w_up, conv_w, conv_b, w_down):
    f = lambda a: np.ascontiguousarray(np.asarray(a, dtype=np.float32))
    ident, bbase, cinv, sel, sbias = _consts()
    if "nc" not in _NC_CACHE:
        _NC_CACHE["nc"] = build_program()
    nc = _NC_CACHE["nc"]
    shared = {
        "meta": f(meta), "w_in": f(w_in[0]), "b_in": f(b_in[0]).reshape(1, NIN), "sinks": f(sinks[0]).reshape(1, 8),
        "w_pool": f(w_pool[0]).reshape(512, 128), "pool_scale": f(pool_scale[0]).reshape(1, 512),
        "g_attn": f(g_attn_out[0]).reshape(1, 512), "g_pool": f(g_pool_out[0]).reshape(1, 512), "w_o": f(w_o[0]),
        "g_pre_mix": f(g_pre_mix[0]).reshape(1, D), "g_post_mix": f(g_post_mix[0]).reshape(1, D),
        "g_pre_ffn": f(g_pre_ffn[0]).reshape(1, D), "g_post_ffn": f(g_post_ffn[0]).reshape(1, D),
        "w_up": f(w_up[0]), "conv_w": f(conv_w[0]), "conv_b": f(conv_b[0]).reshape(1, NUP), "w_down": f(w_down[0]),
        "c_ident": ident, "c_bbase": bbase, "c_cinv": cinv, "c_sel": sel, "c_sbias": sbias,
    }
    xpn = np.asarray(x_prompt, dtype=np.float32)
    xsn = np.asarray(x_sample, dtype=np.float32)
    ckn = np.asarray(cache_k, dtype=np.float32)
    cvn = np.asarray(cache_v, dtype=np.float32)
    spn = np.asarray(state_pool, dtype=np.float32)
    scn = np.asarray(state_conv, dtype=np.float32)
    in_maps = []
    for i in range(8):
        sl = slice(i * NS, (i + 1) * NS)
        m = dict(shared)
        m["xp"] = f(xpn[i])
        m["xs"] = f(xsn[sl, 0, :])
        m["ck"] = f(ckn[0, sl].reshape(NS, 128, 128))
        m["cv"] = f(cvn[0, sl].reshape(NS, 128, 128))
        m["spool"] = f(spn[0, sl])
        m["sconv"] = f(scn[0, sl].reshape(NS * 2, NUP))
        in_maps.append(m)
    res = run_bass_kernel_spmd(nc, in_maps, core_ids=list(range(8)))
    R = res.results
    y_prompt = np.stack([R[i]["y_prompt"] for i in range(8)], 0)
    y_sample = np.concatenate([R[i]["y_sample"] for i in range(8)], 0).reshape(128, 1, D)
    k_prompt = np.stack([R[i]["k_prompt"].reshape(128, 2, 64) for i in range(8)], 0)[None]
    v_prompt = np.stack([R[i]["v_prompt"].reshape(128, 2, 64) for i in range(8)], 0)[None]
    pool_prompt = np.stack([R[i]["pool_prompt"] for i in range(8)], 0)[None]
    conv_prompt = np.stack([R[i]["conv_prompt"] for i in range(8)], 0)[None]
    k_sample = np.concatenate([R[i]["k_sample"].reshape(NS, 128, 2, 64) for i in range(8)], 0)[None]
    v_sample = np.concatenate([R[i]["v_sample"].reshape(NS, 128, 2, 64) for i in range(8)], 0)[None]
    pool_sample = np.concatenate([R[i]["pool_sample"] for i in range(8)], 0)[None]
    conv_sample = np.concatenate([R[i]["conv_sample"] for i in range(8)], 0)[None]
    outs = (y_prompt, y_sample, k_prompt, v_prompt, pool_prompt, conv_prompt, k_sample, v_sample, pool_sample, conv_sample)
    return tuple(np.ascontiguousarray(o, dtype=np.float32) for o in outs)
```

```python
import contextlib
import numpy as np
import concourse.bass as bass
import concourse.mybir as mybir
from concourse.bass_utils import run_bass_kernel_spmd

F32 = mybir.dt.float32
BF16 = mybir.dt.bfloat16
AF = mybir.ActivationFunctionType
ALU = mybir.AluOpType
AX = mybir.AxisListType

D = 1024
NIN = 1280
DFF = 2816
NUP = 5632
NJ = 22
SEQ = 4096
NBLK = 33
NS = 16
HAL = 16
EPS = 1e-6
NEG = -30000.0
POOL_W = (2, 4, 8, 16)

SAME_ENGINE_SYNC = True
NB1 = 4
NB2 = 4
DBG_NSUPER = None
DBG_STOP = None


class _Stop(Exception):
    pass


_STOPPED = [False]


def _stop(name):
    if DBG_STOP == name:
        _STOPPED[0] = True


class Buf:
    def __init__(self, name):
        self.name = name
        self.w = None
        self.r = {}
        self.dsem = None
        self.dcount = 0


class Tile:
    def __init__(self, t, b):
        self.t = t
        self.b = b

    def __getitem__(self, idx):
        return self.t[idx]


class Ring:
    def __init__(self, tiles):
        self.tiles = tiles
        self.i = -1

    def next(self):
        self.i = (self.i + 1) % len(self.tiles)
        return self.tiles[self.i]


class Eng:
    def __init__(self, name, obj, sem):
        self.name = name
        self.obj = obj
        self.sem = sem
        self.count = 0
        self.waited = {}


class Prog:
    def __init__(self, nc, stack):
        self.nc = nc
        self.stack = stack
        self.sems = {}
        self.engs = {}
        for n, o in [("pe", nc.tensor), ("act", nc.scalar), ("dve", nc.vector),
                     ("pool", nc.gpsimd), ("sp", nc.sync)]:
            s = stack.enter_context(nc.semaphore("sem_" + n))
            self.sems["e:" + n] = s
            self.engs[n] = Eng(n, o, s)
        self.nuid = 0
        self.dcounts = {}
        self.noinc = False

    def uid(self):
        self.nuid += 1
        return self.nuid

    def sb(self, name, shape, dtype, stack=None):
        st = stack if stack is not None else self.stack
        t = st.enter_context(self.nc.sbuf_tensor("%s_%d" % (name, self.uid()), list(shape), dtype))
        return Tile(t, Buf(name))

    def ps(self, name, shape, dtype, stack=None):
        st = stack if stack is not None else self.stack
        t = st.enter_context(self.nc.psum_tensor("%s_%d" % (name, self.uid()), list(shape), dtype))
        return Tile(t, Buf(name))

    def ring(self, name, shape, dtype, n, stack=None):
        return Ring([self.sb("%s%d" % (name, i), shape, dtype, stack) for i in range(n)])

    def _dsem(self, b, queue):
        if b.dsem is None:
            b.dsem = {}
        if queue not in b.dsem:
            key = "d:%s:%s:%d" % (b.name, queue, self.uid())
            s = self.stack.enter_context(self.nc.semaphore("ds_%d" % len(self.sems)))
            self.sems[key] = s
            b.dsem[queue] = key
            self.dcounts[key] = 0
        return b.dsem[queue]

    def _wait(self, e, deps):
        for key, val in deps.items():
            if key == "e:" + e.name and not (SAME_ENGINE_SYNC and e.name in ("act", "dve", "pool")):
                continue
            if key in self.dcounts:
                val = 16 * self.dcounts[key]
            if e.waited.get(key, 0) >= val:
                continue
            e.obj.wait_ge(self.sems[key], val)
            e.waited[key] = val

    @staticmethod
    def _collect(reads, writes):
        deps = {}

        def add(d):
            if d is None:
                return
            k, v = d
            if deps.get(k, 0) < v:
                deps[k] = v
        for b in reads:
            add(b.w)
        for b in writes:
            add(b.w)
            for k, v in b.r.items():
                add((k, v))
        return deps

    @staticmethod
    def _commit(dep, reads, writes):
        k, v = dep
        for b in reads:
            if b in writes:
                continue
            if b.r.get(k, 0) < v:
                b.r[k] = v
        for b in writes:
            b.w = dep
            b.r = {}

    @staticmethod
    def _bufs(xs):
        out = []
        for x in xs:
            if isinstance(x, (list, tuple)):
                out.extend(Prog._bufs(x))
            else:
                out.append(x.b if isinstance(x, Tile) else x)
        return out

    def op(self, eng, fn, reads=(), writes=()):
        noinc = self.noinc and eng == "pe"
        self.noinc = False
        if _STOPPED[0]:
            return None
        reads = self._bufs(reads)
        writes = self._bufs(writes)
        e = self.engs[eng]
        self._wait(e, self._collect(reads, writes))
        ins = fn(e.obj)
        if noinc:
            self._commit(("e:" + eng, e.count + 1), reads, writes)
            return ins
        e.count += 1
        ins.then_inc(e.sem, 1)
        self._commit(("e:" + eng, e.count), reads, writes)
        return ins

    def dma(self, queue, out, in_, reads=(), writes=(), sembuf=None, **kw):
        if _STOPPED[0]:
            return None
        reads = self._bufs(reads)
        writes = self._bufs(writes)
        e = self.engs[queue]
        self._wait(e, self._collect(reads, writes))
        if sembuf is None:
            sembuf = (list(writes) + list(reads))[0]
        elif isinstance(sembuf, Tile):
            sembuf = sembuf.b
        key = self._dsem(sembuf, queue)
        ins = e.obj.dma_start(out=out, in_=in_, **kw)
        ins.then_inc(self.sems[key], 16)
        self.dcounts[key] += 1
        self._commit((key, 16 * self.dcounts[key]), reads, writes)
        return ins

    def barrier(self):
        if _STOPPED[0]:
            return
        targets = {}
        for n, e in self.engs.items():
            if e.count > 0:
                targets["e:" + n] = e.count
        for k, c in self.dcounts.items():
            targets[k] = 16 * c
        for n, e in self.engs.items():
            deps = {k: v for k, v in targets.items() if k != "e:" + n}
            self._wait(e, deps)

    def finish(self, eng="sp"):
        e = self.engs[eng]
        targets = {}
        for n, o in self.engs.items():
            if o.count > 0 and n != eng:
                targets["e:" + n] = o.count
        for k, c in self.dcounts.items():
            targets[k] = 16 * c
        self._wait(e, targets)


def build_program():
    _STOPPED[0] = False
    nc = bass.Bass("TRN2", target_bir_lowering=False)

    def din(name, shape):
        return nc.dram_tensor(name, list(shape), F32, kind="ExternalInput").ap()

    def dout(name, shape):
        return nc.dram_tensor(name, list(shape), F32, kind="ExternalOutput").ap()

    xp = din("xp", [SEQ, D])
    meta = din("meta", [16, D])
    xs = din("xs", [NS, D])
    ck = din("ck", [NS, 128, 128])
    cv = din("cv", [NS, 128, 128])
    spool = din("spool", [NS, 15, 512])
    sconv = din("sconv", [NS * 2, NUP])
    w_in = din("w_in", [D, NIN])
    b_in = din("b_in", [1, NIN])
    sinks = din("sinks", [1, 8])
    w_pool = din("w_pool", [512, 128])
    pool_scale = din("pool_scale", [1, 512])
    g_attn = din("g_attn", [1, 512])
    g_pool = din("g_pool", [1, 512])
    w_o = din("w_o", [D, D])
    g_pre_mix = din("g_pre_mix", [1, D])
    g_post_mix = din("g_post_mix", [1, D])
    g_pre_ffn = din("g_pre_ffn", [1, D])
    g_post_ffn = din("g_post_ffn", [1, D])
    w_up = din("w_up", [D, NUP])
    conv_w = din("conv_w", [3, NUP])
    conv_b = din("conv_b", [1, NUP])
    w_down = din("w_down", [DFF, D])
    c_ident = din("c_ident", [128, 128])
    c_bbase = din("c_bbase", [128, 8 * 256])
    c_cinv = din("c_cinv", [1, 4 * 128])
    c_sel = din("c_sel", [120, 2 * 4 * 16])
    c_sbias = din("c_sbias", [128, 129])

    y_prompt = dout("y_prompt", [SEQ, D])
    y_sample = dout("y_sample", [NS, D])
    k_prompt = dout("k_prompt", [128, 128])
    v_prompt = dout("v_prompt", [128, 128])
    pool_prompt = dout("pool_prompt", [15, 512])
    conv_prompt = dout("conv_prompt", [2, NUP])
    k_sample = dout("k_sample", [NS, 128, 128])
    v_sample = dout("v_sample", [NS, 128, 128])
    pool_sample = dout("pool_sample", [NS, 15, 512])
    conv_sample = dout("conv_sample", [NS, 2, NUP])

    x1d = nc.dram_tensor("x1_scratch", [(NBLK + 1) * 128, D], F32, kind="Internal").ap()
    x1d_bufs = [Buf("x1d%d" % i) for i in range(NBLK + 1)]
    sc_d = nc.dram_tensor("sconvT_scratch", [128, 44 * 32], F32, kind="Internal").ap()
    cw_d = nc.dram_tensor("convwT_scratch", [128, 44 * 4], F32, kind="Internal").ap()
    csd_buf = Buf("csd")
    outb = Buf("outs")

    with contextlib.ExitStack() as top:
        P = Prog(nc, top)
        try:
            _body(P, nc, locals())
        except _Stop:
            pass
        P.finish("sp")
        P.finish("pool")
    return nc


def _body(P, nc, L):
    (xp, meta, xs, ck, cv, spool, sconv, w_in, b_in, sinks, w_pool, pool_scale, g_attn, g_pool, w_o, g_pre_mix,
     g_post_mix, g_pre_ffn, g_post_ffn, w_up, conv_w, conv_b, w_down, c_ident, c_bbase, c_cinv, c_sel, c_sbias,
     y_prompt, y_sample, k_prompt, v_prompt, pool_prompt, conv_prompt, k_sample, v_sample, pool_sample, conv_sample,
     x1d, x1d_bufs, outb, sc_d, cw_d, csd_buf) = [L[k] for k in (
        "xp meta xs ck cv spool sconv w_in b_in sinks w_pool pool_scale g_attn g_pool w_o g_pre_mix "
        "g_post_mix g_pre_ffn g_post_ffn w_up conv_w conv_b w_down c_ident c_bbase c_cinv c_sel c_sbias "
        "y_prompt y_sample k_prompt v_prompt pool_prompt conv_prompt k_sample v_sample pool_sample conv_sample "
        "x1d x1d_bufs outb sc_d cw_d csd_buf").split()]
    if True:

        ident_f = P.sb("ident_f", [128, 128], F32)
        ident_b = P.sb("ident_b", [128, 128], BF16)
        eps_t = P.sb("eps", [128, 1], F32)
        P.dma("sp", ident_f[:], c_ident, writes=[ident_f])
        P.op("dve", lambda e: e.tensor_copy(out=ident_b[:], in_=ident_f[:]), reads=[ident_f], writes=[ident_b])
        P.op("dve", lambda e: e.memset(eps_t[:], EPS), writes=[eps_t])
        mhalf_t = P.sb("mhalf", [128, 1], F32)
        P.op("dve", lambda e: e.memset(mhalf_t[:], -0.5), writes=[mhalf_t])

        stat_ring = P.ring("stat", [128, 8], F32, 16)
        _stop("setup0")

        def rstd_from_ssq(ssq_ap, ssq_tile, n, M=128):
            stt = stat_ring.next()
            P.op("dve", lambda e: e.tensor_scalar(out=stt[0:M, 0:1], in0=ssq_ap, scalar1=1.0 / n, scalar2=EPS,
                                                  op0=ALU.mult, op1=ALU.add), reads=[ssq_tile], writes=[stt])
            P.op("pool", lambda e: e.tensor_tensor(out=stt[0:M, 2:3], in0=stt[0:M, 0:1], in1=mhalf_t[0:M, 0:1], op=ALU.pow),
                 reads=[stt, mhalf_t], writes=[stt])
            return stt, stt[0:M, 2:3]

        with contextlib.ExitStack() as s01:
            W_in_sb = P.sb("W_in", [128, 8, NIN], BF16, s01)
            W_kd = P.sb("W_kd", [128, 8, 2, 2, 64], BF16, s01)
            W_o_sb = P.sb("W_o", [128, 8, D], BF16, s01)
            W_pool_sb = P.sb("W_pool", [128, 4, 128], BF16, s01)
            b_in_bc = P.sb("b_in_bc", [128, NIN], F32, s01)
            b_fm = P.sb("b_fm", [128, 10], F32, s01)
            b_kd = P.sb("b_kd", [128, 2], F32, s01)
            g_pre_mix_bc = P.sb("g_pre_mix_bc", [128, D], F32, s01)
            g_mix_bc = P.sb("g_mix_bc", [128, D], F32, s01)
            g_post_mix_bc = P.sb("g_post_mix_bc", [128, D], F32, s01)
            pool_scale_bc = P.sb("pool_scale_bc", [128, 512], F32, s01)
            sink_bc = P.sb("sink_bc", [128, 8], F32, s01)
            B_hi = P.sb("B_hi", [128, 8, 256], BF16, s01)
            B_lo = P.sb("B_lo", [128, 8, 256], BF16, s01)
            cinv_bc = P.sb("cinv_bc", [128, 4, 128], F32, s01)

            P.dma("pool", W_in_sb[:], w_in.rearrange("(c p) n -> p c n", p=128), writes=[W_in_sb])
            for dup in range(2):
                for g in range(2):
                    P.dma("pool", W_kd[:, :, g, dup, :],
                          w_in[:, 512 + g * 64:512 + (g + 1) * 64].rearrange("(c p) d -> p c d", p=128), writes=[W_kd])
            P.dma("pool", W_o_sb[:], w_o.rearrange("(c p) n -> p c n", p=128), writes=[W_o_sb])
            P.dma("pool", W_pool_sb[:], w_pool.rearrange("(g c) d -> c g d", c=128), writes=[W_pool_sb])
            P.dma("sp", b_in_bc[:], b_in.partition_broadcast(128), writes=[b_in_bc])
            P.dma("sp", g_pre_mix_bc[:], g_pre_mix.partition_broadcast(128), writes=[g_pre_mix_bc])
            P.dma("sp", g_mix_bc[:, 0:512], g_attn.partition_broadcast(128), writes=[g_mix_bc])
            P.dma("sp", g_mix_bc[:, 512:1024], g_pool.partition_broadcast(128), writes=[g_mix_bc])
            P.dma("sp", g_post_mix_bc[:], g_post_mix.partition_broadcast(128), writes=[g_post_mix_bc])
            P.dma("sp", pool_scale_bc[:], pool_scale.partition_broadcast(128), writes=[pool_scale_bc])
            P.dma("sp", sink_bc[:], sinks.partition_broadcast(128), writes=[sink_bc])
            P.dma("sp", cinv_bc[:], c_cinv.rearrange("o (g t) -> o g t", g=4).partition_broadcast(128),
                  writes=[cinv_bc])
            ssq_ring = P.ring("ssq", [128, 4], F32, 12, s01)
            RG = {}

            def make_rings(stack, deep):
                RG["xb"] = P.ring("xb", [128, D], BF16, 2 if deep else 1, stack)
                RG["attn"] = P.ring("attn_sb", [128, 512], F32, 4 if deep else 1, stack)
                RG["pool_sb"] = P.ring("pool_sb", [128, 512], F32, 4 if deep else 1, stack)
                RG["mix_in"] = P.ring("mix_in", [128, D], BF16, 4 if deep else 1, stack)
                RG["mixT"] = P.ring("mixT", [128, 8, 128], BF16, 2 if deep else 1, stack)
                RG["x1"] = P.ring("x1", [128, D], F32, 2 if deep else 1, stack)

            class View:
                def __init__(self, ap, bufs):
                    self.t = ap
                    self.bufs = bufs

                def __getitem__(self, idx):
                    return self.t[idx]

            Q = [P.ps("Q%d" % i, [128, 1024], F32, s01) for i in range(4)]
            Hb = [Buf("H%d" % i) for i in range(8)]

            def half_ap(i):
                return Q[i // 2].t[:, (i % 2) * 512:(i % 2 + 1) * 512]

            tr_ring = Ring([View(half_ap(i).bitcast(BF16).rearrange("p (c t) -> p c t", c=8), [Hb[i]]) for i in (0, 1)])
            trA_ring = Ring([View(half_ap(i).bitcast(BF16).rearrange("p (c t) -> p c t", c=8), [Hb[i]]) for i in (6, 7)])
            mm_slots = [(half_ap(i), Hb[i]) for i in (2, 3, 4, 5, 6, 7)]
            mm_i = [0]
            sc_ring = Ring([View(Q[k].t[:].rearrange("p (j k q) -> p j k q", j=4, k=2), [Hb[2 * k], Hb[2 * k + 1]]) for k in (1, 2)])
            o_ring = Ring([View(half_ap(i).rearrange("p (j d) -> p j d", j=4), [Hb[i]]) for i in (6, 7)])
            wo_ring = Ring([View(Q[k].t[:], [Hb[2 * k], Hb[2 * k + 1]]) for k in (1, 2)])

            def next_mm():
                mm_i[0] = (mm_i[0] + 1) % len(mm_slots)
                return mm_slots[mm_i[0]]

            with contextlib.ExitStack() as sb0:
                brow = P.sb("brow", [1, NIN + 256], F32, sb0)
                P.dma("sp", brow[0:1, 0:NIN], b_in, writes=[brow])
                for g in range(2):
                    for dup in range(2):
                        c0 = NIN + g * 128 + dup * 64
                        P.dma("sp", brow[0:1, c0:c0 + 64], b_in[:, 512 + g * 64:512 + (g + 1) * 64], writes=[brow])
                bap, bbuf = mm_slots[0]
                for c in range(12):
                    P.op("pe", lambda e, c=c: e.transpose(out=bap[:, c:c + 1], in_=brow[0:1, c * 128:(c + 1) * 128],
                                                          identity=ident_f[0:1, 0:1]),
                         reads=[brow, ident_f], writes=[bbuf])
                P.op("dve", lambda e: e.tensor_copy(out=b_fm[:], in_=bap[:, 0:10]), reads=[bbuf], writes=[b_fm])
                P.op("dve", lambda e: e.tensor_scalar(out=b_fm[:, 0:4], in0=b_fm[:, 0:4], scalar1=0.125, scalar2=None, op0=ALU.mult),
                     writes=[b_fm])
                Bfull = P.sb("Bfull", [128, 8, 256], F32, sb0)
                P.dma("sp", Bfull[:], c_bbase.rearrange("p (h k) -> p h k", h=8), writes=[Bfull])
                for h in range(8):
                    P.op("dve", lambda e, h=h: e.tensor_scalar(out=Bfull[:, h, :], in0=Bfull[:, h, :],
                                                               scalar1=sink_bc[:, h:h + 1], scalar2=None, op0=ALU.subtract),
                         reads=[sink_bc], writes=[Bfull])
                P.op("dve", lambda e: e.tensor_copy(out=B_hi[:], in_=Bfull[:]), reads=[Bfull], writes=[B_hi])
                P.op("dve", lambda e: e.tensor_tensor(out=B_lo[:], in0=Bfull[:], in1=B_hi[:], op=ALU.subtract),
                     reads=[Bfull, B_hi], writes=[B_lo])
                P.op("dve", lambda e: e.tensor_copy(out=b_kd[:], in_=bap[:, 10:12]), reads=[bbuf], writes=[b_kd])
                P.barrier()

            def run(gen):
                for _ in gen:
                    pass

            def lockstep(gens):
                gens = list(gens)
                while gens:
                    for g_ in list(gens):
                        try:
                            next(g_)
                        except StopIteration:
                            gens.remove(g_)

            def norm_T(x_ap, x_tile, g_bc, dstT, col0):
                run(norm_T_gen(x_ap, x_tile, g_bc, dstT, col0))

            def norm_T_gen(x_ap, x_tile, g_bc, dstT, col0, trr=None):
                ssq = ssq_ring.next()
                xb = RG["xb"].next()
                tr = (trr if trr is not None else tr_ring).next()
                P.op("act", lambda e: e.activation(out=xb[:], in_=x_ap, func=AF.Square, accum_out=ssq[:, 0:1]),
                     reads=[x_tile], writes=[xb, ssq])
                yield
                stt, rstd = rstd_from_ssq(ssq[:, 0:1], ssq, D)
                yield
                P.op("dve", lambda e: e.scalar_tensor_tensor(out=xb[:], in0=x_ap, scalar=rstd, in1=g_bc[:],
                                                             op0=ALU.mult, op1=ALU.mult),
                     reads=[x_tile, stt, g_bc], writes=[xb])
                yield
                for c in range(8):
                    P.noinc = (c != 7)
                    P.op("pe", lambda e, c=c: e.transpose(out=tr[:, c, :], in_=xb[:, c * 128:(c + 1) * 128],
                                                          identity=ident_b[:]),
                         reads=[xb, ident_b], writes=tr.bufs)
                yield
                P.op("act", lambda e: e.copy(out=dstT[:, :, col0:col0 + 128], in_=tr[:]),
                     reads=tr.bufs, writes=[dstT])

            def tail_F_gen(attn, pool_mm_fn, out):
                pool_sb = RG["pool_sb"].next()
                mix_in = RG["mix_in"].next()
                ssq = ssq_ring.next()
                out["mix"] = mix_in
                pool_ps_ap, pool_ps_buf = pool_mm_fn()
                P.op("dve", lambda e: e.tensor_tensor(out=pool_sb[:], in0=pool_ps_ap, in1=pool_scale_bc[:], op=ALU.mult),
                     reads=[pool_ps_buf, pool_scale_bc], writes=[pool_sb])
                yield
                P.op("act", lambda e: e.activation(out=mix_in[:, 0:512], in_=attn[:], func=AF.Square, accum_out=ssq[:, 0:1]),
                     reads=[attn], writes=[mix_in, ssq])
                yield
                P.op("act", lambda e: e.activation(out=mix_in[:, 512:1024], in_=pool_sb[:], func=AF.Square, accum_out=ssq[:, 1:2]),
                     reads=[pool_sb], writes=[mix_in, ssq])
                st_a, r_a = rstd_from_ssq(ssq[:, 0:1], ssq, 512)
                yield
                P.op("dve", lambda e: e.scalar_tensor_tensor(out=mix_in[:, 0:512], in0=attn[:], scalar=r_a,
                                                             in1=g_mix_bc[:, 0:512], op0=ALU.mult, op1=ALU.mult),
                     reads=[attn, st_a, g_mix_bc], writes=[mix_in])
                st_p, r_p = rstd_from_ssq(ssq[:, 1:2], ssq, 512)
                yield
                P.op("dve", lambda e: e.scalar_tensor_tensor(out=mix_in[:, 512:1024], in0=pool_sb[:], scalar=r_p,
                                                             in1=g_mix_bc[:, 512:1024], op0=ALU.mult, op1=ALU.mult),
                     reads=[pool_sb, st_p, g_mix_bc], writes=[mix_in])

            def tail_G_gen(mix_in, x_ap, x_tile, x1row, x1buf):
                tr = tr_ring.next()
                mixT = RG["mixT"].next()
                wo = wo_ring.next()
                ssq2 = ssq_ring.next()
                x1 = RG["x1"].next()
                for c in range(8):
                    P.noinc = (c != 7)
                    P.op("pe", lambda e, c=c: e.transpose(out=tr[:, c, :], in_=mix_in[:, c * 128:(c + 1) * 128],
                                                          identity=ident_b[:]),
                         reads=[mix_in, ident_b], writes=tr.bufs)
                yield
                P.op("act", lambda e: e.copy(out=mixT[:], in_=tr[:]), reads=tr.bufs, writes=[mixT])
                yield
                for half in range(2):
                    for c in range(8):
                        P.noinc = (c != 7)
                        P.op("pe", lambda e, c=c, half=half: e.matmul(
                            out=wo[:, half * 512:(half + 1) * 512], lhsT=mixT[:, c, :],
                            rhs=W_o_sb[:, c, half * 512:(half + 1) * 512], start=(c == 0), stop=(c == 7)),
                            reads=[mixT, W_o_sb], writes=[wo.bufs[half]])
                    yield
                for half in range(2):
                    P.op("act", lambda e, half=half: e.activation(
                        out=x1[:, half * 512:(half + 1) * 512], in_=wo[:, half * 512:(half + 1) * 512],
                        func=AF.Square, accum_out=ssq2[:, half:half + 1]),
                        reads=[wo.bufs[half]], writes=[x1, ssq2])
                    yield
                P.op("dve", lambda e: e.tensor_tensor(out=ssq2[:, 2:3], in0=ssq2[:, 0:1], in1=ssq2[:, 1:2], op=ALU.add),
                     reads=[ssq2], writes=[ssq2])
                yield
                st_m, r_m = rstd_from_ssq(ssq2[:, 2:3], ssq2, D)
                yield
                for half in range(2):
                    P.op("dve", lambda e, half=half: e.scalar_tensor_tensor(
                        out=x1[:, half * 512:(half + 1) * 512], in0=wo[:, half * 512:(half + 1) * 512], scalar=r_m,
                        in1=g_post_mix_bc[:, half * 512:(half + 1) * 512], op0=ALU.mult, op1=ALU.mult),
                        reads=[wo.bufs[half], st_m, g_post_mix_bc], writes=[x1])
                    yield
                P.op("dve", lambda e: e.tensor_tensor(out=x1[:], in0=x1[:], in1=x_ap, op=ALU.add),
                     reads=[x_tile], writes=[x1])
                yield
                P.dma("sp", x1d[x1row:x1row + 128, :], x1[:], reads=[x1], writes=[x1buf], sembuf=x1)

            def mix_tail(x_ap, x_tile, attn, pool_ps_ap, pool_ps_buf, x1row, x1buf):
                out = {}
                run(tail_F_gen(attn, lambda: (pool_ps_ap, pool_ps_buf), out))
                run(tail_G_gen(out["mix"], x_ap, x_tile, x1row, x1buf))

            _stop("setup1")
            with contextlib.ExitStack() as s0:
                x_s = P.sb("x_s", [128, D], F32, s0)
                xT_s = P.sb("xT_s", [128, 8, 128], BF16, s0)
                z_s = P.sb("z_s", [128, NIN], F32, s0)
                q_hb = P.sb("q_hb", [128, 64], F32, s0)
                kn_hb = P.sb("kn_hb", [128, 64], F32, s0)
                vn_hb = P.sb("vn_hb", [128, 64], F32, s0)
                sink_hb = P.sb("sink_hb", [128, 1], F32, s0)
                sbias = P.sb("sbias", [128, 129], F32, s0)
                make_rings(s0, False)
                Kc = P.sb("Kc", [128, 128, 64], F32, s0)
                Vc = P.sb("Vc", [128, 128, 64], F32, s0)
                Kb = [Buf("Kc%d" % h) for h in range(8)]
                Vb = [Buf("Vc%d" % h) for h in range(8)]
                prod = P.sb("prod", [128, 128, 64], F32, s0)
                Sall = P.sb("Sall", [128, 129], F32, s0)
                Pm = P.sb("Pm", [128, 129], F32, s0)
                sm = P.sb("sm", [128, 8], F32, s0)
                o_hb = P.sb("o_hb", [128, 64], F32, s0)
                attn_s = P.sb("attn_s", [128, 512], F32, s0)
                spl = P.sb("spl", [128, 2, 512], F32, s0)
                sel = P.sb("sel", [128, 2, 4, 16], F32, s0)
                wsum = P.sb("wsum", [128, 512], F32, s0)
                d_s = P.sb("d_s", [128, 512], BF16, s0)
                dT_s = P.sb("dT_s", [128, 4, 128], BF16, s0)

                P.op("pool", lambda e: e.memset(x_s[:], 0.0), writes=[x_s])
                P.dma("sp", x_s[0:NS, :], xs, writes=[x_s])
                cstage = prod.t[0:36, 0:88, :].rearrange("p a b -> p (a b)")
                rs_sc = prod.t[:, 96:118, :].rearrange("p a b -> p (a b)").rearrange("p (c r) -> p c r", r=32)
                rs_cw = prod.t[:, 118:121, :].rearrange("p a b -> p (a b)")[:, 0:176].rearrange("p (c r) -> p c r", r=4)
                P.dma("sp", cstage[0:32, :], sconv, writes=[prod])
                P.dma("sp", cstage[32:35, :], conv_w, writes=[prod])
                P.dma("sp", cstage[35:36, :], conv_b, writes=[prod])
                for g0 in range(0, 44, 14):
                    n = min(14, 44 - g0)
                    bap_, bbuf_ = next_mm()
                    for k in range(n):
                        P.op("pe", lambda e, k=k: e.transpose(out=bap_[:, k * 36:(k + 1) * 36], in_=cstage[0:36, (g0 + k) * 128:(g0 + k + 1) * 128],
                                                              identity=ident_f[0:36, 0:36]),
                             reads=[prod, ident_f], writes=[bbuf_])
                    pv = bap_[:, 0:n * 36].rearrange("p (c r) -> p c r", r=36)
                    P.op("dve", lambda e: e.tensor_copy(out=rs_sc[:, g0:g0 + n, :], in_=pv[:, :, 0:32]), reads=[bbuf_], writes=[prod])
                    P.op("dve", lambda e: e.tensor_copy(out=rs_cw[:, g0:g0 + n, :], in_=pv[:, :, 32:36]), reads=[bbuf_], writes=[prod])
                P.dma("sp", sc_d, prod.t[:, 96:118, :].rearrange("p a b -> p (a b)"), reads=[prod], writes=[csd_buf], sembuf=prod)
                P.dma("sp", cw_d, prod.t[:, 118:121, :].rearrange("p a b -> p (a b)")[:, 0:176], reads=[prod], writes=[csd_buf], sembuf=prod)
                P.dma("sp", sbias[:], c_sbias, writes=[sbias])
                P.dma("sp", sel[0:120].rearrange("p t g b -> p (t g b)"), c_sel, writes=[sel])
                for t in range(2):
                    P.dma("sp", spl[0:120, t, :], spool[t * 8:(t + 1) * 8].rearrange("b r c -> (b r) c"), writes=[spl])
                def load_cache(dst, bufs, src):
                    for g in range(2):
                        h0 = 4 * g
                        P.dma("act", dst[h0 * 16:(h0 + 1) * 16], src[:, :, g * 64:(g + 1) * 64], writes=[bufs[h0]])
                    for g in range(2):
                        h0 = 4 * g
                        for j in range(1, 4):
                            P.dma("sp", dst[(h0 + j) * 16:(h0 + j + 1) * 16], dst[h0 * 16:(h0 + 1) * 16],
                                  reads=[bufs[h0]], writes=[bufs[h0 + j]])
                load_cache(Kc, Kb, ck)
                load_cache(Vc, Vb, cv)
                for h in range(8):
                    P.dma("sp", sink_hb[h * 16:(h + 1) * 16, :], sinks[:, h:h + 1].partition_broadcast(16), writes=[sink_hb])
                P.dma("sp", k_sample[:, 0:127, :], ck[:, 1:128, :], writes=[outb])
                P.dma("sp", v_sample[:, 0:127, :], cv[:, 1:128, :], writes=[outb])
                P.dma("sp", pool_sample[:, 0:14, :], spool[:, 1:15, :], writes=[outb])

                norm_T(x_s[:], x_s, g_pre_mix_bc, xT_s, 0)
                for (n0, n1) in ((0, 512), (512, 1024), (1024, NIN)):
                    ap, b = next_mm()
                    for c in range(8):
                        P.op("pe", lambda e, c=c, ap=ap, n0=n0, n1=n1: e.matmul(
                            out=ap[:, 0:n1 - n0], lhsT=xT_s[:, c, :], rhs=W_in_sb[:, c, n0:n1],
                            start=(c == 0), stop=(c == 7)), reads=[xT_s, W_in_sb], writes=[b])
                    P.op("dve", lambda e, ap=ap, n0=n0, n1=n1: e.tensor_tensor(
                        out=z_s[:, n0:n1], in0=ap[:, 0:n1 - n0], in1=b_in_bc[:, n0:n1], op=ALU.add),
                        reads=[b, b_in_bc], writes=[z_s])
                P.dma("pool", k_sample[:, 127, :], z_s[0:NS, 512:640], reads=[z_s], writes=[outb], sembuf=z_s)
                P.dma("pool", v_sample[:, 127, :], z_s[0:NS, 640:768], reads=[z_s], writes=[outb], sembuf=z_s)
                P.dma("pool", pool_sample[:, 14, :], z_s[0:NS, 768:1280], reads=[z_s], writes=[outb], sembuf=z_s)
                _stop("p0a")
                for h in range(8):
                    g = h // 4
                    P.dma("sp", q_hb[h * 16:(h + 1) * 16, :], z_s[0:NS, h * 64:(h + 1) * 64], reads=[z_s], writes=[q_hb])
                    P.dma("sp", kn_hb[h * 16:(h + 1) * 16, :], z_s[0:NS, 512 + g * 64:512 + (g + 1) * 64], reads=[z_s], writes=[kn_hb])
                    P.dma("sp", vn_hb[h * 16:(h + 1) * 16, :], z_s[0:NS, 640 + g * 64:640 + (g + 1) * 64], reads=[z_s], writes=[vn_hb])
                P.op("dve", lambda e: e.tensor_tensor(out=prod[:], in0=Kc[:], in1=q_hb[:].unsqueeze(1).to_broadcast([128, 128, 64]),
                                                      op=ALU.mult), reads=Kb + [q_hb], writes=[prod])
                P.op("dve", lambda e: e.tensor_reduce(out=Sall[:, 0:128], in_=prod[:], axis=AX.X, op=ALU.add),
                     reads=[prod], writes=[Sall])
                P.op("dve", lambda e: e.tensor_tensor(out=o_hb[:], in0=kn_hb[:], in1=q_hb[:], op=ALU.mult),
                     reads=[kn_hb, q_hb], writes=[o_hb])
                P.op("dve", lambda e: e.tensor_reduce(out=Sall[:, 128:129], in_=o_hb[:], axis=AX.X, op=ALU.add),
                     reads=[o_hb], writes=[Sall])
                P.op("dve", lambda e: e.scalar_tensor_tensor(out=Sall[:], in0=Sall[:], scalar=0.125, in1=sbias[:],
                                                             op0=ALU.mult, op1=ALU.add), reads=[sbias], writes=[Sall])
                P.op("dve", lambda e: e.tensor_reduce(out=sm[:, 0:1], in_=Sall[:], axis=AX.X, op=ALU.max),
                     reads=[Sall], writes=[sm])
                P.op("dve", lambda e: e.tensor_tensor(out=sm[:, 1:2], in0=sm[:, 0:1], in1=sink_hb[:], op=ALU.max),
                     reads=[sink_hb], writes=[sm])
                P.op("dve", lambda e: e.tensor_scalar(out=sm[:, 2:3], in0=sm[:, 1:2], scalar1=-1.0, scalar2=None, op0=ALU.mult),
                     writes=[sm])
                P.op("act", lambda e: e.activation(out=Pm[:], in_=Sall[:], func=AF.Exp, bias=sm[:, 2:3], scale=1.0,
                                                   accum_out=sm[:, 3:4]), reads=[Sall, sm], writes=[Pm, sm])
                P.op("act", lambda e: e.activation(out=sm[:, 4:5], in_=sink_hb[:], func=AF.Exp, bias=sm[:, 2:3], scale=1.0),
                     reads=[sink_hb], writes=[sm])
                P.op("dve", lambda e: e.tensor_tensor(out=sm[:, 5:6], in0=sm[:, 3:4], in1=sm[:, 4:5], op=ALU.add), writes=[sm])
                P.op("dve", lambda e: e.reciprocal(out=sm[:, 6:7], in_=sm[:, 5:6]), writes=[sm])
                P.op("dve", lambda e: e.tensor_tensor(out=prod[:], in0=Vc[:],
                                                      in1=Pm[:, 0:128].unsqueeze(2).to_broadcast([128, 128, 64]),
                                                      op=ALU.mult), reads=Vb + [Pm], writes=[prod])
                P.op("dve", lambda e: e.tensor_reduce(out=o_hb[:], in_=prod[:].rearrange("p k d -> p d k"), axis=AX.X,
                                                      op=ALU.add), reads=[prod], writes=[o_hb])
                P.op("dve", lambda e: e.scalar_tensor_tensor(out=o_hb[:], in0=vn_hb[:], scalar=Pm[:, 128:129], in1=o_hb[:],
                                                             op0=ALU.mult, op1=ALU.add), reads=[vn_hb, Pm], writes=[o_hb])
                P.op("dve", lambda e: e.tensor_scalar(out=o_hb[:], in0=o_hb[:], scalar1=sm[:, 6:7], scalar2=None, op0=ALU.mult),
                     reads=[sm], writes=[o_hb])
                P.op("pool", lambda e: e.memset(attn_s[:], 0.0), writes=[attn_s])
                for h in range(8):
                    P.dma("sp", attn_s[0:NS, h * 64:(h + 1) * 64], o_hb[h * 16:(h + 1) * 16, :], reads=[o_hb], writes=[attn_s])
                _stop("p0b")
                ap, b = next_mm()
                for g in range(4):
                    for t in range(2):
                        P.op("pe", lambda e, g=g, t=t, ap=ap: e.matmul(
                            out=ap[0:NS, g * 128:(g + 1) * 128], lhsT=sel[0:120, t, g, :],
                            rhs=spl[0:120, t, g * 128:(g + 1) * 128], start=(t == 0), stop=(t == 1)),
                            reads=[sel, spl], writes=[b])
                P.op("pool", lambda e: e.memset(d_s[:], 0.0), writes=[d_s])
                P.op("dve", lambda e, ap=ap: e.tensor_tensor(out=wsum[0:NS, :], in0=ap[0:NS, :], in1=z_s[0:NS, 768:1280], op=ALU.add),
                     reads=[b, z_s], writes=[wsum])
                for g in range(4):
                    P.op("dve", lambda e, g=g: e.scalar_tensor_tensor(
                        out=d_s[0:NS, g * 128:(g + 1) * 128], in0=wsum[0:NS, g * 128:(g + 1) * 128],
                        scalar=1.0 / POOL_W[g], in1=z_s[0:NS, 768 + g * 128:768 + (g + 1) * 128],
                        op0=ALU.mult, op1=ALU.subtract), reads=[wsum, z_s], writes=[d_s])
                trs = tr_ring.next()
                for g in range(4):
                    P.op("pe", lambda e, g=g: e.transpose(out=trs[:, g, :], in_=d_s[:, g * 128:(g + 1) * 128], identity=ident_b[:]),
                         reads=[d_s, ident_b], writes=trs.bufs)
                P.op("dve", lambda e: e.tensor_copy(out=dT_s[:], in_=trs[:, 0:4, :]), reads=trs.bufs, writes=[dT_s])
                pap, pb = next_mm()
                for g in range(4):
                    P.op("pe", lambda e, g=g, pap=pap: e.matmul(out=pap[:, g * 128:(g + 1) * 128], lhsT=dT_s[:, g, :],
                                                                  rhs=W_pool_sb[:, g, :], start=True, stop=True),
                         reads=[dT_s, W_pool_sb], writes=[pb])
                mix_tail(x_s[:], x_s, attn_s, pap, pb, NBLK * 128, x1d_bufs[NBLK])
                P.barrier()
                _stop("p0")

            with contextlib.ExitStack() as s1:
                T = NB1 * 128
                make_rings(s1, True)
                x_ring = P.ring("x_tm", [128, NB1, D], F32, 2, s1)
                xT = P.sb("xT", [128, 8, T], BF16, s1)
                qT = P.sb("qT", [128, 4, T], BF16, s1)
                kT2 = P.sb("kT2", [128, 2, 2, 128 + T], BF16, s1)
                uT = P.sb("uT", [128, 4, HAL + T], F32, s1)
                pA = P.sb("pA", [128, 4, HAL + T], F32, s1)
                pB = P.sb("pB", [128, 3, HAL + T], F32, s1)
                dd = P.sb("dd", [128, 4, T], BF16, s1)
                kv_ring = P.ring("kv_sb", [128, 256], F32, 2, s1)
                v_aug = P.sb("v_aug", [128, NB1 + 1, 2, 66], BF16, s1)
                PT_ring = P.ring("PT", [128, 4, 2, 128], BF16, 2, s1)
                PT_half = {id(t): [Buf("PTa"), Buf("PTb")] for t in PT_ring.tiles}
                den_ring = P.ring("den", [128, 8], F32, 4, s1)

                P.op("pool", lambda e: e.memset(kT2[:], 0.0), writes=[kT2])
                P.op("pool", lambda e: e.memset(uT[:], 0.0), writes=[uT])
                P.op("pool", lambda e: e.memset(v_aug[:], 0.0), writes=[v_aug])

                supers = [[0]] + [list(range(1 + i * NB1, 1 + (i + 1) * NB1)) for i in range((NBLK - 1) // NB1)]
                if DBG_NSUPER is not None:
                    supers = supers[:DBG_NSUPER]
                xts = {}

                def load_x(si):
                    xt = x_ring.next()
                    for bi, B in enumerate(supers[si]):
                        if B == 0:
                            P.op("pool", lambda e: e.memset(xt[:, 0, :], 0.0), writes=[xt])
                            P.dma("sp", xt[112:128, 0, :], meta, writes=[xt])
                        else:
                            P.dma("sp", xt[:, bi, :], xp[(B - 1) * 128:B * 128, :], writes=[xt])
                    xts[si] = xt

                def stage_B(si, prev_Tn):
                    blocks = supers[si]
                    Tn = len(blocks) * 128
                    if prev_Tn is not None:
                        P.op("pool", lambda e: e.tensor_copy(out=kT2[:, :, :, 0:128], in_=kT2[:, :, :, prev_Tn:prev_Tn + 128]), writes=[kT2])
                        P.op("pool", lambda e: e.tensor_copy(out=uT[:, :, 0:HAL], in_=uT[:, :, prev_Tn:prev_Tn + HAL]), writes=[uT])
                    for c_out in range(4):
                        ap, hb = next_mm()
                        for c in range(8):
                            P.noinc = (c != 7)
                            P.op("pe", lambda e, c=c: e.matmul(
                                out=ap[:, 0:Tn], lhsT=W_in_sb[:, c, c_out * 128:(c_out + 1) * 128], rhs=xT[:, c, 0:Tn],
                                start=(c == 0), stop=(c == 7)), reads=[W_in_sb, xT], writes=[hb])
                        P.op("act", lambda e: e.activation(
                            out=qT[:, c_out, 0:Tn], in_=ap[:, 0:Tn], func=AF.Identity, bias=b_fm[:, c_out:c_out + 1], scale=0.125),
                            reads=[hb, b_fm], writes=[qT])
                    for g in range(2):
                        ap, hb = next_mm()
                        for c in range(8):
                            P.noinc = (c != 7)
                            P.op("pe", lambda e, c=c: e.matmul(
                                out=ap[:, 0:Tn], lhsT=W_kd[:, c, g, :, :].rearrange("p a d -> p (a d)"), rhs=xT[:, c, 0:Tn],
                                start=(c == 0), stop=(c == 7)), reads=[W_kd, xT], writes=[hb])
                        for half in range(2):
                            hs = slice(half * 64, (half + 1) * 64)
                            P.op("act", lambda e, half=half, hs=hs: e.activation(
                                out=kT2[hs, g, half, 128:128 + Tn], in_=ap[hs, 0:Tn], func=AF.Identity, bias=b_kd[hs, g:g + 1], scale=1.0),
                                reads=[hb, b_kd], writes=[kT2])
                    for g in range(4):
                        ap, hb = next_mm()
                        for c in range(8):
                            P.noinc = (c != 7)
                            P.op("pe", lambda e, c=c: e.matmul(
                                out=ap[:, 0:Tn], lhsT=W_in_sb[:, c, 768 + g * 128:768 + (g + 1) * 128], rhs=xT[:, c, 0:Tn],
                                start=(c == 0), stop=(c == 7)), reads=[W_in_sb, xT], writes=[hb])
                        P.op("act", lambda e: e.activation(
                            out=uT[:, g, HAL:HAL + Tn], in_=ap[:, 0:Tn], func=AF.Identity, bias=b_fm[:, 6 + g:7 + g], scale=1.0),
                            reads=[hb, b_fm], writes=[uT])
                    if blocks[0] == 0:
                        P.op("pool", lambda e: e.memset(uT[:, :, HAL:HAL + 112], 0.0), writes=[uT])

                def stage_C_gen(si):
                    blocks = supers[si]
                    Tn = len(blocks) * 128
                    Wd = HAL + Tn
                    P.op("pool", lambda e: e.tensor_tensor(out=pA[:, :, 1:Wd], in0=uT[:, :, 1:Wd], in1=uT[:, :, 0:Wd - 1], op=ALU.add),
                         reads=[uT], writes=[pA])
                    yield
                    P.op("pool", lambda e: e.tensor_tensor(out=pB[:, :, 3:Wd], in0=pA[:, 1:4, 3:Wd], in1=pA[:, 1:4, 1:Wd - 2], op=ALU.add),
                         reads=[pA], writes=[pB])
                    yield

                    def emit_d(g, src, idx):
                        w = POOL_W[g]
                        if blocks[0] == 0:
                            tmpu = RG["pool_sb"].next()
                            P.op("dve", lambda e: e.tensor_tensor(out=tmpu[:, 0:128], in0=src[:, idx, HAL:HAL + 128],
                                                                  in1=cinv_bc[:, g, :], op=ALU.mult),
                                 reads=[src, cinv_bc], writes=[tmpu])
                            P.op("dve", lambda e: e.tensor_tensor(out=dd[:, g, 0:128], in0=tmpu[:, 0:128],
                                                                  in1=uT[:, g, HAL:HAL + 128], op=ALU.subtract),
                                 reads=[tmpu, uT], writes=[dd])
                        else:
                            P.op("dve", lambda e: e.scalar_tensor_tensor(
                                out=dd[:, g, 0:Tn], in0=src[:, idx, HAL:HAL + Tn], scalar=1.0 / w, in1=uT[:, g, HAL:HAL + Tn],
                                op0=ALU.mult, op1=ALU.subtract), reads=[src, uT], writes=[dd])
                    emit_d(0, pA, 0)
                    yield
                    emit_d(1, pB, 0)
                    yield
                    P.op("pool", lambda e: e.tensor_tensor(out=pA[:, 0:2, 7:Wd], in0=pB[:, 1:3, 7:Wd], in1=pB[:, 1:3, 3:Wd - 4], op=ALU.add),
                         reads=[pB], writes=[pA])
                    yield
                    emit_d(2, pA, 0)
                    yield
                    P.op("pool", lambda e: e.tensor_tensor(out=pB[:, 0:1, 15:Wd], in0=pA[:, 1:2, 15:Wd], in1=pA[:, 1:2, 7:Wd - 8], op=ALU.add),
                         reads=[pA], writes=[pB])
                    yield
                    emit_d(3, pB, 0)
                    yield
                    if blocks[-1] == NBLK - 1:
                        ap, hb = half_ap(0), Hb[0]
                        for g in range(4):
                            P.op("pe", lambda e, g=g: e.transpose(
                                out=ap[0:15, g * 128:(g + 1) * 128], in_=uT[:, g, HAL + Tn - 15:HAL + Tn], identity=ident_f[:]),
                                reads=[uT, ident_f], writes=[hb])
                        tmpu = RG["pool_sb"].next()
                        P.op("dve", lambda e: e.tensor_copy(out=tmpu[0:15, :], in_=ap[0:15, :]), reads=[hb], writes=[tmpu])
                        P.dma("pool", pool_prompt, tmpu[0:15, :], reads=[tmpu], writes=[outb], sembuf=tmpu)

                def stage_D(si, prev_nb):
                    blocks = supers[si]
                    if prev_nb is not None:
                        P.op("pool", lambda e: e.tensor_copy(out=v_aug[:, 0], in_=v_aug[:, prev_nb]), writes=[v_aug])
                    for bi, B in enumerate(blocks):
                        ap, hb = next_mm()
                        for c in range(8):
                            P.noinc = (c != 7)
                            P.op("pe", lambda e, c=c: e.matmul(
                                out=ap[:, 0:256], lhsT=xT[:, c, bi * 128:(bi + 1) * 128], rhs=W_in_sb[:, c, 512:768],
                                start=(c == 0), stop=(c == 7)), reads=[xT, W_in_sb], writes=[hb])
                        kv_sb = kv_ring.next()
                        P.op("dve", lambda e: e.tensor_tensor(out=kv_sb[:], in0=ap[:, 0:256], in1=b_in_bc[:, 512:768], op=ALU.add),
                             reads=[hb, b_in_bc], writes=[kv_sb])
                        P.op("pool", lambda e: e.tensor_copy(
                            out=v_aug[:, bi + 1, :, 0:64], in_=kv_sb[:, 128:256].rearrange("p (g d) -> p g d", g=2)),
                            reads=[kv_sb], writes=[v_aug])
                        P.op("pool", lambda e: e.memset(v_aug[:, bi + 1, :, 64:65], 1.0), writes=[v_aug])
                        if B == 0:
                            P.op("pool", lambda e: e.memset(v_aug[0:112, bi + 1, :, :], 0.0), writes=[v_aug])
                        if B == NBLK - 1:
                            P.dma("pool", k_prompt, kv_sb[:, 0:128], reads=[kv_sb], writes=[outb], sembuf=kv_sb)
                            P.dma("pool", v_prompt, kv_sb[:, 128:256], reads=[kv_sb], writes=[outb], sembuf=kv_sb)

                def step(gens):
                    for g_ in list(gens):
                        try:
                            next(g_)
                        except StopIteration:
                            gens.remove(g_)

                def stage_E(si, bg):
                    blocks = supers[si]
                    nb = len(blocks)
                    units = [(bi, g) for bi in range(nb) for g in range(2)]
                    attn_tiles = [RG["attn"].next() for _ in blocks]
                    outs = [dict() for _ in blocks]
                    scv = {}
                    fgens = []
                    pending_F = []

                    def QK(u):
                        bi, g = units[u]
                        sc = sc_ring.next()
                        scv[u] = sc
                        for hf in range(2):
                            h0 = 4 * g + 2 * hf
                            for Bt, first in ((B_hi, True), (B_lo, False)):
                                P.noinc = True
                                P.op("pe", lambda e, hf=hf, h0=h0, Bt=Bt, first=first: e.matmul(
                                    out=sc[:, 2 * hf:2 * hf + 2, :, :].rearrange("p j k q -> p (j k q)"), lhsT=ident_b[:],
                                    rhs=Bt[:, h0:h0 + 2, :].rearrange("p h k -> p (h k)"),
                                    start=first, stop=False), reads=[Bt, ident_b], writes=[sc.bufs[hf]])
                            for j in (2 * hf, 2 * hf + 1):
                                h = 4 * g + j
                                cq, half = h // 2, h % 2
                                for kb in range(2):
                                    k0 = (bi + kb) * 128
                                    last = (j == 2 * hf + 1 and kb == 1)
                                    P.noinc = (not last)
                                    P.op("pe", lambda e, j=j, kb=kb, k0=k0, cq=cq, half=half, last=last: e.matmul(
                                        out=sc[:, j, kb, :], lhsT=kT2[:, g, half, k0:k0 + 128],
                                        rhs=qT[:, cq, bi * 128:(bi + 1) * 128], start=False, stop=last),
                                        reads=[kT2, qT], writes=[sc.bufs[hf]])

                    def SM_PV_gen(u):
                        bi, g = units[u]
                        sc = scv.pop(u)
                        PT = PT_ring.next()
                        o = o_ring.next()
                        den = den_ring.next()
                        attn = attn_tiles[bi]
                        for hf in range(2):
                            P.op("act", lambda e, hf=hf: e.activation(
                                out=PT[:, 2 * hf:2 * hf + 2, :, :].rearrange("p j k q -> p (j k q)"),
                                in_=sc[:, 2 * hf:2 * hf + 2, :, :].rearrange("p j k q -> p (j k q)"), func=AF.Exp),
                                reads=[sc.bufs[hf]], writes=[PT_half[id(PT)][hf]])
                            yield
                        for j in range(4):
                            for kb in range(2):
                                P.noinc = (not (j == 3 and kb == 1))
                                P.op("pe", lambda e, j=j, kb=kb: e.matmul(
                                    out=o[:, j, 0:65], lhsT=PT[:, j, kb, :], rhs=v_aug[:, bi + kb, g, 0:65],
                                    start=(kb == 0), stop=(kb == 1)), reads=[PT_half[id(PT)][j // 2], v_aug], writes=o.bufs)
                        yield
                        P.op("dve", lambda e: e.tensor_scalar(out=den[:, 0:4], in0=o[:, :, 64], scalar1=1.0, scalar2=None, op0=ALU.add),
                             reads=o.bufs, writes=[den])
                        yield
                        P.op("dve", lambda e: e.reciprocal(out=den[:, 4:8], in_=den[:, 0:4]), writes=[den])
                        yield
                        P.op("dve", lambda e: e.tensor_tensor(
                            out=attn[:, g * 256:(g + 1) * 256].rearrange("p (j d) -> p j d", j=4), in0=o[:, :, 0:64],
                            in1=den[:, 4:8].unsqueeze(2).to_broadcast([128, 4, 64]), op=ALU.mult),
                            reads=o.bufs + [den], writes=[attn])

                    QK(0)
                    if len(units) > 1:
                        QK(1)
                    for p in range(0, len(units), 2):
                        pair = [SM_PV_gen(u) for u in (p, p + 1) if u < len(units)]
                        while pair:
                            step(pair)
                            step(fgens)
                        for u in (p + 2, p + 3):
                            if u < len(units):
                                QK(u)
                        for _ in range(4):
                            step(bg)
                        pending_F.append(p // 2)
                        if not bg:
                            for bi in pending_F:
                                fgens.append(stage_F_gen(si, bi, attn_tiles[bi], outs[bi]))
                            pending_F = []
                    while bg:
                        step(bg)
                    for bi in pending_F:
                        fgens.append(stage_F_gen(si, bi, attn_tiles[bi], outs[bi]))
                    lockstep(fgens)
                    return outs

                fslot = [0]

                def stage_F_gen(si, bi, attn, out):
                    def pool_mm():
                        fslot[0] = (fslot[0] + 1) % 2
                        pap, pb = half_ap(fslot[0]), Hb[fslot[0]]
                        for g in range(4):
                            P.noinc = (g != 3)
                            P.op("pe", lambda e, g=g: e.matmul(
                                out=pap[:, g * 128:(g + 1) * 128], lhsT=dd[:, g, bi * 128:(bi + 1) * 128],
                                rhs=W_pool_sb[:, g, :], start=True, stop=True), reads=[dd, W_pool_sb], writes=[pb])
                        return pap, pb
                    return tail_F_gen(attn, pool_mm, out)

                def stage_G_gen(si, bi, mix_in):
                    B = supers[si][bi]
                    return tail_G_gen(mix_in, xts[si][:, bi, :], xts[si], B * 128, x1d_bufs[B])

                def stage_A_gen(si, bi):
                    return norm_T_gen(xts[si][:, bi, :], xts[si], g_pre_mix_bc, xT, bi * 128, trA_ring)

                load_x(0)
                if len(supers) > 1:
                    load_x(1)
                for bi in range(len(supers[0])):
                    run(stage_A_gen(0, bi))
                stage_B(0, None)
                stage_D(0, None)
                for si, blocks in enumerate(supers):
                    nb = len(blocks)
                    has_next = si + 1 < len(supers)
                    nnb = len(supers[si + 1]) if has_next else 0
                    outs = stage_E(si, [stage_C_gen(si)])
                    _stop("p1e")
                    for p0 in range(0, max(nb, nnb), 2):
                        gens = []
                        for bi in range(p0, min(p0 + 2, max(nb, nnb))):
                            if bi < nnb:
                                gens.append(stage_A_gen(si + 1, bi))
                            if bi < nb:
                                gens.append(stage_G_gen(si, bi, outs[bi]["mix"]))
                        lockstep(gens)
                    if has_next:
                        stage_B(si + 1, nb * 128)
                        stage_D(si + 1, nb)
                        if si + 2 < len(supers):
                            load_x(si + 2)
                P.barrier()
                _stop("p1")
        with contextlib.ExitStack() as s2:
            T2 = NB2 * 128
            W_up_sb = P.sb("W_up", [128, 8, NUP], BF16, s2)
            W_dn_sb = P.sb("W_dn", [128, NJ, D], BF16, s2)
            g_pre_ffn_bc = P.sb("g_pre_ffn_bc", [128, D], F32, s2)
            g_post_ffn_bc = P.sb("g_post_ffn_bc", [128, D], F32, s2)
            cwb = P.sb("cwb", [128, 44, 4], F32, s2)
            halo = P.sb("halo", [128, 44, 2], F32, s2)
            WUP_GROUPS = [(0, 6), (6, 12), (12, 17), (17, 22)]
            wup_bufs = [Buf("wup%d" % q) for q in range(len(WUP_GROUPS))]
            wup_of_j = {}
            for q, (j0, j1) in enumerate(WUP_GROUPS):
                for j in range(j0, j1):
                    wup_of_j[j] = wup_bufs[q]
                for hv in range(2):
                    c0, c1 = hv * DFF + j0 * 128, hv * DFF + j1 * 128
                    P.dma("pool", W_up_sb[:, :, c0:c1], w_up[:, c0:c1].rearrange("(c p) n -> p c n", p=128),
                          writes=[wup_bufs[q]])
            for j0 in range(0, NJ, 11):
                P.dma("pool", W_dn_sb[:, j0:j0 + 11, :], w_down[j0 * 128:(j0 + 11) * 128, :].rearrange("(j p) n -> p j n", p=128),
                      writes=[W_dn_sb])
            P.dma("sp", g_pre_ffn_bc[:], g_pre_ffn.partition_broadcast(128), writes=[g_pre_ffn_bc])
            P.dma("sp", g_post_ffn_bc[:], g_post_ffn.partition_broadcast(128), writes=[g_post_ffn_bc])
            P.op("pool", lambda e: e.memset(halo[:], 0.0), writes=[halo])

            xb2_ring = P.ring("xb2", [128, D], BF16, 1, s2)
            ssq2_ring = P.ring("ssq_f", [128, 4], F32, 6, s2)
            x1_in = P.ring("x1_in", [128, D], F32, 2, s2)
            x1nT = P.sb("x1nT", [128, 8, T2], BF16, s2)
            a_sb = P.sb("a_sb", [128, NJ, T2], BF16, s2)
            a_main = a_sb
            a_bufs = [Buf("a%d" % j) for j in range(NJ)]
            a_alias = a_bufs[0:10]
            x1nT_s = P.sb("x1nT_s", [128, 8, NS], BF16, s2)
            a_s = Tile(a_sb.t[:, 9, 0:NJ * NS].rearrange("p (j n) -> p j n", n=NS), Buf("a_s"))
            ytmp_ring = P.ring("ytmp", [128, D], F32, 2, s2)
            ytmp = ytmp_ring.tiles[0]

            tr2 = P.ps("tr2", [128, 8, 128], BF16, s2)
            gv_ps = [P.ps("gv%d" % i, [128, 2, 512], F32, s2) for i in range(2)]
            ffn_ps = P.ps("ffn", [128, 1024], F32, s2)
            ffA = Buf("ffA")
            ffB = Buf("ffB")
            trf = P.ps("trf", [128, 512], F32, s2)

            def norm_pre2(x_ap, x_tile):
                ssq = ssq2_ring.next()
                xb = xb2_ring.next()
                P.op("act", lambda e: e.activation(out=xb[:], in_=x_ap, func=AF.Square, accum_out=ssq[:, 0:1]),
                     reads=[x_tile], writes=[xb, ssq])
                stt, rstd = rstd_from_ssq(ssq[:, 0:1], ssq, D)
                P.op("dve", lambda e: e.scalar_tensor_tensor(out=xb[:], in0=x_ap, scalar=rstd, in1=g_pre_ffn_bc[:],
                                                             op0=ALU.mult, op1=ALU.mult),
                     reads=[x_tile, stt, g_pre_ffn_bc], writes=[xb])
                return xb

            def norm_tr2(xb, dstT, col0):
                for c in range(8):
                    P.noinc = (c != 7)
                    P.op("pe", lambda e, c=c: e.transpose(out=tr2[:, c, :], in_=xb[:, c * 128:(c + 1) * 128], identity=ident_b[:]),
                         reads=[xb, ident_b], writes=[tr2])
                P.op("act", lambda e: e.copy(out=dstT[:, :, col0:col0 + 128], in_=tr2[:]), reads=[tr2], writes=[dstT])

            class ChAcc:
                def __init__(self, at, b):
                    self.at = at
                    self.b = b

            def to_token_major(src, n, dst_dram_fn):
                for q in range(11):
                    for i in range(4):
                        jj = q * 4 + i
                        P.op("pe", lambda e, jj=jj, i=i: e.transpose(out=trf[0:n, i * 128:(i + 1) * 128], in_=src.at(jj),
                                                                     identity=ident_f[:]),
                             reads=[src.b, ident_f, a_alias], writes=[trf])
                    P.op("dve", lambda e: e.tensor_copy(out=ytmp[0:n, 0:512], in_=trf[0:n, :]), reads=[trf], writes=[ytmp])
                    P.dma("pool", dst_dram_fn(q), ytmp[0:n, 0:512], reads=[ytmp], writes=[outb], sembuf=ytmp)

            class UpBuf:
                def __init__(self, name, T, stack):
                    self.up = [P.sb("%s_up%d" % (name, hv), [128, 2 + T], F32, stack) for hv in range(2)]
                    self.hb = [Buf("%s_halo%d" % (name, hv)) for hv in range(2)]
                    self.c = [P.sb("%s_c%d" % (name, hv), [128, T], F32, stack) for hv in range(2)]

            def ffn_norm_pre(B, bi):
                x1 = x1_in.next()
                P.dma("sp", x1[:], x1d[B * 128:(B + 1) * 128, :], reads=[x1d_bufs[B]], writes=[x1])
                return norm_pre2(x1[:], x1)

            def ffn_norm_tr(B, bi, xb, dst=None, ncols=128):
                if dst is not None:
                    for c in range(8):
                        P.noinc = (c != 7)
                        P.op("pe", lambda e, c=c: e.transpose(out=tr2[:, c, :], in_=xb[:, c * 128:(c + 1) * 128], identity=ident_b[:]),
                             reads=[xb, ident_b], writes=[tr2])
                    P.op("dve", lambda e: e.tensor_copy(out=dst[:, :, 0:ncols], in_=tr2[:, :, 0:ncols]), reads=[tr2], writes=[dst])
                    return
                norm_tr2(xb, x1nT, bi * 128)
                if B == 0:
                    P.op("pool", lambda e: e.memset(x1nT[:, :, 0:112], 0.0), writes=[x1nT])

            def ffn_norm(B, bi):
                ffn_norm_tr(B, bi, ffn_norm_pre(B, bi))

            def run2(gen):
                for _ in gen:
                    pass

            def lockstep2(gens):
                gens = list(gens)
                while gens:
                    for g_ in list(gens):
                        try:
                            next(g_)
                        except StopIteration:
                            gens.remove(g_)

            def ffn_up(blocks, sample, ub_ring, scT=None, upS=None):
                run2(ffn_up_gen(blocks, sample, ub_ring, x1nT, a_sb, scT, upS))

            def ffn_halo_gen():
                for j in range(NJ):
                    gv = gv_ps[j % 2]
                    for hv in range(2):
                        col = hv * DFF + j * 128
                        for c in range(8):
                            P.noinc = (c != 7)
                            P.op("pe", lambda e, c=c, hv=hv, col=col: e.matmul(
                                out=gv[:, hv, 0:16], lhsT=W_up_sb[:, c, col:col + 128], rhs=x1nT[:, c, 112:128],
                                start=(c == 0), stop=(c == 7)), reads=[wup_of_j[j], x1nT], writes=[gv])
                    for hv in range(2):
                        ch = hv * NJ + j
                        P.op("act", lambda e, hv=hv, ch=ch: e.copy(out=halo[:, ch, :], in_=gv[:, hv, 14:16]),
                             reads=[gv], writes=[halo])
                    yield

            def ffn_up_gen(blocks, sample, ub_ring, x1nT, a_sb, scT=None, upS=None):
                N = NS if sample else len(blocks) * 128
                pend = None
                for j in range(NJ + 1):
                    if j < NJ:
                        gv = gv_ps[j % 2]
                        ub = ub_ring[j % len(ub_ring)]
                        for hv in range(2):
                            col = hv * DFF + j * 128
                            for c in range(8):
                                P.noinc = (c != 7)
                                P.op("pe", lambda e, c=c, hv=hv, col=col, gv=gv: e.matmul(
                                    out=gv[:, hv, 0:N], lhsT=W_up_sb[:, c, col:col + 128], rhs=x1nT[:, c, 0:N],
                                    start=(c == 0), stop=(c == 7)), reads=[wup_of_j[j], x1nT], writes=[gv])
                        for hv in range(2):
                            ch = hv * NJ + j
                            if not sample:
                                P.op("pool", lambda e, hv=hv, ch=ch: e.tensor_copy(out=ub.up[hv][:, 0:2], in_=halo[:, ch, :]),
                                     reads=[halo], writes=[ub.hb[hv]])
                            P.op("act", lambda e, hv=hv: e.copy(out=ub.up[hv][:, 2:2 + N], in_=gv[:, hv, 0:N]),
                                 reads=[gv], writes=[ub.up[hv]])
                        for hv in range(2):
                            ch = hv * NJ + j
                            P.op("act", lambda e, hv=hv, ch=ch: e.activation(
                                out=ub.c[hv][:, 0:N], in_=gv[:, hv, 0:N], func=AF.Identity, bias=cwb[:, ch, 3:4], scale=cwb[:, ch, 2:3]),
                                reads=[gv, cwb], writes=[ub.c[hv]])
                        for hv in range(2):
                            ch = hv * NJ + j
                            if not sample:
                                P.op("pool", lambda e, hv=hv, ch=ch: e.tensor_copy(out=halo[:, ch, :], in_=ub.up[hv][:, N:N + 2]),
                                     reads=[ub.up[hv]], writes=[halo])
                            else:
                                P.op("pool", lambda e, hv=hv, ch=ch: e.tensor_copy(out=upS.at(ch), in_=ub.up[hv][:, 2:2 + N]),
                                     reads=[ub.up[hv], a_alias], writes=[upS.b])
                        for i in (1, 0):
                            for hv in range(2):
                                ch = hv * NJ + j
                                if sample:
                                    in0 = scT.at(ch).rearrange("p (b i) -> p i b", i=2)[:, i, :]
                                    rd = [scT.b, cwb, a_alias]
                                else:
                                    in0 = ub.up[hv][:, i:i + N]
                                    rd = [ub.up[hv], ub.hb[hv], cwb]
                                P.op("dve", lambda e, hv=hv, ch=ch, i=i, in0=in0: e.scalar_tensor_tensor(
                                    out=ub.c[hv][:, 0:N], in0=in0, scalar=cwb[:, ch, i:i + 1], in1=ub.c[hv][:, 0:N],
                                    op0=ALU.mult, op1=ALU.add), reads=rd, writes=[ub.c[hv]])
                    if pend is not None:
                        pj, pub = pend
                        P.op("act", lambda e: e.activation(out=pub.c[0][:, 0:N], in_=pub.c[0][:, 0:N], func=AF.Gelu_apprx_tanh),
                             writes=[pub.c[0]])
                        P.op("pool", lambda e: e.tensor_tensor(out=a_sb[:, pj, 0:N], in0=pub.c[0][:, 0:N], in1=pub.c[1][:, 0:N], op=ALU.mult),
                             reads=[pub.c[0], pub.c[1]], writes=[a_sb if sample else a_bufs[pj]])
                    pend = (j, ub) if j < NJ else None
                    yield

            ffn_bufs = [(ffn_ps.t, [ffA, ffB]), (gv_ps[1].t[:].rearrange("p a n -> p (a n)"), [gv_ps[1].b, gv_ps[1].b])]

            def ffn_down_mm(B, bi, sample, fb, a_sb=None):
                a_sb = a_sb if a_sb is not None else a_main
                M = NS if sample else 128
                fap, fbufs = fb
                for half in range(2):
                    for j in range(NJ):
                        P.noinc = (j != NJ - 1)
                        P.op("pe", lambda e, j=j, half=half: e.matmul(
                            out=fap[0:M, half * 512:(half + 1) * 512], lhsT=a_sb[:, j, bi * 128:bi * 128 + M],
                            rhs=W_dn_sb[:, j, half * 512:(half + 1) * 512], start=(j == 0), stop=(j == NJ - 1)),
                            reads=([a_sb, a_alias] if sample else [a_bufs[j]]) + [W_dn_sb], writes=[fbufs[half]])

            def ffn_down_tail(B, bi, sample, fb):
                M = NS if sample else 128
                fap, fbufs = fb
                x1 = x1_in.next()
                P.dma("sp", x1[:], x1d[B * 128:(B + 1) * 128, :], reads=[x1d_bufs[B]], writes=[x1])
                ssq = ssq2_ring.next()
                yt = ytmp_ring.next()
                for half in range(2):
                    P.op("act", lambda e, half=half: e.activation(
                        out=yt[0:M, half * 512:(half + 1) * 512], in_=fap[0:M, half * 512:(half + 1) * 512],
                        func=AF.Square, accum_out=ssq[0:M, half:half + 1]),
                        reads=[fbufs[half]], writes=[yt, ssq])
                P.op("dve", lambda e: e.tensor_tensor(out=ssq[0:M, 2:3], in0=ssq[0:M, 0:1], in1=ssq[0:M, 1:2], op=ALU.add), writes=[ssq])
                stt, rstd = rstd_from_ssq(ssq[0:M, 2:3], ssq, D, M)
                for half in range(2):
                    P.op("dve", lambda e, half=half: e.scalar_tensor_tensor(
                        out=yt[0:M, half * 512:(half + 1) * 512], in0=fap[0:M, half * 512:(half + 1) * 512], scalar=stt[0:M, 2:3],
                        in1=g_post_ffn_bc[0:M, half * 512:(half + 1) * 512], op0=ALU.mult, op1=ALU.mult),
                        reads=[fbufs[half], stt, g_post_ffn_bc], writes=[yt])
                P.op("dve", lambda e: e.tensor_tensor(out=yt[0:M, :], in0=yt[0:M, :], in1=x1[0:M, :], op=ALU.add),
                     reads=[x1], writes=[yt])
                if sample:
                    P.dma("pool", y_sample, yt[0:NS, :], reads=[yt], writes=[outb], sembuf=yt)
                else:
                    P.dma("pool", y_prompt[(B - 1) * 128:B * 128, :], yt[:], reads=[yt], writes=[outb], sembuf=yt)

            def ffn_down(B, bi, sample):
                ffn_down_mm(B, bi, sample, ffn_bufs[0])
                ffn_down_tail(B, bi, sample, ffn_bufs[0])

            P.dma("sp", cwb[:].rearrange("p c r -> p (c r)"), cw_d, reads=[csd_buf], writes=[cwb])
            ub_ring = [UpBuf("ub%d" % i, T2, s2) for i in range(2)]
            supers2 = [[0]] + [list(range(1 + i * NB2, 1 + (i + 1) * NB2)) for i in range((NBLK - 1) // NB2)]
            if DBG_NSUPER is not None:
                supers2 = supers2[:DBG_NSUPER]
            for bi, B in enumerate(supers2[0]):
                ffn_norm(B, bi)
            scT_v = a_sb.t[:, 0:6, :].bitcast(F32).rearrange("p a b -> p (a b)")[:, 0:1408].rearrange("p (c r) -> p c r", r=32)
            upS_v = a_sb.t[:, 6:9, :].bitcast(F32).rearrange("p a b -> p (a b)")[:, 0:704].rearrange("p (c r) -> p c r", r=NS)
            scT = ChAcc(lambda ch: scT_v[:, ch, :], Buf("scT"))
            upS = ChAcc(lambda ch: upS_v[:, ch, :], Buf("upS"))
            P.dma("sp", a_sb.t[:, 0:6, :].bitcast(F32).rearrange("p a b -> p (a b)")[:, 0:1408], sc_d,
                  reads=[csd_buf, a_alias], writes=[scT.b])
            P.dma("sp", conv_sample[:, 0, :], sconv.rearrange("(b i) c -> b i c", i=2)[:, 1, :], writes=[outb])
            ub_s = [UpBuf("ubs%d" % i, NS, s2) for i in range(2)]
            ffn_norm_tr(NBLK, 0, ffn_norm_pre(NBLK, 0), dst=x1nT_s, ncols=NS)
            for si, blocks in enumerate(supers2):
                if si == 0:
                    lockstep2([ffn_halo_gen(),
                               ffn_up_gen([NBLK], True, ub_s, x1nT_s, a_s, scT, upS)])
                    ffn_down_mm(NBLK, 0, True, ffn_bufs[0], a_s)
                    ffn_down_tail(NBLK, 0, True, ffn_bufs[0])
                    to_token_major(upS, NS, lambda q: conv_sample[:, 1, q * 512:(q + 1) * 512])
                else:
                    ffn_up(blocks, False, ub_ring)
                nxt = supers2[si + 1] if si + 1 < len(supers2) else []
                if blocks[0] == 0:
                    for bi, B in enumerate(nxt):
                        ffn_norm(B, bi)
                    continue
                nbk = len(blocks)
                fbs = [ffn_bufs[(nbk - 1 - bi) % 2] for bi in range(nbk)]
                ffn_down_mm(blocks[0], 0, False, fbs[0])
                for bi, B in enumerate(blocks):
                    xb_n = ffn_norm_pre(nxt[bi], bi) if bi < len(nxt) else None
                    if xb_n is not None:
                        ffn_norm_tr(nxt[bi], bi, xb_n)
                    if bi + 1 < nbk:
                        ffn_down_mm(blocks[bi + 1], bi + 1, False, fbs[bi + 1])
                    ffn_down_tail(B, bi, False, fbs[bi])
                for bi in range(nbk, len(nxt)):
                    ffn_norm(nxt[bi], bi)
            to_token_major(ChAcc(lambda jj: halo[:, jj, :], halo.b), 2, lambda q: conv_prompt[:, q * 512:(q + 1) * 512])


def _consts():
    ident = np.eye(128, dtype=np.float32)
    slopes = np.exp2(-(np.arange(1, 9, dtype=np.float32) * 1.0)).astype(np.float32)
    key = np.arange(128)[:, None, None]
    kb = np.arange(2)[None, :, None]
    q = np.arange(128)[None, None, :]
    dist = 128 + q - (kb * 128 + key)
    valid = (dist >= 0) & (dist <= 128)
    bb = np.empty((128, 8, 2, 128), np.float32)
    for h in range(8):
        bb[:, h] = np.where(valid, -slopes[h] * dist.astype(np.float32), NEG)
    bbase = bb.reshape(128, 8 * 256)
    cinv = np.ones((4, 128), np.float32)
    for g, w in enumerate(POOL_W):
        pos = np.arange(128) - 112
        cnt = np.where(pos >= 0, np.minimum(pos + 1, w), w).astype(np.float32)
        cinv[g] = 1.0 / cnt
    cinv = cinv.reshape(1, 512)
    sel = np.zeros((120, 2, 4, 16), np.float32)
    for t in range(2):
        for bl in range(8):
            for r in range(15):
                for g, w in enumerate(POOL_W):
                    if r >= 16 - w:
                        sel[bl * 15 + r, t, g, t * 8 + bl] = 1.0
    sel = sel.reshape(120, 128)
    sbias = np.zeros((128, 129), np.float32)
    for h in range(8):
        sbias[h * 16:(h + 1) * 16, 0:128] = -slopes[h] * (128 - np.arange(128, dtype=np.float32))[None, :]
    return ident, bbase, cinv, sel, sbias


_NC_CACHE = {}


def kernel(x_prompt, x_sample, cache_k, cache_v, state_pool, state_conv, meta,
           w_in, b_in, sinks, w_pool, pool_scale, g_attn_out, g_pool_out, w_o,
           g_pre_mix, g_post_mix, g_pre_ffn, g_post_ffn, w_up, conv_w, conv_b, w_down):
    f = lambda a: np.ascontiguousarray(np.asarray(a, dtype=np.float32))
    ident, bbase, cinv, sel, sbias = _consts()
    if "nc" not in _NC_CACHE:
        _NC_CACHE["nc"] = build_program()
    nc = _NC_CACHE["nc"]
    shared = {
        "meta": f(meta), "w_in": f(w_in[0]), "b_in": f(b_in[0]).reshape(1, NIN), "sinks": f(sinks[0]).reshape(1, 8),
        "w_pool": f(w_pool[0]).reshape(512, 128), "pool_scale": f(pool_scale[0]).reshape(1, 512),
        "g_attn": f(g_attn_out[0]).reshape(1, 512), "g_pool": f(g_pool_out[0]).reshape(1, 512), "w_o": f(w_o[0]),
        "g_pre_mix": f(g_pre_mix[0]).reshape(1, D), "g_post_mix": f(g_post_mix[0]).reshape(1, D),
        "g_pre_ffn": f(g_pre_ffn[0]).reshape(1, D), "g_post_ffn": f(g_post_ffn[0]).reshape(1, D),
        "w_up": f(w_up[0]), "conv_w": f(conv_w[0]), "conv_b": f(conv_b[0]).reshape(1, NUP), "w_down": f(w_down[0]),
        "c_ident": ident, "c_bbase": bbase, "c_cinv": cinv, "c_sel": sel, "c_sbias": sbias,
    }
    xpn = np.asarray(x_prompt, dtype=np.float32)
    xsn = np.asarray(x_sample, dtype=np.float32)
    ckn = np.asarray(cache_k, dtype=np.float32)
    cvn = np.asarray(cache_v, dtype=np.float32)
    spn = np.asarray(state_pool, dtype=np.float32)
    scn = np.asarray(state_conv, dtype=np.float32)
    in_maps = []
    for i in range(8):
        sl = slice(i * NS, (i + 1) * NS)
        m = dict(shared)
        m["xp"] = f(xpn[i])
        m["xs"] = f(xsn[sl, 0, :])
        m["ck"] = f(ckn[0, sl].reshape(NS, 128, 128))
        m["cv"] = f(cvn[0, sl].reshape(NS, 128, 128))
        m["spool"] = f(spn[0, sl])
        m["sconv"] = f(scn[0, sl].reshape(NS * 2, NUP))
        in_maps.append(m)
    res = run_bass_kernel_spmd(nc, in_maps, core_ids=list(range(8)))
    R = res.results
    y_prompt = np.stack([R[i]["y_prompt"] for i in range(8)], 0)
    y_sample = np.concatenate([R[i]["y_sample"] for i in range(8)], 0).reshape(128, 1, D)
    k_prompt = np.stack([R[i]["k_prompt"].reshape(128, 2, 64) for i in range(8)], 0)[None]
    v_prompt = np.stack([R[i]["v_prompt"].reshape(128, 2, 64) for i in range(8)], 0)[None]
    pool_prompt = np.stack([R[i]["pool_prompt"] for i in range(8)], 0)[None]
    conv_prompt = np.stack([R[i]["conv_prompt"] for i in range(8)], 0)[None]
    k_sample = np.concatenate([R[i]["k_sample"].reshape(NS, 128, 2, 64) for i in range(8)], 0)[None]
    v_sample = np.concatenate([R[i]["v_sample"].reshape(NS, 128, 2, 64) for i in range(8)], 0)[None]
    pool_sample = np.concatenate([R[i]["pool_sample"] for i in range(8)], 0)[None]
    conv_sample = np.concatenate([R[i]["conv_sample"] for i in range(8)], 0)[None]
    outs = (y_prompt, y_sample, k_prompt, v_prompt, pool_prompt, conv_prompt, k_sample, v_sample, pool_sample, conv_sample)
    return tuple(np.ascontiguousarray(o, dtype=np.float32) for o in outs)
```

```python
import contextlib
import numpy as np
import concourse.bass as bass
import concourse.mybir as mybir
from concourse.bass_utils import run_bass_kernel_spmd

F32 = mybir.dt.float32
BF16 = mybir.dt.bfloat16
AF = mybir.ActivationFunctionType
ALU = mybir.AluOpType
AX = mybir.AxisListType

D = 1024
NIN = 1280
DFF = 2816
NUP = 5632
NJ = 22
SEQ = 4096
NBLK = 33
NS = 16
HAL = 16
EPS = 1e-6
NEG = -30000.0
POOL_W = (2, 4, 8, 16)

SAME_ENGINE_SYNC = True
NB1 = 4
NB2 = 4
DBG_NSUPER = None
DBG_STOP = None


class _Stop(Exception):
    pass


_STOPPED = [False]


def _stop(name):
    if DBG_STOP == name:
        _STOPPED[0] = True


class Buf:
    def __init__(self, name):
        self.name = name
        self.w = None
        self.r = {}
        self.dsem = None
        self.dcount = 0


class Tile:
    def __init__(self, t, b):
        self.t = t
        self.b = b

    def __getitem__(self, idx):
        return self.t[idx]


class Ring:
    def __init__(self, tiles):
        self.tiles = tiles
        self.i = -1

    def next(self):
        self.i = (self.i + 1) % len(self.tiles)
        return self.tiles[self.i]


class Eng:
    def __init__(self, name, obj, sem):
        self.name = name
        self.obj = obj
        self.sem = sem
        self.count = 0
        self.waited = {}


class Prog:
    def __init__(self, nc, stack):
        self.nc = nc
        self.stack = stack
        self.sems = {}
        self.engs = {}
        for n, o in [("pe", nc.tensor), ("act", nc.scalar), ("dve", nc.vector),
                     ("pool", nc.gpsimd), ("sp", nc.sync)]:
            s = stack.enter_context(nc.semaphore("sem_" + n))
            self.sems["e:" + n] = s
            self.engs[n] = Eng(n, o, s)
        self.nuid = 0
        self.dcounts = {}
        self.noinc = False

    def uid(self):
        self.nuid += 1
        return self.nuid

    def sb(self, name, shape, dtype, stack=None):
        st = stack if stack is not None else self.stack
        t = st.enter_context(self.nc.sbuf_tensor("%s_%d" % (name, self.uid()), list(shape), dtype))
        return Tile(t, Buf(name))

    def ps(self, name, shape, dtype, stack=None):
        st = stack if stack is not None else self.stack
        t = st.enter_context(self.nc.psum_tensor("%s_%d" % (name, self.uid()), list(shape), dtype))
        return Tile(t, Buf(name))

    def ring(self, name, shape, dtype, n, stack=None):
        return Ring([self.sb("%s%d" % (name, i), shape, dtype, stack) for i in range(n)])

    def _dsem(self, b, queue):
        if b.dsem is None:
            b.dsem = {}
        if queue not in b.dsem:
            key = "d:%s:%s:%d" % (b.name, queue, self.uid())
            s = self.stack.enter_context(self.nc.semaphore("ds_%d" % len(self.sems)))
            self.sems[key] = s
            b.dsem[queue] = key
            self.dcounts[key] = 0
        return b.dsem[queue]

    def _wait(self, e, deps):
        for key, val in deps.items():
            if key == "e:" + e.name and not (SAME_ENGINE_SYNC and e.name in ("act", "dve", "pool")):
                continue
            if key in self.dcounts:
                val = 16 * self.dcounts[key]
            if e.waited.get(key, 0) >= val:
                continue
            e.obj.wait_ge(self.sems[key], val)
            e.waited[key] = val

    @staticmethod
    def _collect(reads, writes):
        deps = {}

        def add(d):
            if d is None:
                return
            k, v = d
            if deps.get(k, 0) < v:
                deps[k] = v
        for b in reads:
            add(b.w)
        for b in writes:
            add(b.w)
            for k, v in b.r.items():
                add((k, v))
        return deps

    @staticmethod
    def _commit(dep, reads, writes):
        k, v = dep
        for b in reads:
            if b in writes:
                continue
            if b.r.get(k, 0) < v:
                b.r[k] = v
        for b in writes:
            b.w = dep
            b.r = {}

    @staticmethod
    def _bufs(xs):
        out = []
        for x in xs:
            if isinstance(x, (list, tuple)):
                out.extend(Prog._bufs(x))
            else:
                out.append(x.b if isinstance(x, Tile) else x)
        return out

    def op(self, eng, fn, reads=(), writes=()):
        noinc = self.noinc and eng == "pe"
        self.noinc = False
        if _STOPPED[0]:
            return None
        reads = self._bufs(reads)
        writes = self._bufs(writes)
        e = self.engs[eng]
        self._wait(e, self._collect(reads, writes))
        ins = fn(e.obj)
        if noinc:
            self._commit(("e:" + eng, e.count + 1), reads, writes)
            return ins
        e.count += 1
        ins.then_inc(e.sem, 1)
        self._commit(("e:" + eng, e.count), reads, writes)
        return ins

    def dma(self, queue, out, in_, reads=(), writes=(), sembuf=None, **kw):
        if _STOPPED[0]:
            return None
        reads = self._bufs(reads)
        writes = self._bufs(writes)
        e = self.engs[queue]
        self._wait(e, self._collect(reads, writes))
        if sembuf is None:
            sembuf = (list(writes) + list(reads))[0]
        elif isinstance(sembuf, Tile):
            sembuf = sembuf.b
        key = self._dsem(sembuf, queue)
        ins = e.obj.dma_start(out=out, in_=in_, **kw)
        ins.then_inc(self.sems[key], 16)
        self.dcounts[key] += 1
        self._commit((key, 16 * self.dcounts[key]), reads, writes)
        return ins

    def barrier(self):
        if _STOPPED[0]:
            return
        targets = {}
        for n, e in self.engs.items():
            if e.count > 0:
                targets["e:" + n] = e.count
        for k, c in self.dcounts.items():
            targets[k] = 16 * c
        for n, e in self.engs.items():
            deps = {k: v for k, v in targets.items() if k != "e:" + n}
            self._wait(e, deps)

    def finish(self, eng="sp"):
        e = self.engs[eng]
        targets = {}
        for n, o in self.engs.items():
            if o.count > 0 and n != eng:
                targets["e:" + n] = o.count
        for k, c in self.dcounts.items():
            targets[k] = 16 * c
        self._wait(e, targets)


def build_program():
    _STOPPED[0] = False
    nc = bass.Bass("TRN2", target_bir_lowering=False)

    def din(name, shape):
        return nc.dram_tensor(name, list(shape), F32, kind="ExternalInput").ap()

    def dout(name, shape):
        return nc.dram_tensor(name, list(shape), F32, kind="ExternalOutput").ap()

    xp = din("xp", [SEQ, D])
    meta = din("meta", [16, D])
    xs = din("xs", [NS, D])
    ck = din("ck", [NS, 128, 128])
    cv = din("cv", [NS, 128, 128])
    spool = din("spool", [NS, 15, 512])
    sconv = din("sconv", [NS * 2, NUP])
    w_in = din("w_in", [D, NIN])
    b_in = din("b_in", [1, NIN])
    sinks = din("sinks", [1, 8])
    w_pool = din("w_pool", [512, 128])
    pool_scale = din("pool_scale", [1, 512])
    g_attn = din("g_attn", [1, 512])
    g_pool = din("g_pool", [1, 512])
    w_o = din("w_o", [D, D])
    g_pre_mix = din("g_pre_mix", [1, D])
    g_post_mix = din("g_post_mix", [1, D])
    g_pre_ffn = din("g_pre_ffn", [1, D])
    g_post_ffn = din("g_post_ffn", [1, D])
    w_up = din("w_up", [D, NUP])
    conv_w = din("conv_w", [3, NUP])
    conv_b = din("conv_b", [1, NUP])
    w_down = din("w_down", [DFF, D])
    c_ident = din("c_ident", [128, 128])
    c_bbase = din("c_bbase", [128, 8 * 256])
    c_cinv = din("c_cinv", [1, 4 * 128])
    c_sel = din("c_sel", [120, 2 * 4 * 16])
    c_sbias = din("c_sbias", [128, 129])

    y_prompt = dout("y_prompt", [SEQ, D])
    y_sample = dout("y_sample", [NS, D])
    k_prompt = dout("k_prompt", [128, 128])
    v_prompt = dout("v_prompt", [128, 128])
    pool_prompt = dout("pool_prompt", [15, 512])
    conv_prompt = dout("conv_prompt", [2, NUP])
    k_sample = dout("k_sample", [NS, 128, 128])
    v_sample = dout("v_sample", [NS, 128, 128])
    pool_sample = dout("pool_sample", [NS, 15, 512])
    conv_sample = dout("conv_sample", [NS, 2, NUP])

    x1d = nc.dram_tensor("x1_scratch", [(NBLK + 1) * 128, D], F32, kind="Internal").ap()
    x1d_bufs = [Buf("x1d%d" % i) for i in range(NBLK + 1)]
    sc_d = nc.dram_tensor("sconvT_scratch", [128, 44 * 32], F32, kind="Internal").ap()
    cw_d = nc.dram_tensor("convwT_scratch", [128, 44 * 4], F32, kind="Internal").ap()
    csd_buf = Buf("csd")
    outb = Buf("outs")

    with contextlib.ExitStack() as top:
        P = Prog(nc, top)
        try:
            _body(P, nc, locals())
        except _Stop:
            pass
        P.finish("sp")
        P.finish("pool")
    return nc


def _body(P, nc, L):
    (xp, meta, xs, ck, cv, spool, sconv, w_in, b_in, sinks, w_pool, pool_scale, g_attn, g_pool, w_o, g_pre_mix,
     g_post_mix, g_pre_ffn, g_post_ffn, w_up, conv_w, conv_b, w_down, c_ident, c_bbase, c_cinv, c_sel, c_sbias,
     y_prompt, y_sample, k_prompt, v_prompt, pool_prompt, conv_prompt, k_sample, v_sample, pool_sample, conv_sample,
     x1d, x1d_bufs, outb, sc_d, cw_d, csd_buf) = [L[k] for k in (
        "xp meta xs ck cv spool sconv w_in b_in sinks w_pool pool_scale g_attn g_pool w_o g_pre_mix "
        "g_post_mix g_pre_ffn g_post_ffn w_up conv_w conv_b w_down c_ident c_bbase c_cinv c_sel c_sbias "
        "y_prompt y_sample k_prompt v_prompt pool_prompt conv_prompt k_sample v_sample pool_sample conv_sample "
        "x1d x1d_bufs outb sc_d cw_d csd_buf").split()]
    if True:

        ident_f = P.sb("ident_f", [128, 128], F32)
        ident_b = P.sb("ident_b", [128, 128], BF16)
        eps_t = P.sb("eps", [128, 1], F32)
        P.dma("sp", ident_f[:], c_ident, writes=[ident_f])
        P.op("dve", lambda e: e.tensor_copy(out=ident_b[:], in_=ident_f[:]), reads=[ident_f], writes=[ident_b])
        P.op("dve", lambda e: e.memset(eps_t[:], EPS), writes=[eps_t])
        mhalf_t = P.sb("mhalf", [128, 1], F32)
        P.op("dve", lambda e: e.memset(mhalf_t[:], -0.5), writes=[mhalf_t])

        stat_ring = P.ring("stat", [128, 8], F32, 16)
        _stop("setup0")

        def rstd_from_ssq(ssq_ap, ssq_tile, n, M=128):
            stt = stat_ring.next()
            P.op("dve", lambda e: e.tensor_scalar(out=stt[0:M, 0:1], in0=ssq_ap, scalar1=1.0 / n, scalar2=EPS,
                                                  op0=ALU.mult, op1=ALU.add), reads=[ssq_tile], writes=[stt])
            P.op("pool", lambda e: e.tensor_tensor(out=stt[0:M, 2:3], in0=stt[0:M, 0:1], in1=mhalf_t[0:M, 0:1], op=ALU.pow),
                 reads=[stt, mhalf_t], writes=[stt])
            return stt, stt[0:M, 2:3]

        with contextlib.ExitStack() as s01:
            W_in_sb = P.sb("W_in", [128, 8, NIN], BF16, s01)
            W_kd = P.sb("W_kd", [128, 8, 2, 2, 64], BF16, s01)
            W_o_sb = P.sb("W_o", [128, 8, D], BF16, s01)
            W_pool_sb = P.sb("W_pool", [128, 4, 128], BF16, s01)
            b_in_bc = P.sb("b_in_bc", [128, NIN], F32, s01)
            b_fm = P.sb("b_fm", [128, 10], F32, s01)
            b_kd = P.sb("b_kd", [128, 2], F32, s01)
            g_pre_mix_bc = P.sb("g_pre_mix_bc", [128, D], F32, s01)
            g_mix_bc = P.sb("g_mix_bc", [128, D], F32, s01)
            g_post_mix_bc = P.sb("g_post_mix_bc", [128, D], F32, s01)
            pool_scale_bc = P.sb("pool_scale_bc", [128, 512], F32, s01)
            sink_bc = P.sb("sink_bc", [128, 8], F32, s01)
            B_hi = P.sb("B_hi", [128, 8, 256], BF16, s01)
            B_lo = P.sb("B_lo", [128, 8, 256], BF16, s01)
            cinv_bc = P.sb("cinv_bc", [128, 4, 128], F32, s01)

            P.dma("pool", W_in_sb[:], w_in.rearrange("(c p) n -> p c n", p=128), writes=[W_in_sb])
            for dup in range(2):
                for g in range(2):
                    P.dma("pool", W_kd[:, :, g, dup, :],
                          w_in[:, 512 + g * 64:512 + (g + 1) * 64].rearrange("(c p) d -> p c d", p=128), writes=[W_kd])
            P.dma("pool", W_o_sb[:], w_o.rearrange("(c p) n -> p c n", p=128), writes=[W_o_sb])
            P.dma("pool", W_pool_sb[:], w_pool.rearrange("(g c) d -> c g d", c=128), writes=[W_pool_sb])
            P.dma("sp", b_in_bc[:], b_in.partition_broadcast(128), writes=[b_in_bc])
            P.dma("sp", g_pre_mix_bc[:], g_pre_mix.partition_broadcast(128), writes=[g_pre_mix_bc])
            P.dma("sp", g_mix_bc[:, 0:512], g_attn.partition_broadcast(128), writes=[g_mix_bc])
            P.dma("sp", g_mix_bc[:, 512:1024], g_pool.partition_broadcast(128), writes=[g_mix_bc])
            P.dma("sp", g_post_mix_bc[:], g_post_mix.partition_broadcast(128), writes=[g_post_mix_bc])
            P.dma("sp", pool_scale_bc[:], pool_scale.partition_broadcast(128), writes=[pool_scale_bc])
            P.dma("sp", sink_bc[:], sinks.partition_broadcast(128), writes=[sink_bc])
            P.dma("sp", cinv_bc[:], c_cinv.rearrange("o (g t) -> o g t", g=4).partition_broadcast(128),
                  writes=[cinv_bc])
            ssq_ring = P.ring("ssq", [128, 4], F32, 12, s01)
            RG = {}

            def make_rings(stack, deep):
                RG["xb"] = P.ring("xb", [128, D], BF16, 2 if deep else 1, stack)
                RG["attn"] = P.ring("attn_sb", [128, 512], F32, 4 if deep else 1, stack)
                RG["pool_sb"] = P.ring("pool_sb", [128, 512], F32, 4 if deep else 1, stack)
                RG["mix_in"] = P.ring("mix_in", [128, D], BF16, 4 if deep else 1, stack)
                RG["mixT"] = P.ring("mixT", [128, 8, 128], BF16, 2 if deep else 1, stack)
                RG["x1"] = P.ring("x1", [128, D], F32, 2 if deep else 1, stack)

            class View:
                def __init__(self, ap, bufs):
                    self.t = ap
                    self.bufs = bufs

                def __getitem__(self, idx):
                    return self.t[idx]

            Q = [P.ps("Q%d" % i, [128, 1024], F32, s01) for i in range(4)]
            Hb = [Buf("H%d" % i) for i in range(8)]

            def half_ap(i):
                return Q[i // 2].t[:, (i % 2) * 512:(i % 2 + 1) * 512]

            tr_ring = Ring([View(half_ap(i).bitcast(BF16).rearrange("p (c t) -> p c t", c=8), [Hb[i]]) for i in (0, 1)])
            trA_ring = Ring([View(half_ap(i).bitcast(BF16).rearrange("p (c t) -> p c t", c=8), [Hb[i]]) for i in (6, 7)])
            mm_slots = [(half_ap(i), Hb[i]) for i in (2, 3, 4, 5, 6, 7)]
            mm_i = [0]
            sc_ring = Ring([View(Q[k].t[:].rearrange("p (j k q) -> p j k q", j=4, k=2), [Hb[2 * k], Hb[2 * k + 1]]) for k in (1, 2)])
            o_ring = Ring([View(half_ap(i).rearrange("p (j d) -> p j d", j=4), [Hb[i]]) for i in (6, 7)])
            wo_ring = Ring([View(Q[k].t[:], [Hb[2 * k], Hb[2 * k + 1]]) for k in (1, 2)])

            def next_mm():
                mm_i[0] = (mm_i[0] + 1) % len(mm_slots)
                return mm_slots[mm_i[0]]

            with contextlib.ExitStack() as sb0:
                brow = P.sb("brow", [1, NIN + 256], F32, sb0)
                P.dma("sp", brow[0:1, 0:NIN], b_in, writes=[brow])
                for g in range(2):
                    for dup in range(2):
                        c0 = NIN + g * 128 + dup * 64
                        P.dma("sp", brow[0:1, c0:c0 + 64], b_in[:, 512 + g * 64:512 + (g + 1) * 64], writes=[brow])
                bap, bbuf = mm_slots[0]
                for c in range(12):
                    P.op("pe", lambda e, c=c: e.transpose(out=bap[:, c:c + 1], in_=brow[0:1, c * 128:(c + 1) * 128],
                                                          identity=ident_f[0:1, 0:1]),
                         reads=[brow, ident_f], writes=[bbuf])
                P.op("dve", lambda e: e.tensor_copy(out=b_fm[:], in_=bap[:, 0:10]), reads=[bbuf], writes=[b_fm])
                P.op("dve", lambda e: e.tensor_scalar(out=b_fm[:, 0:4], in0=b_fm[:, 0:4], scalar1=0.125, scalar2=None, op0=ALU.mult),
                     writes=[b_fm])
                Bfull = P.sb("Bfull", [128, 8, 256], F32, sb0)
                P.dma("sp", Bfull[:], c_bbase.rearrange("p (h k) -> p h k", h=8), writes=[Bfull])
                for h in range(8):
                    P.op("dve", lambda e, h=h: e.tensor_scalar(out=Bfull[:, h, :], in0=Bfull[:, h, :],
                                                               scalar1=sink_bc[:, h:h + 1], scalar2=None, op0=ALU.subtract),
                         reads=[sink_bc], writes=[Bfull])
                P.op("dve", lambda e: e.tensor_copy(out=B_hi[:], in_=Bfull[:]), reads=[Bfull], writes=[B_hi])
                P.op("dve", lambda e: e.tensor_tensor(out=B_lo[:], in0=Bfull[:], in1=B_hi[:], op=ALU.subtract),
                     reads=[Bfull, B_hi], writes=[B_lo])
                P.op("dve", lambda e: e.tensor_copy(out=b_kd[:], in_=bap[:, 10:12]), reads=[bbuf], writes=[b_kd])
                P.barrier()

            def run(gen):
                for _ in gen:
                    pass

            def lockstep(gens):
                gens = list(gens)
                while gens:
                    for g_ in list(gens):
                        try:
                            next(g_)
                        except StopIteration:
                            gens.remove(g_)

            def norm_T(x_ap, x_tile, g_bc, dstT, col0):
                run(norm_T_gen(x_ap, x_tile, g_bc, dstT, col0))

            def norm_T_gen(x_ap, x_tile, g_bc, dstT, col0, trr=None):
                ssq = ssq_ring.next()
                xb = RG["xb"].next()
                tr = (trr if trr is not None else tr_ring).next()
                P.op("act", lambda e: e.activation(out=xb[:], in_=x_ap, func=AF.Square, accum_out=ssq[:, 0:1]),
                     reads=[x_tile], writes=[xb, ssq])
                yield
                stt, rstd = rstd_from_ssq(ssq[:, 0:1], ssq, D)
                yield
                P.op("dve", lambda e: e.scalar_tensor_tensor(out=xb[:], in0=x_ap, scalar=rstd, in1=g_bc[:],
                                                             op0=ALU.mult, op1=ALU.mult),
                     reads=[x_tile, stt, g_bc], writes=[xb])
                yield
                for c in range(8):
                    P.noinc = (c != 7)
                    P.op("pe", lambda e, c=c: e.transpose(out=tr[:, c, :], in_=xb[:, c * 128:(c + 1) * 128],
                                                          identity=ident_b[:]),
                         reads=[xb, ident_b], writes=tr.bufs)
                yield
                P.op("dve", lambda e: e.tensor_copy(out=dstT[:, :, col0:col0 + 128], in_=tr[:]),
                     reads=tr.bufs, writes=[dstT])

            def tail_F_gen(attn, pool_mm_fn, out):
                pool_sb = RG["pool_sb"].next()
                mix_in = RG["mix_in"].next()
                ssq = ssq_ring.next()
                out["mix"] = mix_in
                pool_ps_ap, pool_ps_buf = pool_mm_fn()
                P.op("dve", lambda e: e.tensor_tensor(out=pool_sb[:], in0=pool_ps_ap, in1=pool_scale_bc[:], op=ALU.mult),
                     reads=[pool_ps_buf, pool_scale_bc], writes=[pool_sb])
                yield
                P.op("act", lambda e: e.activation(out=mix_in[:, 0:512], in_=attn[:], func=AF.Square, accum_out=ssq[:, 0:1]),
                     reads=[attn], writes=[mix_in, ssq])
                yield
                P.op("act", lambda e: e.activation(out=mix_in[:, 512:1024], in_=pool_sb[:], func=AF.Square, accum_out=ssq[:, 1:2]),
                     reads=[pool_sb], writes=[mix_in, ssq])
                st_a, r_a = rstd_from_ssq(ssq[:, 0:1], ssq, 512)
                yield
                P.op("dve", lambda e: e.scalar_tensor_tensor(out=mix_in[:, 0:512], in0=attn[:], scalar=r_a,
                                                             in1=g_mix_bc[:, 0:512], op0=ALU.mult, op1=ALU.mult),
                     reads=[attn, st_a, g_mix_bc], writes=[mix_in])
                st_p, r_p = rstd_from_ssq(ssq[:, 1:2], ssq, 512)
                yield
                P.op("dve", lambda e: e.scalar_tensor_tensor(out=mix_in[:, 512:1024], in0=pool_sb[:], scalar=r_p,
                                                             in1=g_mix_bc[:, 512:1024], op0=ALU.mult, op1=ALU.mult),
                     reads=[pool_sb, st_p, g_mix_bc], writes=[mix_in])

            def tail_G_gen(mix_in, x_ap, x_tile, x1row, x1buf):
                tr = tr_ring.next()
                mixT = RG["mixT"].next()
                wo = wo_ring.next()
                ssq2 = ssq_ring.next()
                x1 = RG["x1"].next()
                for c in range(8):
                    P.noinc = (c != 7)
                    P.op("pe", lambda e, c=c: e.transpose(out=tr[:, c, :], in_=mix_in[:, c * 128:(c + 1) * 128],
                                                          identity=ident_b[:]),
                         reads=[mix_in, ident_b], writes=tr.bufs)
                yield
                P.op("act", lambda e: e.copy(out=mixT[:], in_=tr[:]), reads=tr.bufs, writes=[mixT])
                yield
                for half in range(2):
                    for c in range(8):
                        P.noinc = (c != 7)
                        P.op("pe", lambda e, c=c, half=half: e.matmul(
                            out=wo[:, half * 512:(half + 1) * 512], lhsT=mixT[:, c, :],
                            rhs=W_o_sb[:, c, half * 512:(half + 1) * 512], start=(c == 0), stop=(c == 7)),
                            reads=[mixT, W_o_sb], writes=[wo.bufs[half]])
                    yield
                for half in range(2):
                    P.op("act", lambda e, half=half: e.activation(
                        out=x1[:, half * 512:(half + 1) * 512], in_=wo[:, half * 512:(half + 1) * 512],
                        func=AF.Square, accum_out=ssq2[:, half:half + 1]),
                        reads=[wo.bufs[half]], writes=[x1, ssq2])
                    yield
                P.op("dve", lambda e: e.tensor_tensor(out=ssq2[:, 2:3], in0=ssq2[:, 0:1], in1=ssq2[:, 1:2], op=ALU.add),
                     reads=[ssq2], writes=[ssq2])
                yield
                st_m, r_m = rstd_from_ssq(ssq2[:, 2:3], ssq2, D)
                yield
                for half in range(2):
                    P.op("dve", lambda e, half=half: e.scalar_tensor_tensor(
                        out=x1[:, half * 512:(half + 1) * 512], in0=wo[:, half * 512:(half + 1) * 512], scalar=r_m,
                        in1=g_post_mix_bc[:, half * 512:(half + 1) * 512], op0=ALU.mult, op1=ALU.mult),
                        reads=[wo.bufs[half], st_m, g_post_mix_bc], writes=[x1])
                    yield
                P.op("dve", lambda e: e.tensor_tensor(out=x1[:], in0=x1[:], in1=x_ap, op=ALU.add),
                     reads=[x_tile], writes=[x1])
                yield
                P.dma("sp", x1d[x1row:x1row + 128, :], x1[:], reads=[x1], writes=[x1buf], sembuf=x1)

            def mix_tail(x_ap, x_tile, attn, pool_ps_ap, pool_ps_buf, x1row, x1buf):
                out = {}
                run(tail_F_gen(attn, lambda: (pool_ps_ap, pool_ps_buf), out))
                run(tail_G_gen(out["mix"], x_ap, x_tile, x1row, x1buf))

            _stop("setup1")
            with contextlib.ExitStack() as s0:
                x_s = P.sb("x_s", [128, D], F32, s0)
                xT_s = P.sb("xT_s", [128, 8, 128], BF16, s0)
                z_s = P.sb("z_s", [128, NIN], F32, s0)
                q_hb = P.sb("q_hb", [128, 64], F32, s0)
                kn_hb = P.sb("kn_hb", [128, 64], F32, s0)
                vn_hb = P.sb("vn_hb", [128, 64], F32, s0)
                sink_hb = P.sb("sink_hb", [128, 1], F32, s0)
                sbias = P.sb("sbias", [128, 129], F32, s0)
                make_rings(s0, False)
                Kc = P.sb("Kc", [128, 128, 64], F32, s0)
                Vc = P.sb("Vc", [128, 128, 64], F32, s0)
                Kb = [Buf("Kc%d" % h) for h in range(8)]
                Vb = [Buf("Vc%d" % h) for h in range(8)]
                prod = P.sb("prod", [128, 128, 64], F32, s0)
                Sall = P.sb("Sall", [128, 129], F32, s0)
                Pm = P.sb("Pm", [128, 129], F32, s0)
                sm = P.sb("sm", [128, 8], F32, s0)
                o_hb = P.sb("o_hb", [128, 64], F32, s0)
                attn_s = P.sb("attn_s", [128, 512], F32, s0)
                spl = P.sb("spl", [128, 2, 512], F32, s0)
                sel = P.sb("sel", [128, 2, 4, 16], F32, s0)
                wsum = P.sb("wsum", [128, 512], F32, s0)
                d_s = P.sb("d_s", [128, 512], BF16, s0)
                dT_s = P.sb("dT_s", [128, 4, 128], BF16, s0)

                P.op("pool", lambda e: e.memset(x_s[:], 0.0), writes=[x_s])
                P.dma("sp", x_s[0:NS, :], xs, writes=[x_s])
                cstage = prod.t[0:36, 0:88, :].rearrange("p a b -> p (a b)")
                rs_sc = prod.t[:, 96:118, :].rearrange("p a b -> p (a b)").rearrange("p (c r) -> p c r", r=32)
                rs_cw = prod.t[:, 118:121, :].rearrange("p a b -> p (a b)")[:, 0:176].rearrange("p (c r) -> p c r", r=4)
                P.dma("sp", cstage[0:32, :], sconv, writes=[prod])
                P.dma("sp", cstage[32:35, :], conv_w, writes=[prod])
                P.dma("sp", cstage[35:36, :], conv_b, writes=[prod])
                for g0 in range(0, 44, 14):
                    n = min(14, 44 - g0)
                    bap_, bbuf_ = next_mm()
                    for k in range(n):
                        P.op("pe", lambda e, k=k: e.transpose(out=bap_[:, k * 36:(k + 1) * 36], in_=cstage[0:36, (g0 + k) * 128:(g0 + k + 1) * 128],
                                                              identity=ident_f[0:36, 0:36]),
                             reads=[prod, ident_f], writes=[bbuf_])
                    pv = bap_[:, 0:n * 36].rearrange("p (c r) -> p c r", r=36)
                    P.op("dve", lambda e: e.tensor_copy(out=rs_sc[:, g0:g0 + n, :], in_=pv[:, :, 0:32]), reads=[bbuf_], writes=[prod])
                    P.op("dve", lambda e: e.tensor_copy(out=rs_cw[:, g0:g0 + n, :], in_=pv[:, :, 32:36]), reads=[bbuf_], writes=[prod])
                P.dma("sp", sc_d, prod.t[:, 96:118, :].rearrange("p a b -> p (a b)"), reads=[prod], writes=[csd_buf], sembuf=prod)
                P.dma("sp", cw_d, prod.t[:, 118:121, :].rearrange("p a b -> p (a b)")[:, 0:176], reads=[prod], writes=[csd_buf], sembuf=prod)
                P.dma("sp", sbias[:], c_sbias, writes=[sbias])
                P.dma("sp", sel[0:120].rearrange("p t g b -> p (t g b)"), c_sel, writes=[sel])
                for t in range(2):
                    P.dma("sp", spl[0:120, t, :], spool[t * 8:(t + 1) * 8].rearrange("b r c -> (b r) c"), writes=[spl])
                def load_cache(dst, bufs, src):
                    for g in range(2):
                        h0 = 4 * g
                        P.dma("act", dst[h0 * 16:(h0 + 1) * 16], src[:, :, g * 64:(g + 1) * 64], writes=[bufs[h0]])
                    for g in range(2):
                        h0 = 4 * g
                        for j in range(1, 4):
                            P.dma("sp", dst[(h0 + j) * 16:(h0 + j + 1) * 16], dst[h0 * 16:(h0 + 1) * 16],
                                  reads=[bufs[h0]], writes=[bufs[h0 + j]])
                load_cache(Kc, Kb, ck)
                load_cache(Vc, Vb, cv)
                for h in range(8):
                    P.dma("sp", sink_hb[h * 16:(h + 1) * 16, :], sinks[:, h:h + 1].partition_broadcast(16), writes=[sink_hb])
                P.dma("sp", k_sample[:, 0:127, :], ck[:, 1:128, :], writes=[outb])
                P.dma("sp", v_sample[:, 0:127, :], cv[:, 1:128, :], writes=[outb])
                P.dma("sp", pool_sample[:, 0:14, :], spool[:, 1:15, :], writes=[outb])

                norm_T(x_s[:], x_s, g_pre_mix_bc, xT_s, 0)
                for (n0, n1) in ((0, 512), (512, 1024), (1024, NIN)):
                    ap, b = next_mm()
                    for c in range(8):
                        P.op("pe", lambda e, c=c, ap=ap, n0=n0, n1=n1: e.matmul(
                            out=ap[:, 0:n1 - n0], lhsT=xT_s[:, c, :], rhs=W_in_sb[:, c, n0:n1],
                            start=(c == 0), stop=(c == 7)), reads=[xT_s, W_in_sb], writes=[b])
                    P.op("dve", lambda e, ap=ap, n0=n0, n1=n1: e.tensor_tensor(
                        out=z_s[:, n0:n1], in0=ap[:, 0:n1 - n0], in1=b_in_bc[:, n0:n1], op=ALU.add),
                        reads=[b, b_in_bc], writes=[z_s])
                P.dma("pool", k_sample[:, 127, :], z_s[0:NS, 512:640], reads=[z_s], writes=[outb], sembuf=z_s)
                P.dma("pool", v_sample[:, 127, :], z_s[0:NS, 640:768], reads=[z_s], writes=[outb], sembuf=z_s)
                P.dma("pool", pool_sample[:, 14, :], z_s[0:NS, 768:1280], reads=[z_s], writes=[outb], sembuf=z_s)
                _stop("p0a")
                for h in range(8):
                    g = h // 4
                    P.dma("sp", q_hb[h * 16:(h + 1) * 16, :], z_s[0:NS, h * 64:(h + 1) * 64], reads=[z_s], writes=[q_hb])
                    P.dma("sp", kn_hb[h * 16:(h + 1) * 16, :], z_s[0:NS, 512 + g * 64:512 + (g + 1) * 64], reads=[z_s], writes=[kn_hb])
                    P.dma("sp", vn_hb[h * 16:(h + 1) * 16, :], z_s[0:NS, 640 + g * 64:640 + (g + 1) * 64], reads=[z_s], writes=[vn_hb])
                P.op("dve", lambda e: e.tensor_tensor(out=prod[:], in0=Kc[:], in1=q_hb[:].unsqueeze(1).to_broadcast([128, 128, 64]),
                                                      op=ALU.mult), reads=Kb + [q_hb], writes=[prod])
                P.op("dve", lambda e: e.tensor_reduce(out=Sall[:, 0:128], in_=prod[:], axis=AX.X, op=ALU.add),
                     reads=[prod], writes=[Sall])
                P.op("dve", lambda e: e.tensor_tensor(out=o_hb[:], in0=kn_hb[:], in1=q_hb[:], op=ALU.mult),
                     reads=[kn_hb, q_hb], writes=[o_hb])
                P.op("dve", lambda e: e.tensor_reduce(out=Sall[:, 128:129], in_=o_hb[:], axis=AX.X, op=ALU.add),
                     reads=[o_hb], writes=[Sall])
                P.op("dve", lambda e: e.scalar_tensor_tensor(out=Sall[:], in0=Sall[:], scalar=0.125, in1=sbias[:],
                                                             op0=ALU.mult, op1=ALU.add), reads=[sbias], writes=[Sall])
                P.op("dve", lambda e: e.tensor_reduce(out=sm[:, 0:1], in_=Sall[:], axis=AX.X, op=ALU.max),
                     reads=[Sall], writes=[sm])
                P.op("dve", lambda e: e.tensor_tensor(out=sm[:, 1:2], in0=sm[:, 0:1], in1=sink_hb[:], op=ALU.max),
                     reads=[sink_hb], writes=[sm])
                P.op("dve", lambda e: e.tensor_scalar(out=sm[:, 2:3], in0=sm[:, 1:2], scalar1=-1.0, scalar2=None, op0=ALU.mult),
                     writes=[sm])
                P.op("act", lambda e: e.activation(out=Pm[:], in_=Sall[:], func=AF.Exp, bias=sm[:, 2:3], scale=1.0,
                                                   accum_out=sm[:, 3:4]), reads=[Sall, sm], writes=[Pm, sm])
                P.op("act", lambda e: e.activation(out=sm[:, 4:5], in_=sink_hb[:], func=AF.Exp, bias=sm[:, 2:3], scale=1.0),
                     reads=[sink_hb], writes=[sm])
                P.op("dve", lambda e: e.tensor_tensor(out=sm[:, 5:6], in0=sm[:, 3:4], in1=sm[:, 4:5], op=ALU.add), writes=[sm])
                P.op("dve", lambda e: e.reciprocal(out=sm[:, 6:7], in_=sm[:, 5:6]), writes=[sm])
                P.op("dve", lambda e: e.tensor_tensor(out=prod[:], in0=Vc[:],
                                                      in1=Pm[:, 0:128].unsqueeze(2).to_broadcast([128, 128, 64]),
                                                      op=ALU.mult), reads=Vb + [Pm], writes=[prod])
                P.op("dve", lambda e: e.tensor_reduce(out=o_hb[:], in_=prod[:].rearrange("p k d -> p d k"), axis=AX.X,
                                                      op=ALU.add), reads=[prod], writes=[o_hb])
                P.op("dve", lambda e: e.scalar_tensor_tensor(out=o_hb[:], in0=vn_hb[:], scalar=Pm[:, 128:129], in1=o_hb[:],
                                                             op0=ALU.mult, op1=ALU.add), reads=[vn_hb, Pm], writes=[o_hb])
                P.op("dve", lambda e: e.tensor_scalar(out=o_hb[:], in0=o_hb[:], scalar1=sm[:, 6:7], scalar2=None, op0=ALU.mult),
                     reads=[sm], writes=[o_hb])
                P.op("pool", lambda e: e.memset(attn_s[:], 0.0), writes=[attn_s])
                for h in range(8):
                    P.dma("sp", attn_s[0:NS, h * 64:(h + 1) * 64], o_hb[h * 16:(h + 1) * 16, :], reads=[o_hb], writes=[attn_s])
                _stop("p0b")
                ap, b = next_mm()
                for g in range(4):
                    for t in range(2):
                        P.op("pe", lambda e, g=g, t=t, ap=ap: e.matmul(
                            out=ap[0:NS, g * 128:(g + 1) * 128], lhsT=sel[0:120, t, g, :],
                            rhs=spl[0:120, t, g * 128:(g + 1) * 128], start=(t == 0), stop=(t == 1)),
                            reads=[sel, spl], writes=[b])
                P.op("pool", lambda e: e.memset(d_s[:], 0.0), writes=[d_s])
                P.op("dve", lambda e, ap=ap: e.tensor_tensor(out=wsum[0:NS, :], in0=ap[0:NS, :], in1=z_s[0:NS, 768:1280], op=ALU.add),
                     reads=[b, z_s], writes=[wsum])
                for g in range(4):
                    P.op("dve", lambda e, g=g: e.scalar_tensor_tensor(
                        out=d_s[0:NS, g * 128:(g + 1) * 128], in0=wsum[0:NS, g * 128:(g + 1) * 128],
                        scalar=1.0 / POOL_W[g], in1=z_s[0:NS, 768 + g * 128:768 + (g + 1) * 128],
                        op0=ALU.mult, op1=ALU.subtract), reads=[wsum, z_s], writes=[d_s])
                trs = tr_ring.next()
                for g in range(4):
                    P.op("pe", lambda e, g=g: e.transpose(out=trs[:, g, :], in_=d_s[:, g * 128:(g + 1) * 128], identity=ident_b[:]),
                         reads=[d_s, ident_b], writes=trs.bufs)
                P.op("dve", lambda e: e.tensor_copy(out=dT_s[:], in_=trs[:, 0:4, :]), reads=trs.bufs, writes=[dT_s])
                pap, pb = next_mm()
                for g in range(4):
                    P.op("pe", lambda e, g=g, pap=pap: e.matmul(out=pap[:, g * 128:(g + 1) * 128], lhsT=dT_s[:, g, :],
                                                                  rhs=W_pool_sb[:, g, :], start=True, stop=True),
                         reads=[dT_s, W_pool_sb], writes=[pb])
                mix_tail(x_s[:], x_s, attn_s, pap, pb, NBLK * 128, x1d_bufs[NBLK])
                P.barrier()
                _stop("p0")

            with contextlib.ExitStack() as s1:
                T = NB1 * 128
                make_rings(s1, True)
                x_ring = P.ring("x_tm", [128, NB1, D], F32, 2, s1)
                xT = P.sb("xT", [128, 8, T], BF16, s1)
                qT = P.sb("qT", [128, 4, T], BF16, s1)
                kT2 = P.sb("kT2", [128, 2, 2, 128 + T], BF16, s1)
                uT = P.sb("uT", [128, 4, HAL + T], F32, s1)
                pA = P.sb("pA", [128, 4, HAL + T], F32, s1)
                pB = P.sb("pB", [128, 3, HAL + T], F32, s1)
                dd = P.sb("dd", [128, 4, T], BF16, s1)
                kv_ring = P.ring("kv_sb", [128, 256], F32, 2, s1)
                v_aug = P.sb("v_aug", [128, NB1 + 1, 2, 66], BF16, s1)
                PT_ring = P.ring("PT", [128, 4, 2, 128], BF16, 2, s1)
                PT_half = {id(t): [Buf("PTa"), Buf("PTb")] for t in PT_ring.tiles}
                den_ring = P.ring("den", [128, 8], F32, 4, s1)

                P.op("pool", lambda e: e.memset(kT2[:], 0.0), writes=[kT2])
                P.op("pool", lambda e: e.memset(uT[:], 0.0), writes=[uT])
                P.op("pool", lambda e: e.memset(v_aug[:], 0.0), writes=[v_aug])

                supers = [[0]] + [list(range(1 + i * NB1, 1 + (i + 1) * NB1)) for i in range((NBLK - 1) // NB1)]
                if DBG_NSUPER is not None:
                    supers = supers[:DBG_NSUPER]
                xts = {}

                def load_x(si):
                    xt = x_ring.next()
                    for bi, B in enumerate(supers[si]):
                        if B == 0:
                            P.op("pool", lambda e: e.memset(xt[:, 0, :], 0.0), writes=[xt])
                            P.dma("sp", xt[112:128, 0, :], meta, writes=[xt])
                        else:
                            P.dma("sp", xt[:, bi, :], xp[(B - 1) * 128:B * 128, :], writes=[xt])
                    xts[si] = xt

                def stage_B(si, prev_Tn):
                    blocks = supers[si]
                    Tn = len(blocks) * 128
                    if prev_Tn is not None:
                        P.op("pool", lambda e: e.tensor_copy(out=kT2[:, :, :, 0:128], in_=kT2[:, :, :, prev_Tn:prev_Tn + 128]), writes=[kT2])
                        P.op("pool", lambda e: e.tensor_copy(out=uT[:, :, 0:HAL], in_=uT[:, :, prev_Tn:prev_Tn + HAL]), writes=[uT])
                    for c_out in range(4):
                        ap, hb = next_mm()
                        for c in range(8):
                            P.noinc = (c != 7)
                            P.op("pe", lambda e, c=c: e.matmul(
                                out=ap[:, 0:Tn], lhsT=W_in_sb[:, c, c_out * 128:(c_out + 1) * 128], rhs=xT[:, c, 0:Tn],
                                start=(c == 0), stop=(c == 7)), reads=[W_in_sb, xT], writes=[hb])
                        P.op("act", lambda e: e.activation(
                            out=qT[:, c_out, 0:Tn], in_=ap[:, 0:Tn], func=AF.Identity, bias=b_fm[:, c_out:c_out + 1], scale=0.125),
                            reads=[hb, b_fm], writes=[qT])
                    for g in range(2):
                        ap, hb = next_mm()
                        for c in range(8):
                            P.noinc = (c != 7)
                            P.op("pe", lambda e, c=c: e.matmul(
                                out=ap[:, 0:Tn], lhsT=W_kd[:, c, g, :, :].rearrange("p a d -> p (a d)"), rhs=xT[:, c, 0:Tn],
                                start=(c == 0), stop=(c == 7)), reads=[W_kd, xT], writes=[hb])
                        for half in range(2):
                            hs = slice(half * 64, (half + 1) * 64)
                            P.op("act", lambda e, half=half, hs=hs: e.activation(
                                out=kT2[hs, g, half, 128:128 + Tn], in_=ap[hs, 0:Tn], func=AF.Identity, bias=b_kd[hs, g:g + 1], scale=1.0),
                                reads=[hb, b_kd], writes=[kT2])
                    for g in range(4):
                        ap, hb = next_mm()
                        for c in range(8):
                            P.noinc = (c != 7)
                            P.op("pe", lambda e, c=c: e.matmul(
                                out=ap[:, 0:Tn], lhsT=W_in_sb[:, c, 768 + g * 128:768 + (g + 1) * 128], rhs=xT[:, c, 0:Tn],
                                start=(c == 0), stop=(c == 7)), reads=[W_in_sb, xT], writes=[hb])
                        P.op("act", lambda e: e.activation(
                            out=uT[:, g, HAL:HAL + Tn], in_=ap[:, 0:Tn], func=AF.Identity, bias=b_fm[:, 6 + g:7 + g], scale=1.0),
                            reads=[hb, b_fm], writes=[uT])
                    if blocks[0] == 0:
                        P.op("pool", lambda e: e.memset(uT[:, :, HAL:HAL + 112], 0.0), writes=[uT])

                def stage_C_gen(si):
                    blocks = supers[si]
                    Tn = len(blocks) * 128
                    Wd = HAL + Tn
                    P.op("pool", lambda e: e.tensor_tensor(out=pA[:, :, 1:Wd], in0=uT[:, :, 1:Wd], in1=uT[:, :, 0:Wd - 1], op=ALU.add),
                         reads=[uT], writes=[pA])
                    yield
                    P.op("pool", lambda e: e.tensor_tensor(out=pB[:, :, 3:Wd], in0=pA[:, 1:4, 3:Wd], in1=pA[:, 1:4, 1:Wd - 2], op=ALU.add),
                         reads=[pA], writes=[pB])
                    yield

                    def emit_d(g, src, idx):
                        w = POOL_W[g]
                        if blocks[0] == 0:
                            tmpu = RG["pool_sb"].next()
                            P.op("dve", lambda e: e.tensor_tensor(out=tmpu[:, 0:128], in0=src[:, idx, HAL:HAL + 128],
                                                                  in1=cinv_bc[:, g, :], op=ALU.mult),
                                 reads=[src, cinv_bc], writes=[tmpu])
                            P.op("dve", lambda e: e.tensor_tensor(out=dd[:, g, 0:128], in0=tmpu[:, 0:128],
                                                                  in1=uT[:, g, HAL:HAL + 128], op=ALU.subtract),
                                 reads=[tmpu, uT], writes=[dd])
                        else:
                            P.op("dve", lambda e: e.scalar_tensor_tensor(
                                out=dd[:, g, 0:Tn], in0=src[:, idx, HAL:HAL + Tn], scalar=1.0 / w, in1=uT[:, g, HAL:HAL + Tn],
                                op0=ALU.mult, op1=ALU.subtract), reads=[src, uT], writes=[dd])
                    emit_d(0, pA, 0)
                    yield
                    emit_d(1, pB, 0)
                    yield
                    P.op("pool", lambda e: e.tensor_tensor(out=pA[:, 0:2, 7:Wd], in0=pB[:, 1:3, 7:Wd], in1=pB[:, 1:3, 3:Wd - 4], op=ALU.add),
                         reads=[pB], writes=[pA])
                    yield
                    emit_d(2, pA, 0)
                    yield
                    P.op("pool", lambda e: e.tensor_tensor(out=pB[:, 0:1, 15:Wd], in0=pA[:, 1:2, 15:Wd], in1=pA[:, 1:2, 7:Wd - 8], op=ALU.add),
                         reads=[pA], writes=[pB])
                    yield
                    emit_d(3, pB, 0)
                    yield
                    if blocks[-1] == NBLK - 1:
                        ap, hb = half_ap(0), Hb[0]
                        for g in range(4):
                            P.op("pe", lambda e, g=g: e.transpose(
                                out=ap[0:15, g * 128:(g + 1) * 128], in_=uT[:, g, HAL + Tn - 15:HAL + Tn], identity=ident_f[:]),
                                reads=[uT, ident_f], writes=[hb])
                        tmpu = RG["pool_sb"].next()
                        P.op("dve", lambda e: e.tensor_copy(out=tmpu[0:15, :], in_=ap[0:15, :]), reads=[hb], writes=[tmpu])
                        P.dma("pool", pool_prompt, tmpu[0:15, :], reads=[tmpu], writes=[outb], sembuf=tmpu)

                def stage_D(si, prev_nb):
                    blocks = supers[si]
                    if prev_nb is not None:
                        P.op("pool", lambda e: e.tensor_copy(out=v_aug[:, 0], in_=v_aug[:, prev_nb]), writes=[v_aug])
                    for bi, B in enumerate(blocks):
                        ap, hb = next_mm()
                        for c in range(8):
                            P.noinc = (c != 7)
                            P.op("pe", lambda e, c=c: e.matmul(
                                out=ap[:, 0:256], lhsT=xT[:, c, bi * 128:(bi + 1) * 128], rhs=W_in_sb[:, c, 512:768],
                                start=(c == 0), stop=(c == 7)), reads=[xT, W_in_sb], writes=[hb])
                        kv_sb = kv_ring.next()
                        P.op("dve", lambda e: e.tensor_tensor(out=kv_sb[:], in0=ap[:, 0:256], in1=b_in_bc[:, 512:768], op=ALU.add),
                             reads=[hb, b_in_bc], writes=[kv_sb])
                        P.op("pool", lambda e: e.tensor_copy(
                            out=v_aug[:, bi + 1, :, 0:64], in_=kv_sb[:, 128:256].rearrange("p (g d) -> p g d", g=2)),
                            reads=[kv_sb], writes=[v_aug])
                        P.op("pool", lambda e: e.memset(v_aug[:, bi + 1, :, 64:65], 1.0), writes=[v_aug])
                        if B == 0:
                            P.op("pool", lambda e: e.memset(v_aug[0:112, bi + 1, :, :], 0.0), writes=[v_aug])
                        if B == NBLK - 1:
                            P.dma("pool", k_prompt, kv_sb[:, 0:128], reads=[kv_sb], writes=[outb], sembuf=kv_sb)
                            P.dma("pool", v_prompt, kv_sb[:, 128:256], reads=[kv_sb], writes=[outb], sembuf=kv_sb)

                def step(gens):
                    for g_ in list(gens):
                        try:
                            next(g_)
                        except StopIteration:
                            gens.remove(g_)

                def stage_E(si, bg):
                    blocks = supers[si]
                    nb = len(blocks)
                    units = [(bi, g) for bi in range(nb) for g in range(2)]
                    attn_tiles = [RG["attn"].next() for _ in blocks]
                    outs = [dict() for _ in blocks]
                    scv = {}
                    fgens = []
                    pending_F = []

                    def QK(u):
                        bi, g = units[u]
                        sc = sc_ring.next()
                        scv[u] = sc
                        for hf in range(2):
                            h0 = 4 * g + 2 * hf
                            for Bt, first in ((B_hi, True), (B_lo, False)):
                                P.noinc = True
                                P.op("pe", lambda e, hf=hf, h0=h0, Bt=Bt, first=first: e.matmul(
                                    out=sc[:, 2 * hf:2 * hf + 2, :, :].rearrange("p j k q -> p (j k q)"), lhsT=ident_b[:],
                                    rhs=Bt[:, h0:h0 + 2, :].rearrange("p h k -> p (h k)"),
                                    start=first, stop=False), reads=[Bt, ident_b], writes=[sc.bufs[hf]])
                            for j in (2 * hf, 2 * hf + 1):
                                h = 4 * g + j
                                cq, half = h // 2, h % 2
                                for kb in range(2):
                                    k0 = (bi + kb) * 128
                                    last = (j == 2 * hf + 1 and kb == 1)
                                    P.noinc = (not last)
                                    P.op("pe", lambda e, j=j, kb=kb, k0=k0, cq=cq, half=half, last=last: e.matmul(
                                        out=sc[:, j, kb, :], lhsT=kT2[:, g, half, k0:k0 + 128],
                                        rhs=qT[:, cq, bi * 128:(bi + 1) * 128], start=False, stop=last),
                                        reads=[kT2, qT], writes=[sc.bufs[hf]])

                    def SM_PV_gen(u):
                        bi, g = units[u]
                        sc = scv.pop(u)
                        PT = PT_ring.next()
                        o = o_ring.next()
                        den = den_ring.next()
                        attn = attn_tiles[bi]
                        for hf in range(2):
                            P.op("act", lambda e, hf=hf: e.activation(
                                out=PT[:, 2 * hf:2 * hf + 2, :, :].rearrange("p j k q -> p (j k q)"),
                                in_=sc[:, 2 * hf:2 * hf + 2, :, :].rearrange("p j k q -> p (j k q)"), func=AF.Exp),
                                reads=[sc.bufs[hf]], writes=[PT_half[id(PT)][hf]])
                            yield
                        for j in range(4):
                            for kb in range(2):
                                P.noinc = (not (j == 3 and kb == 1))
                                P.op("pe", lambda e, j=j, kb=kb: e.matmul(
                                    out=o[:, j, 0:65], lhsT=PT[:, j, kb, :], rhs=v_aug[:, bi + kb, g, 0:65],
                                    start=(kb == 0), stop=(kb == 1)), reads=[PT_half[id(PT)][j // 2], v_aug], writes=o.bufs)
                        yield
                        P.op("dve", lambda e: e.tensor_scalar(out=den[:, 0:4], in0=o[:, :, 64], scalar1=1.0, scalar2=None, op0=ALU.add),
                             reads=o.bufs, writes=[den])
                        yield
                        P.op("dve", lambda e: e.reciprocal(out=den[:, 4:8], in_=den[:, 0:4]), writes=[den])
                        yield
                        P.op("dve", lambda e: e.tensor_tensor(
                            out=attn[:, g * 256:(g + 1) * 256].rearrange("p (j d) -> p j d", j=4), in0=o[:, :, 0:64],
                            in1=den[:, 4:8].unsqueeze(2).to_broadcast([128, 4, 64]), op=ALU.mult),
                            reads=o.bufs + [den], writes=[attn])

                    QK(0)
                    if len(units) > 1:
                        QK(1)
                    for p in range(0, len(units), 2):
                        pair = [SM_PV_gen(u) for u in (p, p + 1) if u < len(units)]
                        while pair:
                            step(pair)
                            step(fgens)
                        for u in (p + 2, p + 3):
                            if u < len(units):
                                QK(u)
                        for _ in range(4):
                            step(bg)
                        pending_F.append(p // 2)
                        if not bg:
                            for bi in pending_F:
                                fgens.append(stage_F_gen(si, bi, attn_tiles[bi], outs[bi]))
                            pending_F = []
                    while bg:
                        step(bg)
                    for bi in pending_F:
                        fgens.append(stage_F_gen(si, bi, attn_tiles[bi], outs[bi]))
                    lockstep(fgens)
                    return outs

                fslot = [0]

                def stage_F_gen(si, bi, attn, out):
                    def pool_mm():
                        fslot[0] = (fslot[0] + 1) % 2
                        pap, pb = half_ap(fslot[0]), Hb[fslot[0]]
                        for g in range(4):
                            P.noinc = (g != 3)
                            P.op("pe", lambda e, g=g: e.matmul(
                                out=pap[:, g * 128:(g + 1) * 128], lhsT=dd[:, g, bi * 128:(bi + 1) * 128],
                                rhs=W_pool_sb[:, g, :], start=True, stop=True), reads=[dd, W_pool_sb], writes=[pb])
                        return pap, pb
                    return tail_F_gen(attn, pool_mm, out)

                def stage_G_gen(si, bi, mix_in):
                    B = supers[si][bi]
                    return tail_G_gen(mix_in, xts[si][:, bi, :], xts[si], B * 128, x1d_bufs[B])

                def stage_A_gen(si, bi):
                    return norm_T_gen(xts[si][:, bi, :], xts[si], g_pre_mix_bc, xT, bi * 128, trA_ring)

                load_x(0)
                if len(supers) > 1:
                    load_x(1)
                for bi in range(len(supers[0])):
                    run(stage_A_gen(0, bi))
                stage_B(0, None)
                stage_D(0, None)
                for si, blocks in enumerate(supers):
                    nb = len(blocks)
                    has_next = si + 1 < len(supers)
                    nnb = len(supers[si + 1]) if has_next else 0
                    outs = stage_E(si, [stage_C_gen(si)])
                    _stop("p1e")
                    for p0 in range(0, max(nb, nnb), 2):
                        gens = []
                        for bi in range(p0, min(p0 + 2, max(nb, nnb))):
                            if bi < nnb:
                                gens.append(stage_A_gen(si + 1, bi))
                            if bi < nb:
                                gens.append(stage_G_gen(si, bi, outs[bi]["mix"]))
                        lockstep(gens)
                    if has_next:
                        stage_B(si + 1, nb * 128)
                        stage_D(si + 1, nb)
                        if si + 2 < len(supers):
                            load_x(si + 2)
                P.barrier()
                _stop("p1")
        with contextlib.ExitStack() as s2:
            T2 = NB2 * 128
            W_up_sb = P.sb("W_up", [128, 8, NUP], BF16, s2)
            W_dn_sb = P.sb("W_dn", [128, NJ, D], BF16, s2)
            g_pre_ffn_bc = P.sb("g_pre_ffn_bc", [128, D], F32, s2)
            g_post_ffn_bc = P.sb("g_post_ffn_bc", [128, D], F32, s2)
            cwb = P.sb("cwb", [128, 44, 4], F32, s2)
            halo = P.sb("halo", [128, 44, 2], F32, s2)
            WUP_GROUPS = [(0, 6), (6, 12), (12, 17), (17, 22)]
            wup_bufs = [Buf("wup%d" % q) for q in range(len(WUP_GROUPS))]
            wup_of_j = {}
            for q, (j0, j1) in enumerate(WUP_GROUPS):
                for j in range(j0, j1):
                    wup_of_j[j] = wup_bufs[q]
                for hv in range(2):
                    c0, c1 = hv * DFF + j0 * 128, hv * DFF + j1 * 128
                    P.dma("pool", W_up_sb[:, :, c0:c1], w_up[:, c0:c1].rearrange("(c p) n -> p c n", p=128),
                          writes=[wup_bufs[q]])
            for j0 in range(0, NJ, 11):
                P.dma("pool", W_dn_sb[:, j0:j0 + 11, :], w_down[j0 * 128:(j0 + 11) * 128, :].rearrange("(j p) n -> p j n", p=128),
                      writes=[W_dn_sb])
            P.dma("sp", g_pre_ffn_bc[:], g_pre_ffn.partition_broadcast(128), writes=[g_pre_ffn_bc])
            P.dma("sp", g_post_ffn_bc[:], g_post_ffn.partition_broadcast(128), writes=[g_post_ffn_bc])
            P.op("pool", lambda e: e.memset(halo[:], 0.0), writes=[halo])

            xb2_ring = P.ring("xb2", [128, D], BF16, 1, s2)
            ssq2_ring = P.ring("ssq_f", [128, 4], F32, 6, s2)
            x1_in = P.ring("x1_in", [128, D], F32, 2, s2)
            x1nT = P.sb("x1nT", [128, 8, T2], BF16, s2)
            a_sb = P.sb("a_sb", [128, NJ, T2], BF16, s2)
            a_main = a_sb
            a_bufs = [Buf("a%d" % j) for j in range(NJ)]
            a_alias = a_bufs[0:10]
            x1nT_s = P.sb("x1nT_s", [128, 8, NS], BF16, s2)
            a_s = Tile(a_sb.t[:, 9, 0:NJ * NS].rearrange("p (j n) -> p j n", n=NS), Buf("a_s"))
            ytmp_ring = P.ring("ytmp", [128, D], F32, 2, s2)
            ytmp = ytmp_ring.tiles[0]

            tr2 = P.ps("tr2", [128, 8, 128], BF16, s2)
            gv_ps = [P.ps("gv%d" % i, [128, 2, 512], F32, s2) for i in range(2)]
            ffn_ps = P.ps("ffn", [128, 1024], F32, s2)
            ffA = Buf("ffA")
            ffB = Buf("ffB")
            trf = P.ps("trf", [128, 512], F32, s2)

            def norm_pre2(x_ap, x_tile):
                ssq = ssq2_ring.next()
                xb = xb2_ring.next()
                P.op("act", lambda e: e.activation(out=xb[:], in_=x_ap, func=AF.Square, accum_out=ssq[:, 0:1]),
                     reads=[x_tile], writes=[xb, ssq])
                stt, rstd = rstd_from_ssq(ssq[:, 0:1], ssq, D)
                P.op("dve", lambda e: e.scalar_tensor_tensor(out=xb[:], in0=x_ap, scalar=rstd, in1=g_pre_ffn_bc[:],
                                                             op0=ALU.mult, op1=ALU.mult),
                     reads=[x_tile, stt, g_pre_ffn_bc], writes=[xb])
                return xb

            def norm_tr2(xb, dstT, col0):
                for c in range(8):
                    P.noinc = (c != 7)
                    P.op("pe", lambda e, c=c: e.transpose(out=tr2[:, c, :], in_=xb[:, c * 128:(c + 1) * 128], identity=ident_b[:]),
                         reads=[xb, ident_b], writes=[tr2])
                P.op("act", lambda e: e.copy(out=dstT[:, :, col0:col0 + 128], in_=tr2[:]), reads=[tr2], writes=[dstT])

            class ChAcc:
                def __init__(self, at, b):
                    self.at = at
                    self.b = b

            def to_token_major(src, n, dst_dram_fn):
                for q in range(11):
                    for i in range(4):
                        jj = q * 4 + i
                        P.op("pe", lambda e, jj=jj, i=i: e.transpose(out=trf[0:n, i * 128:(i + 1) * 128], in_=src.at(jj),
                                                                     identity=ident_f[:]),
                             reads=[src.b, ident_f, a_alias], writes=[trf])
                    P.op("dve", lambda e: e.tensor_copy(out=ytmp[0:n, 0:512], in_=trf[0:n, :]), reads=[trf], writes=[ytmp])
                    P.dma("pool", dst_dram_fn(q), ytmp[0:n, 0:512], reads=[ytmp], writes=[outb], sembuf=ytmp)

            class UpBuf:
                def __init__(self, name, T, stack):
                    self.up = [P.sb("%s_up%d" % (name, hv), [128, 2 + T], F32, stack) for hv in range(2)]
                    self.hb = [Buf("%s_halo%d" % (name, hv)) for hv in range(2)]
                    self.c = [P.sb("%s_c%d" % (name, hv), [128, T], F32, stack) for hv in range(2)]

            def ffn_norm_pre(B, bi):
                x1 = x1_in.next()
                P.dma("sp", x1[:], x1d[B * 128:(B + 1) * 128, :], reads=[x1d_bufs[B]], writes=[x1])
                return norm_pre2(x1[:], x1)

            def ffn_norm_tr(B, bi, xb, dst=None, ncols=128):
                if dst is not None:
                    for c in range(8):
                        P.noinc = (c != 7)
                        P.op("pe", lambda e, c=c: e.transpose(out=tr2[:, c, :], in_=xb[:, c * 128:(c + 1) * 128], identity=ident_b[:]),
                             reads=[xb, ident_b], writes=[tr2])
                    P.op("dve", lambda e: e.tensor_copy(out=dst[:, :, 0:ncols], in_=tr2[:, :, 0:ncols]), reads=[tr2], writes=[dst])
                    return
                norm_tr2(xb, x1nT, bi * 128)
                if B == 0:
                    P.op("pool", lambda e: e.memset(x1nT[:, :, 0:112], 0.0), writes=[x1nT])

            def ffn_norm(B, bi):
                ffn_norm_tr(B, bi, ffn_norm_pre(B, bi))

            def run2(gen):
                for _ in gen:
                    pass

            def lockstep2(gens):
                gens = list(gens)
                while gens:
                    for g_ in list(gens):
                        try:
                            next(g_)
                        except StopIteration:
                            gens.remove(g_)

            def ffn_up(blocks, sample, ub_ring, scT=None, upS=None):
                run2(ffn_up_gen(blocks, sample, ub_ring, x1nT, a_sb, scT, upS))

            def ffn_halo_gen():
                for j in range(NJ):
                    gv = gv_ps[j % 2]
                    for hv in range(2):
                        col = hv * DFF + j * 128
                        for c in range(8):
                            P.noinc = (c != 7)
                            P.op("pe", lambda e, c=c, hv=hv, col=col: e.matmul(
                                out=gv[:, hv, 0:16], lhsT=W_up_sb[:, c, col:col + 128], rhs=x1nT[:, c, 112:128],
                                start=(c == 0), stop=(c == 7)), reads=[wup_of_j[j], x1nT], writes=[gv])
                    for hv in range(2):
                        ch = hv * NJ + j
                        P.op("act", lambda e, hv=hv, ch=ch: e.copy(out=halo[:, ch, :], in_=gv[:, hv, 14:16]),
                             reads=[gv], writes=[halo])
                    yield

            def ffn_up_gen(blocks, sample, ub_ring, x1nT, a_sb, scT=None, upS=None):
                N = NS if sample else len(blocks) * 128
                pend = None
                for j in range(NJ + 1):
                    if j < NJ:
                        gv = gv_ps[j % 2]
                        ub = ub_ring[j % len(ub_ring)]
                        for hv in range(2):
                            col = hv * DFF + j * 128
                            for c in range(8):
                                P.noinc = (c != 7)
                                P.op("pe", lambda e, c=c, hv=hv, col=col, gv=gv: e.matmul(
                                    out=gv[:, hv, 0:N], lhsT=W_up_sb[:, c, col:col + 128], rhs=x1nT[:, c, 0:N],
                                    start=(c == 0), stop=(c == 7)), reads=[wup_of_j[j], x1nT], writes=[gv])
                        for hv in range(2):
                            ch = hv * NJ + j
                            if not sample:
                                P.op("pool", lambda e, hv=hv, ch=ch: e.tensor_copy(out=ub.up[hv][:, 0:2], in_=halo[:, ch, :]),
                                     reads=[halo], writes=[ub.hb[hv]])
                            P.op("act", lambda e, hv=hv: e.copy(out=ub.up[hv][:, 2:2 + N], in_=gv[:, hv, 0:N]),
                                 reads=[gv], writes=[ub.up[hv]])
                        for hv in range(2):
                            ch = hv * NJ + j
                            P.op("act", lambda e, hv=hv, ch=ch: e.activation(
                                out=ub.c[hv][:, 0:N], in_=gv[:, hv, 0:N], func=AF.Identity, bias=cwb[:, ch, 3:4], scale=cwb[:, ch, 2:3]),
                                reads=[gv, cwb], writes=[ub.c[hv]])
                        for hv in range(2):
                            ch = hv * NJ + j
                            if not sample:
                                P.op("pool", lambda e, hv=hv, ch=ch: e.tensor_copy(out=halo[:, ch, :], in_=ub.up[hv][:, N:N + 2]),
                                     reads=[ub.up[hv]], writes=[halo])
                            else:
                                P.op("pool", lambda e, hv=hv, ch=ch: e.tensor_copy(out=upS.at(ch), in_=ub.up[hv][:, 2:2 + N]),
                                     reads=[ub.up[hv], a_alias], writes=[upS.b])
                        for i in (1, 0):
                            for hv in range(2):
                                ch = hv * NJ + j
                                if sample:
                                    in0 = scT.at(ch).rearrange("p (b i) -> p i b", i=2)[:, i, :]
                                    rd = [scT.b, cwb, a_alias]
                                else:
                                    in0 = ub.up[hv][:, i:i + N]
                                    rd = [ub.up[hv], ub.hb[hv], cwb]
                                P.op("dve", lambda e, hv=hv, ch=ch, i=i, in0=in0: e.scalar_tensor_tensor(
                                    out=ub.c[hv][:, 0:N], in0=in0, scalar=cwb[:, ch, i:i + 1], in1=ub.c[hv][:, 0:N],
                                    op0=ALU.mult, op1=ALU.add), reads=rd, writes=[ub.c[hv]])
                    if pend is not None:
                        pj, pub = pend
                        P.op("act", lambda e: e.activation(out=pub.c[0][:, 0:N], in_=pub.c[0][:, 0:N], func=AF.Gelu_apprx_tanh),
                             writes=[pub.c[0]])
                        P.op("pool", lambda e: e.tensor_tensor(out=a_sb[:, pj, 0:N], in0=pub.c[0][:, 0:N], in1=pub.c[1][:, 0:N], op=ALU.mult),
                             reads=[pub.c[0], pub.c[1]], writes=[a_sb if sample else a_bufs[pj]])
                    pend = (j, ub) if j < NJ else None
                    yield

            ffn_bufs = [(ffn_ps.t, [ffA, ffB]), (gv_ps[1].t[:].rearrange("p a n -> p (a n)"), [gv_ps[1].b, gv_ps[1].b])]

            def ffn_down_mm(B, bi, sample, fb, a_sb=None):
                a_sb = a_sb if a_sb is not None else a_main
                M = NS if sample else 128
                fap, fbufs = fb
                for half in range(2):
                    for j in range(NJ):
                        P.noinc = (j != NJ - 1)
                        P.op("pe", lambda e, j=j, half=half: e.matmul(
                            out=fap[0:M, half * 512:(half + 1) * 512], lhsT=a_sb[:, j, bi * 128:bi * 128 + M],
                            rhs=W_dn_sb[:, j, half * 512:(half + 1) * 512], start=(j == 0), stop=(j == NJ - 1)),
                            reads=([a_sb, a_alias] if sample else [a_bufs[j]]) + [W_dn_sb], writes=[fbufs[half]])

            def ffn_down_tail(B, bi, sample, fb):
                M = NS if sample else 128
                fap, fbufs = fb
                x1 = x1_in.next()
                P.dma("sp", x1[:], x1d[B * 128:(B + 1) * 128, :], reads=[x1d_bufs[B]], writes=[x1])
                ssq = ssq2_ring.next()
                yt = ytmp_ring.next()
                for half in range(2):
                    P.op("act", lambda e, half=half: e.activation(
                        out=yt[0:M, half * 512:(half + 1) * 512], in_=fap[0:M, half * 512:(half + 1) * 512],
                        func=AF.Square, accum_out=ssq[0:M, half:half + 1]),
                        reads=[fbufs[half]], writes=[yt, ssq])
                P.op("dve", lambda e: e.tensor_tensor(out=ssq[0:M, 2:3], in0=ssq[0:M, 0:1], in1=ssq[0:M, 1:2], op=ALU.add), writes=[ssq])
                stt, rstd = rstd_from_ssq(ssq[0:M, 2:3], ssq, D, M)
                for half in range(2):
                    P.op("dve", lambda e, half=half: e.scalar_tensor_tensor(
                        out=yt[0:M, half * 512:(half + 1) * 512], in0=fap[0:M, half * 512:(half + 1) * 512], scalar=stt[0:M, 2:3],
                        in1=g_post_ffn_bc[0:M, half * 512:(half + 1) * 512], op0=ALU.mult, op1=ALU.mult),
                        reads=[fbufs[half], stt, g_post_ffn_bc], writes=[yt])
                P.op("dve", lambda e: e.tensor_tensor(out=yt[0:M, :], in0=yt[0:M, :], in1=x1[0:M, :], op=ALU.add),
                     reads=[x1], writes=[yt])
                if sample:
                    P.dma("pool", y_sample, yt[0:NS, :], reads=[yt], writes=[outb], sembuf=yt)
                else:
                    P.dma("pool", y_prompt[(B - 1) * 128:B * 128, :], yt[:], reads=[yt], writes=[outb], sembuf=yt)

            def ffn_down(B, bi, sample):
                ffn_down_mm(B, bi, sample, ffn_bufs[0])
                ffn_down_tail(B, bi, sample, ffn_bufs[0])

            P.dma("sp", cwb[:].rearrange("p c r -> p (c r)"), cw_d, reads=[csd_buf], writes=[cwb])
            ub_ring = [UpBuf("ub%d" % i, T2, s2) for i in range(2)]
            supers2 = [[0]] + [list(range(1 + i * NB2, 1 + (i + 1) * NB2)) for i in range((NBLK - 1) // NB2)]
            if DBG_NSUPER is not None:
                supers2 = supers2[:DBG_NSUPER]
            for bi, B in enumerate(supers2[0]):
                ffn_norm(B, bi)
            scT_v = a_sb.t[:, 0:6, :].bitcast(F32).rearrange("p a b -> p (a b)")[:, 0:1408].rearrange("p (c r) -> p c r", r=32)
            upS_v = a_sb.t[:, 6:9, :].bitcast(F32).rearrange("p a b -> p (a b)")[:, 0:704].rearrange("p (c r) -> p c r", r=NS)
            scT = ChAcc(lambda ch: scT_v[:, ch, :], Buf("scT"))
            upS = ChAcc(lambda ch: upS_v[:, ch, :], Buf("upS"))
            P.dma("sp", a_sb.t[:, 0:6, :].bitcast(F32).rearrange("p a b -> p (a b)")[:, 0:1408], sc_d,
                  reads=[csd_buf, a_alias], writes=[scT.b])
            P.dma("sp", conv_sample[:, 0, :], sconv.rearrange("(b i) c -> b i c", i=2)[:, 1, :], writes=[outb])
            ub_s = [UpBuf("ubs%d" % i, NS, s2) for i in range(2)]
            ffn_norm_tr(NBLK, 0, ffn_norm_pre(NBLK, 0), dst=x1nT_s, ncols=NS)
            for si, blocks in enumerate(supers2):
                if si == 0:
                    lockstep2([ffn_halo_gen(),
                               ffn_up_gen([NBLK], True, ub_s, x1nT_s, a_s, scT, upS)])
                    ffn_down_mm(NBLK, 0, True, ffn_bufs[0], a_s)
                    ffn_down_tail(NBLK, 0, True, ffn_bufs[0])
                    to_token_major(upS, NS, lambda q: conv_sample[:, 1, q * 512:(q + 1) * 512])
                else:
                    ffn_up(blocks, False, ub_ring)
                nxt = supers2[si + 1] if si + 1 < len(supers2) else []
                if blocks[0] == 0:
                    for bi, B in enumerate(nxt):
                        ffn_norm(B, bi)
                    continue
                nbk = len(blocks)
                fbs = [ffn_bufs[(nbk - 1 - bi) % 2] for bi in range(nbk)]
                ffn_down_mm(blocks[0], 0, False, fbs[0])
                for bi, B in enumerate(blocks):
                    xb_n = ffn_norm_pre(nxt[bi], bi) if bi < len(nxt) else None
                    if xb_n is not None:
                        ffn_norm_tr(nxt[bi], bi, xb_n)
                    if bi + 1 < nbk:
                        ffn_down_mm(blocks[bi + 1], bi + 1, False, fbs[bi + 1])
                    ffn_down_tail(B, bi, False, fbs[bi])
                for bi in range(nbk, len(nxt)):
                    ffn_norm(nxt[bi], bi)
            to_token_major(ChAcc(lambda jj: halo[:, jj, :], halo.b), 2, lambda q: conv_prompt[:, q * 512:(q + 1) * 512])


def _consts():
    ident = np.eye(128, dtype=np.float32)
    slopes = np.exp2(-(np.arange(1, 9, dtype=np.float32) * 1.0)).astype(np.float32)
    key = np.arange(128)[:, None, None]
    kb = np.arange(2)[None, :, None]
    q = np.arange(128)[None, None, :]
    dist = 128 + q - (kb * 128 + key)
    valid = (dist >= 0) & (dist <= 128)
    bb = np.empty((128, 8, 2, 128), np.float32)
    for h in range(8):
        bb[:, h] = np.where(valid, -slopes[h] * dist.astype(np.float32), NEG)
    bbase = bb.reshape(128, 8 * 256)
    cinv = np.ones((4, 128), np.float32)
    for g, w in enumerate(POOL_W):
        pos = np.arange(128) - 112
        cnt = np.where(pos >= 0, np.minimum(pos + 1, w), w).astype(np.float32)
        cinv[g] = 1.0 / cnt
    cinv = cinv.reshape(1, 512)
    sel = np.zeros((120, 2, 4, 16), np.float32)
    for t in range(2):
        for bl in range(8):
            for r in range(15):
                for g, w in enumerate(POOL_W):
                    if r >= 16 - w:
                        sel[bl * 15 + r, t, g, t * 8 + bl] = 1.0
    sel = sel.reshape(120, 128)
    sbias = np.zeros((128, 129), np.float32)
    for h in range(8):
        sbias[h * 16:(h + 1) * 16, 0:128] = -slopes[h] * (128 - np.arange(128, dtype=np.float32))[None, :]
    return ident, bbase, cinv, sel, sbias


_NC_CACHE = {}


def kernel(x_prompt, x_sample, cache_k, cache_v, state_pool, state_conv, meta,
           w_in, b_in, sinks, w_pool, pool_scale, g_attn_out, g_pool_out, w_o,
           g_pre_mix, g_post_mix, g_pre_ffn, g_post_ffn, w_up, conv_w, conv_b, w_down):
    f = lambda a: np.ascontiguousarray(np.asarray(a, dtype=np.float32))
    ident, bbase, cinv, sel, sbias = _consts()
    if "nc" not in _NC_CACHE:
        _NC_CACHE["nc"] = build_program()
    nc = _NC_CACHE["nc"]
    shared = {
        "meta": f(meta), "w_in": f(w_in[0]), "b_in": f(b_in[0]).reshape(1, NIN), "sinks": f(sinks[0]).reshape(1, 8),
        "w_pool": f(w_pool[0]).reshape(512, 128), "pool_scale": f(pool_scale[0]).reshape(1, 512),
        "g_attn": f(g_attn_out[0]).reshape(1, 512), "g_pool": f(g_pool_out[0]).reshape(1, 512), "w_o": f(w_o[0]),
        "g_pre_mix": f(g_pre_mix[0]).reshape(1, D), "g_post_mix": f(g_post_mix[0]).reshape(1, D),
        "g_pre_ffn": f(g_pre_ffn[0]).reshape(1, D), "g_post_ffn": f(g_post_ffn[0]).reshape(1, D),
        "w_up": f(w_up[0]), "conv_w": f(conv_w[0]), "conv_b": f(conv_b[0]).reshape(1, NUP), "w_down": f(w_down[0]),
        "c_ident": ident, "c_bbase": bbase, "c_cinv": cinv, "c_sel": sel, "c_sbias": sbias,
    }
    xpn = np.asarray(x_prompt, dtype=np.float32)
    xsn = np.asarray(x_sample, dtype=np.float32)
    ckn = np.asarray(cache_k, dtype=np.float32)
    cvn = np.asarray(cache_v, dtype=np.float32)
    spn = np.asarray(state_pool, dtype=np.float32)
    scn = np.asarray(state_conv, dtype=np.float32)
    in_maps = []
    for i in range(8):
        sl = slice(i * NS, (i + 1) * NS)
        m = dict(shared)
        m["xp"] = f(xpn[i])
        m["xs"] = f(xsn[sl, 0, :])
        m["ck"] = f(ckn[0, sl].reshape(NS, 128, 128))
        m["cv"] = f(cvn[0, sl].reshape(NS, 128, 128))
        m["spool"] = f(spn[0, sl])
        m["sconv"] = f(scn[0, sl].reshape(NS * 2, NUP))
        in_maps.append(m)
    res = run_bass_kernel_spmd(nc, in_maps, core_ids=list(range(8)))
    R = res.results
    y_prompt = np.stack([R[i]["y_prompt"] for i in range(8)], 0)
    y_sample = np.concatenate([R[i]["y_sample"] for i in range(8)], 0).reshape(128, 1, D)
    k_prompt = np.stack([R[i]["k_prompt"].reshape(128, 2, 64) for i in range(8)], 0)[None]
    v_prompt = np.stack([R[i]["v_prompt"].reshape(128, 2, 64) for i in range(8)], 0)[None]
    pool_prompt = np.stack([R[i]["pool_prompt"] for i in range(8)], 0)[None]
    conv_prompt = np.stack([R[i]["conv_prompt"] for i in range(8)], 0)[None]
    k_sample = np.concatenate([R[i]["k_sample"].reshape(NS, 128, 2, 64) for i in range(8)], 0)[None]
    v_sample = np.concatenate([R[i]["v_sample"].reshape(NS, 128, 2, 64) for i in range(8)], 0)[None]
    pool_sample = np.concatenate([R[i]["pool_sample"] for i in range(8)], 0)[None]
    conv_sample = np.concatenate([R[i]["conv_sample"] for i in range(8)], 0)[None]
    outs = (y_prompt, y_sample, k_prompt, v_prompt, pool_prompt, conv_prompt, k_sample, v_sample, pool_sample, conv_sample)
    return tuple(np.ascontiguousarray(o, dtype=np.float32) for o in outs)
```
